# Optimizing a Trainium2 kernel written in Bass

```python
import math
import jax, jax.numpy as jnp
from jax import lax
import numpy as np

D_MODEL = 1024
BATCH = 4
SEQ = 4096
DEPTH = 4
DEC_BATCH = 32
DEC_SEQ = 4
PAST_LEN = 8192
PAGE_SIZE = 128

D_MIX = D_MODEL
GLA_HEADS = 4
GLA_WIDTH = D_MIX // 4
GLA_DV = GLA_WIDTH // GLA_HEADS
GLA_DK = GLA_DV // 2
GLA_RANK = 16
GLA_TAU = 16.0
GLA_CHUNK = 64
SWA_WIDTH = D_MIX // 2
SWA_HEADS = 8
SWA_HD = SWA_WIDTH // SWA_HEADS
DILATED_PATTERNS = ((128, 1), (512, 4), (2048, 16))
SWA_WMAX = max(w for w, _ in DILATED_PATTERNS)
SWA_QBLOCK = 128
ROPE_THETA = 500000.0
ROPE_DIMS = SWA_HD // 4
SSM_WIDTH = D_MIX // 4
SSM_GROUP = 16
SSM_GROUPS = SSM_WIDTH // SSM_GROUP
SSM_STATE = 64
NORM_EPS = 1e-6

IN_SPLITS = (GLA_HEADS * GLA_DK, GLA_HEADS * GLA_DK, GLA_WIDTH, GLA_RANK, GLA_WIDTH,
             SWA_WIDTH, SWA_WIDTH, SWA_WIDTH, SWA_WIDTH,
             SSM_WIDTH, SSM_WIDTH)
IN_COLS = sum(IN_SPLITS)
IN_OFFSETS = tuple(int(v) for v in np.cumsum(IN_SPLITS)[:-1])

kernel_name = "hymba_gla_dilated_s5_decoder_step"

F32 = jnp.float32


def rms_norm(x, g):
    xf = x.astype(F32)
    y = xf * lax.rsqrt(jnp.mean(xf * xf, axis=-1, keepdims=True) + NORM_EPS)
    return (y * g.astype(F32)).astype(x.dtype)


def partial_rope(x, pos):
    half = ROPE_DIMS // 2
    inv = ROPE_THETA ** (-jnp.arange(half, dtype=F32) * (2.0 / ROPE_DIMS))
    ang = pos.astype(F32)[:, None] * inv[None, :]
    cos = jnp.cos(ang)[None, :, None, :]
    sin = jnp.sin(ang)[None, :, None, :]
    xf = x.astype(F32)
    x1, x2 = xf[..., :half], xf[..., half:ROPE_DIMS]
    out = jnp.concatenate([x1 * cos - x2 * sin, x2 * cos + x1 * sin, xf[..., ROPE_DIMS:]], axis=-1)
    return out.astype(x.dtype)


def gla_mix(q, k, v, g, s0):
    Bn, T, H, DK = q.shape
    DV = v.shape[-1]
    C = GLA_CHUNK if T % GLA_CHUNK == 0 else T
    nc = T // C
    q = q.astype(F32).reshape(Bn, nc, C, H, DK) * (DK ** -0.5)
    k = k.astype(F32).reshape(Bn, nc, C, H, DK)
    v = v.astype(F32).reshape(Bn, nc, C, H, DV)
    b = jnp.cumsum(g.astype(F32).reshape(Bn, nc, C, H, DK), axis=2)
    b_last = b[:, :, -1]
    qd = q * jnp.exp(b)
    kd = k * jnp.exp(-b)
    causal = jnp.tril(jnp.ones((C, C), dtype=bool))
    att = jnp.where(causal, jnp.einsum('bnihd,bnjhd->bnhij', qd, kd), 0.0)
    intra = jnp.einsum('bnhij,bnjhe->bnihe', att, v)
    ds = jnp.einsum('bnjhd,bnjhe->bnhde', k * jnp.exp(b_last[:, :, None] - b), v)
    decay = jnp.exp(b_last)

    def step(s, inp):
        dec, d = inp
        return dec[..., None] * s + d, s

    s_fin, s_prev = lax.scan(step, s0.astype(F32), (jnp.swapaxes(decay, 0, 1), jnp.swapaxes(ds, 0, 1)))
    s_prev = jnp.swapaxes(s_prev, 0, 1)
    inter = jnp.einsum('bnihd,bnhde->bnihe', qd, s_prev)
    return (inter + intra).reshape(Bn, T, H, DV), s_fin


def dilated_window_attn(q, k_ext, v_ext, n_invalid):
    Bn, T, H, hd = q.shape
    QB = SWA_QBLOCK if T % SWA_QBLOCK == 0 else T
    nb = T // QB
    scale = hd ** -0.5
    qi = jnp.arange(QB)

    def block(bi):
        start = bi * QB
        qb = lax.dynamic_slice_in_dim(q, start, QB, axis=1).astype(F32) * scale
        kb = lax.dynamic_slice_in_dim(k_ext, start, SWA_WMAX + QB, axis=1)
        vb = lax.dynamic_slice_in_dim(v_ext, start, SWA_WMAX + QB, axis=1)
        outs, lses = [], []
        for window, dil in DILATED_PATTERNS:
            offs = jnp.arange(0, window + 1, dil)
            idx = SWA_WMAX + qi[:, None] - offs[None, :]
            valid = (start + idx) >= n_invalid
            kg = kb[:, idx].astype(F32)
            vg = vb[:, idx].astype(F32)
            s = jnp.einsum('bqhd,bqjhd->bqhj', qb, kg)
            s = jnp.where(valid[None, :, None, :], s, -jnp.inf)
            m = jnp.max(s, axis=-1, keepdims=True)
            e = jnp.exp(s - m)
            den = jnp.sum(e, axis=-1)
            outs.append(jnp.einsum('bqhj,bqjhd->bqhd', e, vg) / den[..., None])
            lses.append(m[..., 0] + jnp.log(den))
        w = jax.nn.softmax(jnp.stack(lses, axis=0), axis=0)
        return jnp.sum(w[..., None] * jnp.stack(outs, axis=0), axis=0)

    out = lax.map(block, jnp.arange(nb))
    return jnp.transpose(out, (1, 0, 2, 3, 4)).reshape(Bn, T, H, hd)


def s5_mix(u, x0_re, x0_im, lam_re, lam_im, log_dt, b_re, b_im, c_re, c_im, d_skip, w_glu, b_glu):
    Bn, T, _ = u.shape
    uf = u.astype(F32)
    ug = uf.reshape(Bn, T, SSM_GROUPS, SSM_GROUP)
    lam = lax.complex(lam_re.astype(F32), lam_im.astype(F32))
    dt = jnp.exp(log_dt.astype(F32))[:, None]
    lam_bar = jnp.exp(lam * dt)
    b_bar = ((lam_bar - 1.0) / lam)[..., None] * lax.complex(b_re.astype(F32), b_im.astype(F32))
    bu = jnp.einsum('btgc,gpc->btgp', ug.astype(jnp.complex64), b_bar)
    x0 = lax.complex(x0_re.astype(F32), x0_im.astype(F32))
    bu = bu.at[:, 0].add(lam_bar[None] * x0)
    a = jnp.broadcast_to(lam_bar, bu.shape)

    def combine(l, r):
        return (l[0] * r[0], r[0] * l[1] + r[1])

    _, xs = lax.associative_scan(combine, (a, bu), axis=1)
    y = (jnp.einsum('btgp,gcp->btgc', jnp.real(xs), c_re.astype(F32))
         - jnp.einsum('btgp,gcp->btgc', jnp.imag(xs), c_im.astype(F32)))
    y = y.reshape(Bn, T, SSM_WIDTH) + d_skip.astype(F32) * uf
    z = jax.nn.gelu(y)
    out = z * jax.nn.sigmoid(z @ w_glu.astype(F32) + b_glu.astype(F32))
    x_last = xs[:, -1]
    return out, jnp.real(x_last), jnp.imag(x_last)


def layer(x, c, pos0, gla_state, k_ctx, v_ctx, ssm_re, ssm_im,
          w_ada, b_ada, g_pre, g_post, w_in, w_gla_lr, b_gla_lr, g_gla,
          lam_re, lam_im, log_dt, b_re, b_im, c_re, c_im, d_skip, w_glu, b_glu, w_out):
    Bn, T, _ = x.shape
    dt = x.dtype
    mod = jax.nn.silu(c.astype(F32)) @ w_ada.astype(F32) + b_ada.astype(F32)
    shift, scale, gate = jnp.split(mod, 3, axis=-1)
    h = rms_norm(x, g_pre).astype(F32)
    h = (h * (1.0 + scale[:, None]) + shift[:, None]).astype(dt)
    proj = h @ w_in
    aq, ak, av, alr, agate, bq, bk, bv, bgate, cu, cgate = jnp.split(proj, IN_OFFSETS, axis=-1)

    glog = jax.nn.log_sigmoid((alr @ w_gla_lr + b_gla_lr).astype(F32)) / GLA_TAU
    o_a, gla_new = gla_mix(aq.reshape(Bn, T, GLA_HEADS, GLA_DK), ak.reshape(Bn, T, GLA_HEADS, GLA_DK),
                           av.reshape(Bn, T, GLA_HEADS, GLA_DV), glog.reshape(Bn, T, GLA_HEADS, GLA_DK),
                           gla_state)
    o_a = rms_norm(o_a, g_gla.reshape(GLA_HEADS, GLA_DV)).reshape(Bn, T, GLA_WIDTH)
    o_a = (o_a * jax.nn.silu(agate.astype(F32))).astype(dt)

    pos = pos0 + jnp.arange(T)
    q = partial_rope(bq.reshape(Bn, T, SWA_HEADS, SWA_HD), pos)
    k = partial_rope(bk.reshape(Bn, T, SWA_HEADS, SWA_HD), pos)
    v = bv.reshape(Bn, T, SWA_HEADS, SWA_HD)
    Lc = k_ctx.shape[1]
    pad = jnp.zeros((Bn, SWA_WMAX - Lc, SWA_HEADS, SWA_HD), dtype=k.dtype)
    k_ext = jnp.concatenate([pad, k_ctx.astype(k.dtype), k], axis=1)
    v_ext = jnp.concatenate([pad, v_ctx.astype(v.dtype), v], axis=1)
    o_b = dilated_window_attn(q, k_ext, v_ext, SWA_WMAX - Lc).reshape(Bn, T, SWA_WIDTH)
    o_b = (o_b * jax.nn.silu(bgate.astype(F32))).astype(dt)
    keep = min(SWA_WMAX, Lc + T)
    n_ext = SWA_WMAX + T
    k_buf = k_ext[:, n_ext - keep:]
    v_buf = v_ext[:, n_ext - keep:]

    o_c, s_re, s_im = s5_mix(cu, ssm_re, ssm_im, lam_re, lam_im, log_dt, b_re, b_im, c_re, c_im,
                             d_skip, w_glu, b_glu)
    o_c = (o_c * jax.nn.silu(cgate.astype(F32))).astype(dt)

    mixed = jnp.concatenate([o_a, o_b, o_c], axis=-1)
    y = rms_norm(mixed @ w_out, g_post).astype(F32)
    x_new = (x.astype(F32) + gate[:, None] * y).astype(dt)
    return x_new, gla_new, k_buf, v_buf, s_re, s_im


def setup_inputs(seed: int = 0) -> dict:
    key = jax.random.key(seed)
    ks = jax.random.split(key, 32)
    L_BUF = min(SWA_WMAX, PAST_LEN)
    nrm = lambda k, shape, s=1.0: jax.random.normal(k, shape, F32) * s
    inp = {}
    inp['x_prompt'] = nrm(ks[0], (BATCH, SEQ, D_MODEL))
    inp['x_sample'] = nrm(ks[1], (DEC_BATCH, DEC_SEQ, D_MODEL))
    inp['c_prompt'] = nrm(ks[2], (BATCH, D_MODEL))
    inp['c_sample'] = nrm(ks[3], (DEC_BATCH, D_MODEL))
    inp['state_gla'] = nrm(ks[4], (DEPTH, DEC_BATCH, GLA_HEADS, GLA_DK, GLA_DV), 0.5)
    inp['cache_swa_k'] = nrm(ks[5], (DEPTH, DEC_BATCH, L_BUF, SWA_HEADS, SWA_HD))
    inp['cache_swa_v'] = nrm(ks[6], (DEPTH, DEC_BATCH, L_BUF, SWA_HEADS, SWA_HD))
    inp['state_ssm_re'] = nrm(ks[7], (DEPTH, DEC_BATCH, SSM_GROUPS, SSM_STATE), 0.1)
    inp['state_ssm_im'] = nrm(ks[8], (DEPTH, DEC_BATCH, SSM_GROUPS, SSM_STATE), 0.1)
    inp['w_ada'] = nrm(ks[9], (DEPTH, D_MODEL, 3 * D_MODEL), 0.5 / math.sqrt(D_MODEL))
    inp['b_ada'] = nrm(ks[10], (DEPTH, 3 * D_MODEL), 0.02)
    inp['g_pre'] = 1.0 + nrm(ks[11], (DEPTH, D_MODEL), 0.05)
    inp['g_post'] = 1.0 + nrm(ks[12], (DEPTH, D_MODEL), 0.05)
    inp['w_in'] = nrm(ks[13], (DEPTH, D_MODEL, IN_COLS), 1.0 / math.sqrt(D_MODEL))
    inp['w_gla_lr'] = nrm(ks[14], (DEPTH, GLA_RANK, GLA_HEADS * GLA_DK), 1.0 / math.sqrt(GLA_RANK))
    inp['b_gla_lr'] = nrm(ks[15], (DEPTH, GLA_HEADS * GLA_DK), 0.1)
    inp['g_gla'] = 1.0 + nrm(ks[16], (DEPTH, GLA_WIDTH), 0.05)
    inp['ssm_lambda_re'] = -0.5 + nrm(ks[17], (DEPTH, SSM_GROUPS, SSM_STATE), 0.01)
    inp['ssm_lambda_im'] = (math.pi * jnp.arange(SSM_STATE, dtype=F32))[None, None, :] + nrm(ks[18], (DEPTH, SSM_GROUPS, SSM_STATE), 0.01)
    inp['ssm_log_dt'] = jax.random.uniform(ks[19], (DEPTH, SSM_GROUPS), F32, math.log(1e-3), math.log(1e-1))
    inp['ssm_b_re'] = nrm(ks[20], (DEPTH, SSM_GROUPS, SSM_STATE, SSM_GROUP), 0.5 / math.sqrt(SSM_GROUP))
    inp['ssm_b_im'] = nrm(ks[21], (DEPTH, SSM_GROUPS, SSM_STATE, SSM_GROUP), 0.5 / math.sqrt(SSM_GROUP))
    inp['ssm_c_re'] = nrm(ks[22], (DEPTH, SSM_GROUPS, SSM_GROUP, SSM_STATE), 0.5 / math.sqrt(SSM_STATE))
    inp['ssm_c_im'] = nrm(ks[23], (DEPTH, SSM_GROUPS, SSM_GROUP, SSM_STATE), 0.5 / math.sqrt(SSM_STATE))
    inp['ssm_d'] = nrm(ks[24], (DEPTH, SSM_WIDTH))
    inp['w_glu'] = nrm(ks[25], (DEPTH, SSM_WIDTH, SSM_WIDTH), 1.0 / math.sqrt(SSM_WIDTH))
    inp['b_glu'] = nrm(ks[26], (DEPTH, SSM_WIDTH), 0.02)
    inp['w_out'] = nrm(ks[27], (DEPTH, D_MIX, D_MODEL), 1.0 / math.sqrt(D_MIX))
    return inp


def reference(x_prompt, x_sample, c_prompt, c_sample, state_gla, cache_swa_k, cache_swa_v,
              state_ssm_re, state_ssm_im, w_ada, b_ada, g_pre, g_post, w_in, w_gla_lr, b_gla_lr,
              g_gla, ssm_lambda_re, ssm_lambda_im, ssm_log_dt, ssm_b_re, ssm_b_im, ssm_c_re,
              ssm_c_im, ssm_d, w_glu, b_glu, w_out):
    Bp = x_prompt.shape[0]
    xp, xs = x_prompt, x_sample
    gla_p, gla_s, kp, vp, ks_, vs_ = [], [], [], [], [], []
    rep, imp, res, ims = [], [], [], []
    zero_gla = jnp.zeros((Bp, GLA_HEADS, GLA_DK, GLA_DV), F32)
    empty_kv = jnp.zeros((Bp, 0, SWA_HEADS, SWA_HD), x_prompt.dtype)
    zero_ssm = jnp.zeros((Bp, SSM_GROUPS, SSM_STATE), F32)
    for l in range(DEPTH):
        lp = (w_ada[l], b_ada[l], g_pre[l], g_post[l], w_in[l], w_gla_lr[l], b_gla_lr[l], g_gla[l],
              ssm_lambda_re[l], ssm_lambda_im[l], ssm_log_dt[l], ssm_b_re[l], ssm_b_im[l],
              ssm_c_re[l], ssm_c_im[l], ssm_d[l], w_glu[l], b_glu[l], w_out[l])
        xp, g1, k1, v1, r1, i1 = layer(xp, c_prompt, 0, zero_gla, empty_kv, empty_kv,
                                       zero_ssm, zero_ssm, *lp)
        xs, g2, k2, v2, r2, i2 = layer(xs, c_sample, PAST_LEN, state_gla[l], cache_swa_k[l],
                                       cache_swa_v[l], state_ssm_re[l], state_ssm_im[l], *lp)
        gla_p.append(g1); kp.append(k1); vp.append(v1); rep.append(r1); imp.append(i1)
        gla_s.append(g2); ks_.append(k2); vs_.append(v2); res.append(r2); ims.append(i2)
    return (xp, xs, jnp.stack(gla_p), jnp.stack(gla_s), jnp.stack(kp), jnp.stack(vp),
            jnp.stack(ks_), jnp.stack(vs_), jnp.stack(rep), jnp.stack(imp),
            jnp.stack(res), jnp.stack(ims))
```

```python
import math
import numpy as np
import ml_dtypes
from contextlib import ExitStack
import concourse.bass as bass
import concourse.mybir as mybir
from concourse.bass_utils import run_bass_kernel_spmd

F32 = mybir.dt.float32
BF16 = mybir.dt.bfloat16
AF = mybir.ActivationFunctionType
ALU = mybir.AluOpType
AX = mybir.AxisListType
PI = math.pi

D = 1024
NCOL = 3344
LC = 2048
SSM_L = 128
NKT = 18
BLK = 256
NTB = BLK // 128
EPS = 1e-6


class Res:
    __slots__ = ("name", "last_w", "reads", "sem", "cnt", "excl")

    def __init__(self, name, excl=False):
        self.excl = excl
        self.name = name
        self.last_w = None
        self.reads = []
        self.sem = None
        self.cnt = 0


class StopBuild(Exception):
    pass


class Prog:
    nops = 0
    maxops = 0
    log = None

    def _tick(self, eng, tag):
        self.nops += 1
        if self.log is not None:
            self.log.append((self.nops, eng, tag))
        if self.maxops and self.nops > self.maxops:
            raise StopBuild()

    ENGS = ("pe", "act", "dve", "pool", "sp")
    COMPUTE = ("pe", "act", "dve", "pool")

    def __init__(self, nc):
        self.nc = nc
        self.streams = {e: [] for e in self.ENGS}
        self.needed = {e: set() for e in self.COMPUTE}
        self.known_c = {e: {x: -1 for x in self.COMPUTE} for e in self.ENGS}
        self.known_d = {e: {} for e in self.ENGS}
        self.dma_sems = []

    def _collect(self, eng, reads, writes, is_dma):
        deps = []
        for r in reads:
            if r.last_w is not None:
                deps.append(r.last_w)
            if r.excl:
                deps.extend(r.reads)
        for w in writes:
            if w.last_w is not None:
                deps.append(w.last_w)
            deps.extend(w.reads)
        waits = []
        for d in deps:
            if d[0] == "c":
                _, x, idx = d
                if x == eng and not is_dma and eng == "pe":
                    continue
                if self.known_c[eng][x] >= idx:
                    continue
                self.known_c[eng][x] = idx
                self.needed[x].add(idx)
                waits.append(d)
            else:
                _, owner, cnt = d
                k = self.known_d[eng].get(id(owner), 0)
                if k >= cnt:
                    continue
                self.known_d[eng][id(owner)] = cnt
                waits.append(d)
        return waits

    def op(self, eng, fn, reads=(), writes=()):
        self._tick(eng, "op")
        waits = self._collect(eng, reads, writes, False)
        idx = len(self.streams[eng])
        ev = ("c", eng, idx)
        self.streams[eng].append({"fn": fn, "waits": waits, "ev": ev})
        for r in reads:
            r.reads.append(ev)
        for w in writes:
            w.last_w = ev
            w.reads = []
        return ev

    def dma(self, eng, fn, reads=(), writes=(), owner=None):
        self._tick(eng, "dma")
        waits = self._collect(eng, reads, writes, True)
        if owner is None:
            owner = writes[0] if writes else reads[0]
        if owner.sem is None:
            owner.sem = True
            self.dma_sems.append(owner)
        owner.cnt += 16
        ev = ("d", owner, owner.cnt)
        self.streams[eng].append({"fn": fn, "waits": waits, "ev": ev})
        for r in reads:
            r.reads.append(ev)
        for w in writes:
            w.last_w = ev
            w.reads = []
        return ev

    def emit(self):
        nc = self.nc
        es = ExitStack()
        with es:
            EPOCH = 12000
            for i, o in enumerate(self.dma_sems):
                o.sem = es.enter_context(nc.semaphore(f"d_{i}"))
            rank = {}
            csem = {}
            for e in self.COMPUTE:
                rank[e] = {idx: (n // EPOCH, n % EPOCH + 1) for n, idx in enumerate(sorted(self.needed[e]))}
                nep = max(1, (len(self.needed[e]) + EPOCH - 1) // EPOCH)
                csem[e] = [es.enter_context(nc.semaphore(f"s_{e}{k}")) for k in range(nep)]
            block = es.enter_context(nc.Block())
            me = self

            def replay(e, name, final=False):
                for o in me.streams[name]:
                    for d in o["waits"]:
                        if d[0] == "c":
                            ep, rv = rank[d[1]][d[2]]
                            e.wait_ge(csem[d[1]][ep], rv)
                        else:
                            e.wait_ge(d[1].sem, d[2])
                    ins = o["fn"](e)
                    ev = o["ev"]
                    if ev[0] == "c":
                        if ev[2] in rank[ev[1]]:
                            ins.then_inc(csem[ev[1]][rank[ev[1]][ev[2]][0]], 1)
                    else:
                        ins.then_inc(ev[1].sem, 16)
                if final:
                    for o in me.dma_sems:
                        e.wait_ge(o.sem, o.cnt)

            @block.tensor
            def _(e):
                replay(e, "pe")

            @block.scalar
            def _(e):
                replay(e, "act")

            @block.vector
            def _(e):
                replay(e, "dve")

            @block.gpsimd
            def _(e):
                replay(e, "pool")

            @block.sync
            def _(e):
                replay(e, "sp", final=True)


class Buf:
    def __init__(self, t, r):
        self.t = t
        self.r = r

    def __getitem__(self, k):
        return self.t[k]


class Ring:
    def __init__(self, bufs):
        self.bufs = bufs
        self.i = 0

    def next(self):
        b = self.bufs[self.i % len(self.bufs)]
        self.i += 1
        return b


def mult_mask(o):
    o = np.asarray(o)
    m = ((o >= 0) & (o <= 128)).astype(np.float32)
    m += ((o >= 0) & (o <= 512) & (o % 4 == 0))
    m += ((o >= 0) & (o <= 2048) & (o % 16 == 0))
    return m


def host_consts(T):
    c = {}
    bf = ml_dtypes.bfloat16
    c["c_ident_bf"] = np.eye(128, dtype=np.float32).astype(bf)
    c["c_ident_f"] = np.eye(128, dtype=np.float32)
    ps = np.zeros((128, 128), np.float32)
    for k in range(128):
        ps[k, (k + 64) % 128] = 1.0
    c["c_pswap"] = ps
    sg = np.ones((128, 1), np.float32)
    sg[64:] = -1.0
    c["c_sgn"] = sg
    half = 8
    inv = (500000.0 ** (-np.arange(half, dtype=np.float32) * (2.0 / 16))).astype(np.float32)

    def rope_tab(pos):
        ang = pos.astype(np.float32)[None, :] * inv[:, None]
        cosd = np.ones((64, len(pos)), np.float32)
        sind = np.zeros((64, len(pos)), np.float32)
        cosd[0:8] = np.cos(ang)
        cosd[8:16] = np.cos(ang)
        sind[0:8] = np.sin(ang)
        sind[8:16] = np.sin(ang)
        return np.concatenate([cosd, cosd], 0), np.concatenate([sind, sind], 0)

    c["c_cos_p"], c["c_sin_p"] = [a.astype(bf) for a in rope_tab(np.arange(T))]
    c["c_cos_s"], c["c_sin_s"] = rope_tab(8192 + (np.arange(16) % 4))
    rot = np.zeros((128, 128), np.float32)
    for hb in (0, 64):
        for j in range(8):
            rot[hb + j + 8, hb + j] = -1.0
            rot[hb + j, hb + j + 8] = 1.0
    c["c_rot"] = rot.astype(bf)
    k = np.arange(128)[:, None]
    q = np.arange(128)[None, :]
    sw = np.zeros((128, 17, 128), np.float32)
    for m in range(17):
        sw[:, m, :] = mult_mask((16 - m) * 128 + q - k)
    c["c_swam"] = sw.astype(bf)
    sws = np.zeros((128, 4, 16, 16), np.float32)
    for b in range(4):
        for t in range(16):
            for i in range(4):
                sws[:, b, t, 4 * b + i] = mult_mask(2048 + i - 128 * t - np.arange(128))
    c["c_swam_s"] = sws.astype(bf)
    swn = np.zeros((16, 16), np.float32)
    for kk in range(16):
        for qq in range(16):
            if kk // 4 == qq // 4:
                swn[kk, qq] = mult_mask(qq - kk)
    c["c_swam_n"] = swn.astype(bf)
    j = np.arange(128)[:, None]
    i = np.arange(128)[None, :]
    cz = (j <= i).astype(np.float32)
    c["c_causal_p"] = np.tile(cz, (1, 4)).astype(bf)
    czs = ((j[:16, :] <= i[:, :16]) & (j[:16, :] // 4 == i[:, :16] // 4)).astype(np.float32)
    c["c_causal_s"] = np.tile(czs, (1, 4)).astype(np.float32)
    rp = np.ones((128, BLK), np.float32)
    rp[:, ::128] = 0.0
    c["c_reset_p"] = rp
    rsx = np.ones((128, 16), np.float32)
    rsx[:, ::4] = 0.0
    c["c_reset_s"] = rsx
    hm = np.zeros((128, 4), np.float32)
    for h in range(4):
        hm[32 * h:32 * h + 32, h] = 1.0
    c["c_hm"] = hm
    bm = np.zeros((16, 4), np.float32)
    for b in range(4):
        bm[4 * b:4 * b + 4, b] = 1.0
    c["c_bm"] = bm
    bcol = np.zeros((128, 4, 16), np.float32)
    for b in range(4):
        bcol[:, b, 4 * b:4 * b + 4] = 1.0
    c["c_bcol"] = bcol
    gm = np.zeros((128, 8), np.float32)
    for g in range(8):
        gm[16 * g:16 * g + 16, g] = 1.0
    c["c_gm"] = gm
    c["c_iota"] = np.tile(np.arange(1, SSM_L + 1, dtype=np.float32)[None, :], (128, 1))
    return c


def build(T, DEPTH):
    NB = T // BLK
    NT = T // 128
    KEEP = min(LC, T)
    nc = bass.Bass("TRN2", target_bir_lowering=False)
    P = Prog(nc)
    es = ExitStack()
    cst = host_consts(T)

    def DI(name, shape, dt=F32):
        return nc.dram_tensor(name, list(shape), dt, kind="ExternalInput").ap()

    def DO(name, shape):
        return nc.dram_tensor(name, list(shape), F32, kind="ExternalOutput").ap()

    xp = DI("xp", [T, D]); xs = DI("xs", [16, D]); cT = DI("cT", [128, 8, 5])
    w_in_r = DI("w_in_r", [DEPTH, 128, 8, NCOL]); w_out_r = DI("w_out_r", [DEPTH, 128, 8, D])
    w_ada_r = DI("w_ada_r", [DEPTH, 128, 8, 3 * D]); b_ada_b = DI("b_ada_b", [DEPTH, 128, 3 * D])
    g_pre_b = DI("g_pre_b", [DEPTH, 128, D]); g_post_b = DI("g_post_b", [DEPTH, 128, D])
    g_gla_b = DI("g_gla_b", [DEPTH, 128, 256]); w_lr = DI("w_lr", [DEPTH, 16, 128]); b_lr = DI("b_lr", [DEPTH, 128, 1])
    lamF_re = DI("lamF_re", [DEPTH, 128, 16]); lamF_im = DI("lamF_im", [DEPTH, 128, 16]); logdtF = DI("logdtF", [DEPTH, 128, 16])
    lamB_re = DI("lamB_re", [DEPTH, 128, 128]); lamB_im = DI("lamB_im", [DEPTH, 128, 128]); logdtB = DI("logdtB", [DEPTH, 128, 128])
    Bt_re = DI("Bt_re", [DEPTH, 128, 128]); Bt_im = DI("Bt_im", [DEPTH, 128, 128])
    Ct_re = DI("Ct_re", [DEPTH, 64, 256]); Ct_im = DI("Ct_im", [DEPTH, 64, 256])
    dsk = DI("dsk", [DEPTH, 128, 2]); bglu = DI("bglu", [DEPTH, 128, 2]); w_glu_r = DI("w_glu_r", [DEPTH, 128, 2, 256])
    sgla = DI("sgla", [DEPTH, 4, 128, 64]); sssm = DI("sssm", [DEPTH, 128, 4, 16])
    ck = DI("ck", [DEPTH, 4, LC, 512]); cv = DI("cv", [DEPTH, 4, LC, 512])
    cin = {}
    for name, arr in cst.items():
        cin[name] = DI(name, arr.shape, BF16 if arr.dtype == ml_dtypes.bfloat16 else F32)

    yp = DO("yp", [T, D]); ys = DO("ys", [16, D])
    glap = DO("glap", [DEPTH, 128, 64]); glas = DO("glas", [DEPTH, 4, 128, 64])
    kp = DO("kp", [DEPTH, KEEP, 512]); vp = DO("vp", [DEPTH, KEEP, 512])
    ks = DO("ks", [DEPTH, 4, LC, 512]); vs = DO("vs", [DEPTH, 4, LC, 512])
    ssmp = DO("ssmp", [DEPTH, 128, 16]); ssms = DO("ssms", [DEPTH, 128, 4, 16])
    xsc = [nc.dram_tensor(f"xsc{i}", [T, D], F32, kind="Internal").ap() for i in range(2)]
    xss = [nc.dram_tensor(f"xss{i}", [16, D], F32, kind="Internal").ap() for i in range(2)]
    r_xsc = [[Res(f"xsc{i}_{b}") for b in range(NB)] for i in range(3)]
    r_xss = [Res(f"xss{i}") for i in range(3)]
    WBLOCKS = [(1296 + 128 * p, 128) for p in range(4)] + [(784 + 128 * p, 128) for p in range(4)] + \
              [(1808, 512), (2320, 512), (256, 256), (528, 256)] + [(2832 + 128 * k, 128) for k in range(2)] + \
              [(3088 + 128 * k, 128) for k in range(2)] + [(0, 128), (128, 128), (512, 16)]
    wsc = [nc.dram_tensor(f"wsc{i}", [128, 8 * NCOL], BF16, kind="Internal").ap() for i in range(2)]
    wosc = [nc.dram_tensor(f"wosc{i}", [128, 8 * D], BF16, kind="Internal").ap() for i in range(2)]
    r_wsc = [{c0: Res(f"wsc{i}_{c0}") for (c0, m) in WBLOCKS} for i in range(2)]
    r_wosc = [[Res(f"wosc{i}_{h}") for h in range(2)] for i in range(2)]

    with es:
        def sb(name, shape, dt=F32):
            return Buf(es.enter_context(nc.sbuf_tensor("sb_" + name, list(shape), dt)), Res(name))

        def ring(name, n, shape, dt=F32):
            return Ring([sb(f"{name}{i}", shape, dt) for i in range(n)])

        def rs(bufs):
            return [b.r if isinstance(b, Buf) else b for b in bufs]

        def MM(out, lhsT, rhs, R, W, start=True, stop=True):
            P.op("pe", lambda e: e.matmul(out, lhsT, rhs, start=start, stop=stop), rs(R), rs(W))

        def TRN(out, in_, ident, R, W):
            P.op("pe", lambda e: e.transpose(out, in_, ident), rs(R), rs(W))

        def ACTV(out, in_, func, R, W, bias=None, scale=None, accum=None):
            kw = {}
            if bias is not None:
                kw["bias"] = bias
            if scale is not None:
                kw["scale"] = scale
            if accum is not None:
                kw["accum_out"] = accum
            P.op("act", lambda e: e.activation(out, in_, func, **kw), rs(R), rs(W))

        def TT(eng, out, a, b, op, R, W):
            P.op(eng, lambda e: e.tensor_tensor(out, a, b, op), rs(R), rs(W))

        def TS(eng, out, a, s1, s2, op0, op1, R, W):
            if op1 is None:
                P.op(eng, lambda e: e.tensor_scalar(out, a, s1, None, op0), rs(R), rs(W))
            else:
                P.op(eng, lambda e: e.tensor_scalar(out, a, s1, s2, op0, op1), rs(R), rs(W))

        def STT(eng, out, a, s, b, op0, op1, R, W):
            P.op(eng, lambda e: e.scalar_tensor_tensor(out, a, s, b, op0, op1), rs(R), rs(W))

        def CP(eng, out, in_, R, W):
            if eng == "act":
                P.op(eng, lambda e: e.copy(out, in_), rs(R), rs(W))
            else:
                P.op(eng, lambda e: e.tensor_copy(out, in_), rs(R), rs(W))

        def MS(eng, out, val, W):
            P.op(eng, lambda e: e.memset(out, val), [], rs(W))

        def DMA(q, out, in_, R, W, owner=None):
            P.dma(q, lambda e: e.dma_start(out=out, in_=in_), rs(R), rs(W), owner.r if isinstance(owner, Buf) else owner)

        pf = Ring([Buf(es.enter_context(nc.psum_tensor(f"pf{i}", [128, 512], F32)), Res(f"pf{i}", True)) for i in range(4)])
        pacc = [Buf(es.enter_context(nc.psum_tensor(f"pacc{i}", [128, 512], F32)), Res(f"pacc{i}", True)) for i in range(2)]
        pbr = Ring([Buf(es.enter_context(nc.psum_tensor(f"pb{i}", [128, 1024], BF16)), Res(f"pb{i}", True)) for i in range(2)])

        C = {}
        for name, arr in cst.items():
            if name in ("c_cos_p", "c_sin_p"):
                continue
            C[name] = sb(name, arr.shape, BF16 if arr.dtype == ml_dtypes.bfloat16 else F32)
        for name, arr in cst.items():
            if name in ("c_cos_p", "c_sin_p"):
                continue
            DMA("sp", C[name][:], cin[name], [], [C[name]])
        ident = C["c_ident_bf"]

        w_glu = sb("w_glu", [128, 2, 256], BF16)
        wlr = sb("wlr", [16, 128], BF16)
        nblr = sb("nblr", [128, 1])
        ggla = sb("ggla", [128, 256])
        dskb = sb("dskb", [128, 2]); bglub = sb("bglub", [128, 2])
        kT = sb("kT", [128, 4, NKT * 128], BF16)
        Vr = sb("Vr", [128, NKT, 8, 65], BF16)
        kTs = sb("kTs", [128, 4, 16], BF16)
        Vsn = sb("Vsn", [16, 8, 65], BF16)
        MS("pool", Vr[:], 1.0, [Vr])
        MS("pool", Vsn[:], 1.0, [Vsn])
        modp = [sb(f"modp{i}", [128, D], BF16) for i in range(3)]
        mods = [sb(f"mods{i}", [16, D], BF16) for i in range(3)]
        scb_p = sb("scb_p", [128, 8, 128], BF16)
        scb_s = sb("scb_s", [128, 8, 16], BF16)
        cosT = sb("cosT", [128, 16, SSM_L], BF16); sinT = sb("sinT", [128, 16, SSM_L], BF16)
        cosS = sb("cosS", [128, 16, 16], BF16); sinS = sb("sinS", [128, 16, 16], BF16)
        rhoF = sb("rhoF", [128, 16]); rhoS = sb("rhoS", [128, 16, 16])
        LA = sb("LA", [128, 16, 128], BF16); LB = sb("LB", [128, 16, 128], BF16)
        C1g = sb("C1g", [128, 16, 128], BF16); C2g = sb("C2g", [128, 16, 128], BF16)
        MS("pool", C1g[:], 0.0, [C1g]); MS("pool", C2g[:], 0.0, [C2g])
        RotP = sb("RotP", [128, 16, 128]); RotS = RotP
        Sg = sb("Sg", [128, 64]); Sgb = sb("Sgb", [128, 64], BF16)
        Sgs = sb("Sgs", [128, 4, 64]); Sgsb = sb("Sgsb", [128, 4, 64], BF16)
        winit = sb("winit", [128, 16])
        x0s = sb("x0s", [128, 4, 16])

        xt_r = ring("xt", 1, [128, D])
        st_r = ring("st", 4, [128, 4])
        hb_r = ring("hb", 1, [128, D], BF16)
        junk_r = hb_r
        hT = sb("hT", [128, 8, BLK], BF16)
        qT = sb("qT", [128, 4, BLK], BF16)
        xb_r = ring("xb", 1, [128, BLK], BF16)
        f5_r = ring("f5", 2, [128, 512])
        b5_r = ring("b5", 1, [128, 128], BF16)
        sbg = sb("sbg", [128, NTB, 512], BF16)
        sag = sb("sag", [128, NTB, 256], BF16)
        vg = sb("vg", [128, NTB, 256], BF16)
        uT = sb("uT", [128, 2, BLK], BF16)
        scg = sb("scg", [128, 2, BLK], BF16)
        mixT = sb("mixT", [128, 8, BLK], BF16)
        otm = sb("otm", [128, NTB, 768], BF16)
        qd = sb("qd", [128, BLK], BF16); kd = sb("kd", [128, BLK], BF16); kdec = sb("kdec", [128, BLK], BF16)
        qdh = [sb(f"qdh{h}", [128, BLK], BF16) for h in range(4)]
        qdhb = sb("qdhb", [128, 16, 16], BF16)
        E1 = sb("E1", [128, BLK]); E2 = sb("E2", [128, BLK]); csb = sb("csb", [128, BLK])
        alrT = sb("alrT", [16, BLK], BF16)
        kdTh = ring("kdTh", 1, [128, 4, 128], BF16)
        for b_ in kdTh.bufs:
            MS("pool", b_[:], 0.0, [b_])
        kdThs = sb("kdThs", [16, 16, 128], BF16)
        MS("pool", kdThs[:], 0.0, [kdThs])
        attT_r = ring("attT", 1, [128, 512], BF16)
        osb_r = ring("osb", 1, [128, 256]); osq_r = ring("osq", 1, [128, 256])
        kout_r = ring("kout", 1, [128, 512]); vout_r = kout_r
        yo_r = ring("yo", 1, [128, D])
        wst_r = ring("wst", 2, [128, 8, 512], BF16)
        pt_r = ring("ptr", 2, [128, 512], BF16); pm_r = ring("pmr", 2, [128, 512], BF16)
        ssmf_r = ring("ssmf", 2, [128, SSM_L]); ssmw_r = ring("ssmw", 2, [128, SSM_L])
        ssmu_r = ring("ssmu", 2, [128, SSM_L], BF16)
        yt = sb("yt", [128, 2, SSM_L]); zT = sb("zT", [128, 2, SSM_L], BF16); sgs = sb("sgs", [128, 2, SSM_L], BF16)
        ctile_k = ring("ctk", 1, [128, 512]); ctile_v = ring("ctv", 1, [128, 512])
        ckb_r = ring("ckb", 1, [128, 512], BF16); cvb_r = ring("cvb", 2, [128, 8, 65], BF16)
        for b_ in cvb_r.bufs:
            MS("pool", b_[:], 1.0, [b_])
        ckT_r = ring("ckT", 2, [128, 4, 128], BF16)
        sm = {}
        for nm in ("dtF", "aF", "thF", "t1", "t2", "cL", "sL", "sLs"):
            sm[nm] = sb("sm_" + nm, [128, 16])
        sB = {}
        for nm in ("lre", "lim", "ldt", "dt", "a", "th", "rho", "c", "s", "lbr", "lbi", "nr", "ni", "den", "kr", "ki", "bre", "bim", "t1", "t2", "t3"):
            sB[nm] = sb("sB_" + nm, [128, 128])
        BbA = sb("BbA", [128, 2, 128]); BbB = sb("BbB", [128, 2, 128])
        A1 = sb("A1", [128, 256]); A2 = sb("A2", [128, 256])
        angt = ring("angt", 2, [128, SSM_L])
        rtab_r = ring("rtab", 1, [128, 2, BLK], BF16)


        I32 = mybir.dt.int32
        sc_f = [sb(f"sc_f{i}", [128, 128]) for i in range(3)]
        sc_i = sb("sc_i", [128, 128], I32)

        def sincos(sin_out, cos_out, ang, w, Rin, Wout):
            tA, tB, tC = sc_f
            TS("dve", tA[:, 0:w], ang, 1.0 / (2 * PI), None, ALU.mult, None, Rin, [tA])
            CP("dve", sc_i[:, 0:w], tA[:, 0:w], [tA], [sc_i])
            CP("dve", tB[:, 0:w], sc_i[:, 0:w], [sc_i], [tB])
            TT("dve", tA[:, 0:w], tA[:, 0:w], tB[:, 0:w], ALU.subtract, [tA, tB], [tA])
            ACTV(tB[:, 0:w], tA[:, 0:w], AF.Sin, [tA], [tB], scale=PI)
            ACTV(tC[:, 0:w], tA[:, 0:w], AF.Sin, [tA], [tC], scale=PI / 2)
            TT("dve", tA[:, 0:w], tB[:, 0:w], tB[:, 0:w], ALU.mult, [tB], [tA])
            TS("dve", cos_out, tA[:, 0:w], -2.0, 1.0, ALU.mult, ALU.add, [tA], Wout)
            TT("dve", tC[:, 0:w], tC[:, 0:w], tC[:, 0:w], ALU.mult, [tC], [tC])
            TS("dve", tC[:, 0:w], tC[:, 0:w], -2.0, 1.0, ALU.mult, ALU.add, [tC], [tC])
            TT("dve", tC[:, 0:w], tC[:, 0:w], tB[:, 0:w], ALU.mult, [tC, tB], [tC])
            TS("dve", sin_out, tC[:, 0:w], 2.0, None, ALU.mult, None, [tC], Wout)

        nPI = sb("nPI", [128, 1])
        MS("dve", nPI[:], -PI, [nPI])
        one_c = sb("one_c", [128, 1])
        MS("dve", one_c[:], 1.0, [one_c])
        eps_c = sb("eps_c", [128, 1])
        MS("dve", eps_c[:], EPS, [eps_c])

        def layer_setup(l):
            for (c0, m) in WBLOCKS:
                wt_ = wst_r.next()
                DMA("pool", wt_[:, :, 0:m], w_in_r[l, :, :, c0:c0 + m], [], [wt_])
                DMA("sp", wsc[l % 2][:, 8 * c0:8 * c0 + 8 * m].rearrange("p (k m) -> p k m", m=m), wt_[:, :, 0:m], [wt_], [r_wsc[l % 2][c0]], owner=wt_)
            for half in range(2):
                wt_ = wst_r.next()
                DMA("pool", wt_[:], w_out_r[l, :, :, half * 512:(half + 1) * 512], [], [wt_])
                DMA("sp", wosc[l % 2][:, half * 4096:(half + 1) * 4096].rearrange("p (k m) -> p k m", m=512), wt_[:], [wt_], [r_wosc[l % 2][half]], owner=wt_)
            DMA("pool", w_glu[:], w_glu_r[l], [], [w_glu])
            DMA("pool", wlr[:], w_lr[l], [], [wlr])
            DMA("sp", nblr[:], b_lr[l], [], [nblr])
            TS("dve", nblr[:], nblr[:], -1.0, None, ALU.mult, None, [nblr], [nblr])
            DMA("sp", ggla[:], g_gla_b[l], [], [ggla])
            gpost = xt_r.next()
            DMA("sp", gpost[:], g_post_b[l], [], [gpost])
            DMA("sp", dskb[:], dsk[l], [], [dskb])
            DMA("sp", bglub[:], bglu[l], [], [bglub])
            if l == 0:
                sc = f5_r.next()
                scv = sc[:, 0:40].rearrange("p (k r) -> p k r", r=5)
                DMA("sp", scv, cT[:, :, :], [], [sc])
                ACTV(scv, scv, AF.Silu, [sc], [sc])
                CP("dve", scb_p[:], scv[:, :, 0:1].to_broadcast([128, 8, 128]), [sc], [scb_p])
                for b in range(4):
                    CP("dve", scb_s[:, :, 4 * b:4 * b + 4], scv[:, :, 1 + b:2 + b].to_broadcast([128, 8, 4]), [sc], [scb_s])
            gp2 = yo_r.next()
            DMA("sp", gp2[:], g_pre_b[l], [], [gp2])
            for cb in range(6):
                wst = wst_r.next()
                DMA("pool", wst[:], w_ada_r[l, :, :, cb * 512:(cb + 1) * 512], [], [wst])
                bad = f5_r.next()
                DMA("sp", bad[:], b_ada_b[l, :, cb * 512:(cb + 1) * 512], [], [bad])
                part = cb // 2
                co = (cb % 2) * 512
                for (lhs, npart, mod) in ((scb_p, 128, modp), (scb_s, 16, mods)):
                    ps = pf.next()
                    for k in range(8):
                        MM(ps[0:npart, :], lhs[:, k, 0:npart], wst[:, k, :], [lhs, wst], [ps], start=(k == 0), stop=(k == 7))
                    tmp = f5_r.next()
                    TT("dve", tmp[0:npart, :], ps[0:npart, :], bad[0:npart, :], ALU.add, [ps, bad], [tmp])
                    if part == 0:
                        CP("pool", mod[1][0:npart, co:co + 512], tmp[0:npart, :], [tmp], [mod[1]])
                    elif part == 1:
                        STT("dve", mod[0][0:npart, co:co + 512], tmp[0:npart, :], 1.0, gp2[0:npart, co:co + 512], ALU.add, ALU.mult, [tmp, gp2], [mod[0]])
                    else:
                        TT("dve", mod[2][0:npart, co:co + 512], tmp[0:npart, :], gpost[0:npart, co:co + 512], ALU.mult, [tmp, gpost], [mod[2]])
            lre = sm["t1"]; lim = sm["t2"]
            DMA("sp", lre[:], lamF_re[l], [], [lre]); DMA("sp", lim[:], lamF_im[l], [], [lim])
            DMA("sp", sm["dtF"][:], logdtF[l], [], [sm["dtF"]])
            ACTV(sm["dtF"][:], sm["dtF"][:], AF.Exp, [sm["dtF"]], [sm["dtF"]])
            TT("dve", sm["aF"][:], lre[:], sm["dtF"][:], ALU.mult, [lre, sm["dtF"]], [sm["aF"]])
            TT("dve", sm["thF"][:], lim[:], sm["dtF"][:], ALU.mult, [lim, sm["dtF"]], [sm["thF"]])
            ACTV(rhoF[:], sm["aF"][:], AF.Exp, [sm["aF"]], [rhoF])
            TT("dve", rhoS[:], rhoF[:].unsqueeze(2).to_broadcast([128, 16, 16]),
               C["c_reset_s"][:].unsqueeze(1).to_broadcast([128, 16, 16]), ALU.mult, [rhoF, C["c_reset_s"]], [rhoS])
            for g in range(16):
                a1 = angt.next()
                TS("dve", a1[:], C["c_iota"][:], sm["thF"][:, g:g + 1], None, ALU.mult, None, [C["c_iota"], sm["thF"]], [a1])
                sincos(sinT[:, g, :], cosT[:, g, :], a1[:], SSM_L, [a1], [sinT, cosT])
            for b in range(4):
                CP("pool", cosS[:, :, 4 * b:4 * b + 4], cosT[:, :, 0:4], [cosT], [cosS])
                CP("pool", sinS[:, :, 4 * b:4 * b + 4], sinT[:, :, 0:4], [sinT], [sinS])
            build_rot(float(SSM_L))

        def build_rot(Lr):
            Rot = RotP
            if True:
                TS("dve", sm["t1"][:], sm["thF"][:], Lr, None, ALU.mult, None, [sm["thF"]], [sm["t1"]])
                sincos(sm["sL"][:], sm["cL"][:], sm["t1"][:], 16, [sm["t1"]], [sm["sL"], sm["cL"]])
                TS("dve", sm["sLs"][:], sm["sL"][:], C["c_sgn"][:, 0:1], None, ALU.mult, None, [sm["sL"], C["c_sgn"]], [sm["sLs"]])
                for g in range(16):
                    TS("dve", Rot[:, g, :], C["c_ident_f"][:], sm["cL"][:, g:g + 1], None, ALU.mult, None, [C["c_ident_f"], sm["cL"]], [Rot])
                    STT("dve", Rot[:, g, :], C["c_pswap"][:], sm["sLs"][:, g:g + 1], Rot[:, g, :], ALU.mult, ALU.add, [C["c_pswap"], sm["sLs"], Rot], [Rot])

        def layer_setup2(l):
            q = sB
            DMA("sp", q["lre"][:], lamB_re[l], [], [q["lre"]]); DMA("sp", q["lim"][:], lamB_im[l], [], [q["lim"]])
            DMA("sp", q["ldt"][:], logdtB[l], [], [q["ldt"]])
            DMA("sp", q["bre"][:], Bt_re[l], [], [q["bre"]]); DMA("sp", q["bim"][:], Bt_im[l], [], [q["bim"]])
            ACTV(q["dt"][:], q["ldt"][:], AF.Exp, [q["ldt"]], [q["dt"]])
            TT("dve", q["a"][:], q["lre"][:], q["dt"][:], ALU.mult, [q["lre"], q["dt"]], [q["a"]])
            TT("dve", q["th"][:], q["lim"][:], q["dt"][:], ALU.mult, [q["lim"], q["dt"]], [q["th"]])
            ACTV(q["rho"][:], q["a"][:], AF.Exp, [q["a"]], [q["rho"]])
            sincos(q["s"][:], q["c"][:], q["th"][:], 128, [q["th"]], [q["s"], q["c"]])
            STT("dve", q["lbr"][:], q["rho"][:], 1.0, q["c"][:], ALU.mult, ALU.mult, [q["rho"], q["c"]], [q["lbr"]])
            TS("dve", q["lbr"][:], q["lbr"][:], -1.0, None, ALU.add, None, [q["lbr"]], [q["lbr"]])
            TT("dve", q["lbi"][:], q["rho"][:], q["s"][:], ALU.mult, [q["rho"], q["s"]], [q["lbi"]])
            TT("dve", q["t1"][:], q["lbr"][:], q["lre"][:], ALU.mult, [q["lbr"], q["lre"]], [q["t1"]])
            TT("dve", q["t2"][:], q["lbi"][:], q["lim"][:], ALU.mult, [q["lbi"], q["lim"]], [q["t2"]])
            TT("dve", q["nr"][:], q["t1"][:], q["t2"][:], ALU.add, [q["t1"], q["t2"]], [q["nr"]])
            TT("dve", q["t1"][:], q["lbi"][:], q["lre"][:], ALU.mult, [q["lbi"], q["lre"]], [q["t1"]])
            TT("dve", q["t2"][:], q["lbr"][:], q["lim"][:], ALU.mult, [q["lbr"], q["lim"]], [q["t2"]])
            TT("dve", q["ni"][:], q["t1"][:], q["t2"][:], ALU.subtract, [q["t1"], q["t2"]], [q["ni"]])
            TT("dve", q["t1"][:], q["lre"][:], q["lre"][:], ALU.mult, [q["lre"]], [q["t1"]])
            TT("dve", q["t2"][:], q["lim"][:], q["lim"][:], ALU.mult, [q["lim"]], [q["t2"]])
            TT("dve", q["den"][:], q["t1"][:], q["t2"][:], ALU.add, [q["t1"], q["t2"]], [q["den"]])
            P.op("dve", lambda e: e.reciprocal(q["den"][:], q["den"][:]), rs([q["den"]]), rs([q["den"]]))
            TT("dve", q["kr"][:], q["nr"][:], q["den"][:], ALU.mult, [q["nr"], q["den"]], [q["kr"]])
            TT("dve", q["ki"][:], q["ni"][:], q["den"][:], ALU.mult, [q["ni"], q["den"]], [q["ki"]])
            TT("dve", q["t1"][:], q["kr"][:], q["bre"][:], ALU.mult, [q["kr"], q["bre"]], [q["t1"]])
            TT("dve", q["t2"][:], q["ki"][:], q["bim"][:], ALU.mult, [q["ki"], q["bim"]], [q["t2"]])
            TT("dve", q["t3"][:], q["t1"][:], q["t2"][:], ALU.subtract, [q["t1"], q["t2"]], [q["t3"]])
            TT("dve", q["t1"][:], q["kr"][:], q["bim"][:], ALU.mult, [q["kr"], q["bim"]], [q["t1"]])
            TT("dve", q["t2"][:], q["ki"][:], q["bre"][:], ALU.mult, [q["ki"], q["bre"]], [q["t2"]])
            TT("dve", q["t1"][:], q["t1"][:], q["t2"][:], ALU.add, [q["t1"], q["t2"]], [q["t1"]])
            for gc in range(2):
                CP("dve", BbA[:, gc, 0:64], q["t3"][:, gc * 64:(gc + 1) * 64], [q["t3"]], [BbA])
                CP("dve", BbA[:, gc, 64:128], q["t1"][:, gc * 64:(gc + 1) * 64], [q["t1"]], [BbA])
                CP("dve", BbB[:, gc, 0:64], q["t1"][:, gc * 64:(gc + 1) * 64], [q["t1"]], [BbB])
                TS("dve", BbB[:, gc, 64:128], q["t3"][:, gc * 64:(gc + 1) * 64], -1.0, None, ALU.mult, None, [q["t3"]], [BbB])
            for g in range(16):
                gc, gl = g // 8, g % 8
                TS("dve", LA[:, g, :], BbA[:, gc, :], C["c_gm"][:, gl:gl + 1], None, ALU.mult, None, [BbA, C["c_gm"]], [LA])
                TS("dve", LB[:, g, :], BbB[:, gc, :], C["c_gm"][:, gl:gl + 1], None, ALU.mult, None, [BbB, C["c_gm"]], [LB])
            DMA("sp", A1[0:64, :], Ct_re[l], [], [A1]); DMA("sp", A1[64:128, :], Ct_im[l], [], [A1])
            DMA("sp", A2[0:64, :], Ct_im[l], [], [A2]); DMA("sp", A2[64:128, :], Ct_re[l], [], [A2])
            TS("dve", A1[64:128, :], A1[64:128, :], -1.0, None, ALU.mult, None, [A1], [A1])
            TS("dve", A2[:], A2[:], -1.0, None, ALU.mult, None, [A2], [A2])
            for g in range(16):
                gl = g % 8
                CP("pool", C1g[:, g, 16 * gl:16 * gl + 16], A1[:, 16 * g:16 * g + 16], [A1], [C1g])
                CP("pool", C2g[:, g, 16 * gl:16 * gl + 16], A2[:, 16 * g:16 * g + 16], [A2], [C2g])

        import os
        STAGE = int(os.environ.get("MK_STAGE", "99"))

        class Stop(Exception):
            pass

        def chk(k):
            if STAGE == k:
                raise Stop()

        ROPES = [0]

        def do_block(l, mode, blk):
            if mode == "p":
                n, nt = 128, NTB
                mod = modp
            else:
                n, nt = 16, 1
                mod = mods
            NTOK = n * nt
            last = (l == DEPTH - 1)

            def chk(k):
                if STAGE >= 100:
                    if mode == "s" and STAGE - 100 == k:
                        raise Stop()
                elif mode == "p" and STAGE == k:
                    raise Stop()

            def xsrc(i):
                if mode == "p":
                    t0 = blk * BLK + i * 128
                    if l == 0:
                        return xp[t0:t0 + 128, :], []
                    return xsc[(l - 1) % 2][t0:t0 + 128, :], [r_xsc[(l - 1) % 2][blk]]
                if l == 0:
                    return xs[:, :], []
                return xss[(l - 1) % 2][:, :], [r_xss[(l - 1) % 2]]

            def xdst(i):
                if mode == "p":
                    t0 = blk * BLK + i * 128
                    if last:
                        return yp[t0:t0 + 128, :], []
                    return xsc[l % 2][t0:t0 + 128, :], [r_xsc[l % 2][blk]]
                if last:
                    return ys[:, :], []
                return xss[l % 2][:, :], [r_xss[l % 2]]

            for i in range(nt):
                xt = xt_r.next()
                src, rsrc = xsrc(i)
                DMA("sp", xt[0:n, :], src, rsrc, [xt])
                jk = junk_r.next(); st = st_r.next()
                ACTV(jk[0:n, :], xt[0:n, :], AF.Square, [xt], [jk, st], accum=st[0:n, 0:1])
                ACTV(st[0:n, 1:2], st[0:n, 0:1], AF.Sqrt, [st, eps_c], [st], bias=eps_c[0:n, 0:1], scale=1.0 / D)
                P.op("dve", lambda e, st=st: e.reciprocal(st[0:n, 2:3], st[0:n, 1:2]), rs([st]), rs([st]))
                hf = yo_r.next()
                STT("dve", hf[0:n, :], xt[0:n, :], st[0:n, 2:3], mod[0][0:n, :], ALU.mult, ALU.mult, [xt, st, mod[0]], [hf])
                hb = hb_r.next()
                TT("dve", hb[0:n, :], hf[0:n, :], mod[1][0:n, :], ALU.add, [hf, mod[1]], [hb])
                pb = pbr.next()
                for k in range(8):
                    TRN(pb[:, k * 128:k * 128 + n], hb[0:n, k * 128:(k + 1) * 128], ident[0:n, 0:n], [hb, ident], [pb])
                CP("act", hT[:, :, i * n:(i + 1) * n], pb[:].rearrange("p (k t) -> p k t", t=128)[:, :, 0:n], [pb], [hT])

            def wload(c0, m):
                wb_ = wst_r.next()
                DMA("sp", wb_[:, :, 0:m], wsc[l % 2][:, 8 * c0:8 * c0 + 8 * m].rearrange("p (k m) -> p k m", m=m), [r_wsc[l % 2][c0]], [wb_])
                return wb_

            chk(31)
            def fm(c0, m):
                wb_ = wload(c0, m)
                ps = pf.next()
                for k in range(8):
                    MM(ps[0:m, 0:NTOK], wb_[:, k, 0:m], hT[:, k, 0:NTOK], [wb_, hT], [ps], start=(k == 0), stop=(k == 7))
                return ps

            def tm(i, wb_, wd, ps, o0):
                for k in range(8):
                    MM(ps[0:n, o0:o0 + wd], hT[:, k, i * n:(i + 1) * n], wb_[:, k, 0:wd], [wb_, hT], [ps], start=(k == 0), stop=(k == 7))

            if mode == "p":
                rt = rtab_r.next()
                DMA("sp", rt[:, 0, 0:BLK], cin["c_cos_p"][:, blk * BLK:(blk + 1) * BLK], [], [rt])
                DMA("sp", rt[:, 1, 0:BLK], cin["c_sin_p"][:, blk * BLK:(blk + 1) * BLK], [], [rt])
                cosb, sinb, rtb = rt[:, 0, 0:BLK], rt[:, 1, 0:BLK], rt
                s0 = (blk * NTB) % NKT
                kdst = lambda p: kT[:, p, s0 * 128:s0 * 128 + BLK]
                kdst_b = kT
            else:
                cosb, sinb, rtb = C["c_cos_s"][:], C["c_sin_s"][:], C["c_cos_s"]
                kdst = lambda p: kTs[:, p, :]
                kdst_b = kTs

            def rope(ps, dst, dstb, scale):
                xb = xb_r.next()
                P.op("act", lambda e, xb=xb, ps=ps: e.mul(xb[:, 0:NTOK], ps[:, 0:NTOK], float(scale)), rs([ps]), rs([xb]))
                ps2 = pf.next()
                MM(ps2[:, 0:NTOK], C["c_rot"][:], xb[:, 0:NTOK], [C["c_rot"], xb], [ps2])
                chk(321)
                t1 = f5_r.next(); t2 = f5_r.next()
                STT("dve", t1[:, 0:NTOK], ps[:, 0:NTOK], scale, cosb, ALU.mult, ALU.mult, [ps, rtb, C["c_sin_s"]], [t1])
                TT("dve", t2[:, 0:NTOK], ps2[:, 0:NTOK], sinb, ALU.mult, [ps2, rtb, C["c_sin_s"]], [t2])
                chk(322)
                TT("dve", dst, t1[:, 0:NTOK], t2[:, 0:NTOK], ALU.add, [t1, t2], [dstb])
                ROPES[0] += 1
                if ROPES[0] == int(os.environ.get("MK_ROPES", "0")):
                    raise Stop()
                chk(323)

            for p in range(4):
                ps = fm(1296 + 128 * p, 128)
                chk(32)
                rope(ps, kdst(p), kdst_b, 1.0)
            for p in range(4):
                ps = fm(784 + 128 * p, 128)
                rope(ps, qT[:, p, 0:NTOK], qT, 0.125)
            chk(33)
            if mode == "p":
                for i in range(nt):
                    tok0 = blk * BLK + i * 128
                    if tok0 >= T - KEEP:
                        pb = pbr.next()
                        sl = (blk * NTB + i) % NKT
                        for p in range(4):
                            TRN(pb[:, p * 128:(p + 1) * 128], kT[:, p, sl * 128:(sl + 1) * 128], ident[:], [kT, ident], [pb])
                        ko = kout_r.next()
                        CP("act", ko[:], pb[:, 0:512], [pb], [ko])
                        DMA("sp", kp[l, tok0 - (T - KEEP):tok0 - (T - KEEP) + 128, :], ko[:], [ko], [])
            else:
                pb = pbr.next()
                for p in range(4):
                    TRN(pb[0:16, p * 128:(p + 1) * 128], kTs[:, p, :], ident[:], [kTs, ident], [pb])
                ko = kout_r.next()
                CP("act", ko[0:16, :], pb[0:16, 0:512], [pb], [ko])
                for b in range(4):
                    DMA("sp", ks[l, b, LC - 4:LC, :], ko[4 * b:4 * b + 4, :], [ko], [])
            chk(34)
            wv = wload(1808, 512)
            for i in range(nt):
                ps = pf.next()
                tm(i, wv, 512, ps, 0)
                vo = vout_r.next()
                if mode == "p":
                    sl = (blk * NTB + i) % NKT
                    CP("act", Vr[:, sl, :, 0:64], ps[:, :].rearrange("p (h d) -> p h d", d=64), [ps], [Vr])
                    tok0 = blk * BLK + i * 128
                    if tok0 >= T - KEEP:
                        CP("dve", vo[:], ps[:, :], [ps], [vo])
                        DMA("sp", vp[l, tok0 - (T - KEEP):tok0 - (T - KEEP) + 128, :], vo[:], [vo], [])
                else:
                    CP("act", Vsn[:, :, 0:64], ps[0:16, :].rearrange("p (h d) -> p h d", d=64), [ps], [Vsn])
                    CP("dve", vo[0:16, :], ps[0:16, :], [ps], [vo])
                    for b in range(4):
                        DMA("sp", vs[l, b, LC - 4:LC, :], vo[4 * b:4 * b + 4, :], [vo], [])
            wg = wload(2320, 512)
            for i in range(nt):
                ps = pf.next()
                tm(i, wg, 512, ps, 0)
                ACTV(sbg[0:n, i, :], ps[0:n, :], AF.Silu, [ps], [sbg])
            wa1 = wload(256, 256)
            wa2 = wload(528, 256)
            for i in range(nt):
                ps = pf.next()
                tm(i, wa1, 256, ps, 0)
                tm(i, wa2, 256, ps, 256)
                CP("dve", vg[0:n, i, :], ps[0:n, 0:256], [ps], [vg])
                ACTV(sag[0:n, i, :], ps[0:n, 256:512], AF.Silu, [ps], [sag])
            chk(35)
            for k2 in range(2):
                ps = fm(2832 + 128 * k2, 128)
                CP("act", uT[:, k2, 0:NTOK], ps[:, 0:NTOK], [ps], [uT])
                ps = fm(3088 + 128 * k2, 128)
                ACTV(scg[:, k2, 0:NTOK], ps[:, 0:NTOK], AF.Silu, [ps], [scg])

            chk(36)
            psq = fm(0, 128)
            psk = fm(128, 128)
            psa = fm(512, 16)
            CP("act", alrT[:, 0:NTOK], psa[0:16, 0:NTOK], [psa], [alrT])
            psl = pf.next()
            MM(psl[:, 0:NTOK], wlr[:], alrT[:, 0:NTOK], [wlr, alrT], [psl])
            t1 = f5_r.next()
            ACTV(t1[:, 0:NTOK], psl[:, 0:NTOK], AF.Exp, [psl, nblr], [t1], bias=nblr[:, 0:1], scale=-1.0)
            ACTV(t1[:, 0:NTOK], t1[:, 0:NTOK], AF.Ln, [t1, one_c], [t1], bias=one_c[:, 0:1])
            rmask = C["c_reset_p"] if mode == "p" else C["c_reset_s"]
            P.op("dve", lambda e, t1=t1: e.tensor_tensor_scan(csb[:, 0:NTOK], rmask[:, 0:NTOK], t1[:, 0:NTOK], 0.0, ALU.mult, ALU.add),
                 rs([rmask, t1]), rs([csb]))
            ACTV(E1[:, 0:NTOK], csb[:, 0:NTOK], AF.Exp, [csb], [E1], scale=-1.0 / 16)
            ACTV(E2[:, 0:NTOK], csb[:, 0:NTOK], AF.Exp, [csb], [E2], scale=1.0 / 16)
            STT("dve", qd[:, 0:NTOK], psq[:, 0:NTOK], 32 ** -0.5, E1[:, 0:NTOK], ALU.mult, ALU.mult, [psq, E1], [qd])
            TT("dve", kd[:, 0:NTOK], psk[:, 0:NTOK], E2[:, 0:NTOK], ALU.mult, [psk, E2], [kd])
            cl = 128 if mode == "p" else 4
            nch = NTOK // cl
            TT("dve", kdec[:, 0:NTOK].rearrange("p (c t) -> p c t", t=cl), kd[:, 0:NTOK].rearrange("p (c t) -> p c t", t=cl),
               E1[:, cl - 1:NTOK:cl].unsqueeze(2).to_broadcast([128, nch, cl]), ALU.mult, [kd, E1], [kdec])
            for h in range(4):
                TS("dve", qdh[h][:, 0:NTOK], qd[:, 0:NTOK], C["c_hm"][:, h:h + 1], None, ALU.mult, None, [qd, C["c_hm"]], [qdh[h]])
            if mode == "s":
                for h in range(4):
                    for b in range(4):
                        TT("dve", qdhb[:, h * 4 + b, :], qdh[h][:, 0:16], C["c_bcol"][:, b, :], ALU.mult, [qdh[h], C["c_bcol"]], [qdhb])
            causal = C["c_causal_p"] if mode == "p" else C["c_causal_s"]
            for i in range(nt):
                tk = slice(i * n, (i + 1) * n)
                pb = pbr.next()
                TRN(pb[0:n, 0:128], kdec[:, tk], ident[:], [kdec, ident], [pb])
                if mode == "p":
                    kth = kdTh.next()
                    for h in range(4):
                        CP("dve", kth[0:n, h, 32 * h:32 * h + 32], pb[0:n, 32 * h:32 * h + 32], [pb], [kth])
                else:
                    kth = kdThs
                    kraw = b5_r.next()
                    CP("dve", kraw[0:16, 0:128], pb[0:16, 0:128], [pb], [kraw])
                    for h in range(4):
                        for b in range(4):
                            TS("dve", kdThs[0:16, h * 4 + b, 32 * h:32 * h + 32], kraw[0:16, 32 * h:32 * h + 32], C["c_bm"][0:16, b:b + 1], None,
                               ALU.mult, None, [kraw, C["c_bm"]], [kdThs])
                psA = pf.next()
                for h in range(4):
                    MM(psA[0:n, h * n:(h + 1) * n], kd[:, tk], qdh[h][:, tk], [kd, qdh[h]], [psA])
                att = attT_r.next()
                TT("dve", att[0:n, 0:4 * n], psA[0:n, 0:4 * n], causal[0:n, 0:4 * n], ALU.mult, [psA, causal], [att])
                pso = pf.next()
                for h in range(4):
                    MM(pso[0:n, 64 * h:64 * h + 64], att[0:n, h * n:(h + 1) * n], vg[0:n, i, 64 * h:64 * h + 64], [att, vg], [pso], start=True, stop=False)
                    if mode == "p":
                        MM(pso[0:n, 64 * h:64 * h + 64], qdh[h][:, tk], Sgb[:], [qdh[h], Sgb], [pso], start=False, stop=True)
                    else:
                        for b in range(4):
                            MM(pso[0:n, 64 * h:64 * h + 64], qdhb[:, h * 4 + b, :], Sgsb[:, b, :], [qdhb, Sgsb], [pso], start=False, stop=(b == 3))
                if mode == "p":
                    psd = pf.next()
                    for h in range(4):
                        MM(psd[:, 0:64], kth[0:n, h, :], vg[0:n, i, 64 * h:64 * h + 64], [kth, vg], [psd], start=(h == 0), stop=(h == 3))
                    STT("dve", Sg[:], Sg[:], E1[:, i * 128 + 127:i * 128 + 128], psd[:, 0:64], ALU.mult, ALU.add, [Sg, E1, psd], [Sg])
                    CP("pool", Sgb[:], Sg[:], [Sg], [Sgb])
                else:
                    psd = pf.next()
                    for b in range(4):
                        for h in range(4):
                            MM(psd[:, 64 * b:64 * b + 64], kdThs[0:16, h * 4 + b, :], vg[0:16, 0, 64 * h:64 * h + 64], [kdThs, vg], [psd], start=(h == 0), stop=(h == 3))
                    for b in range(4):
                        STT("dve", Sgs[:, b, :], Sgs[:, b, :], E1[:, 4 * b + 3:4 * b + 4], psd[:, 64 * b:64 * b + 64], ALU.mult, ALU.add, [Sgs, E1, psd], [Sgs])
                osb = osb_r.next(); osq = osq_r.next(); st = st_r.next()
                CP("act", osb[0:n, :], pso[0:n, 0:256], [pso], [osb])
                TT("dve", osq[0:n, :], osb[0:n, :], osb[0:n, :], ALU.mult, [osb], [osq])
                P.op("dve", lambda e, st=st, osq=osq: e.tensor_reduce(st[0:n, 0:4], osq[0:n, :].rearrange("p (h d) -> p h d", d=64), AX.X, ALU.add),
                     rs([osq]), rs([st]))
                ACTV(st[0:n, 0:4], st[0:n, 0:4], AF.Sqrt, [st, eps_c], [st], bias=eps_c[0:n, 0:1], scale=1.0 / 64)
                P.op("dve", lambda e, st=st: e.reciprocal(st[0:n, 0:4], st[0:n, 0:4]), rs([st]), rs([st]))
                TT("dve", osb[0:n, :].rearrange("p (h d) -> p h d", d=64), osb[0:n, :].rearrange("p (h d) -> p h d", d=64),
                   st[0:n, 0:4].unsqueeze(2).to_broadcast([n, 4, 64]), ALU.mult, [osb, st], [osb])
                TT("dve", osb[0:n, :], osb[0:n, :], ggla[0:n, :], ALU.mult, [osb, ggla], [osb])
                TT("dve", otm[0:n, i, 0:256], osb[0:n, :], sag[0:n, i, :], ALU.mult, [osb, sag], [otm])

            chk(37)
            Ls = SSM_L if mode == "p" else 16
            for c0 in range(0, NTOK, Ls):
                tk = slice(c0, c0 + Ls)
                psY, psW = pacc
                for g in range(16):
                    gc, gl = g // 8, g % 8
                    ps = pf.next()
                    MM(ps[:, 0:Ls], LA[:, g, :], uT[:, gc, tk], [LA, uT], [ps])
                    MM(ps[:, 256:256 + Ls], LB[:, g, :], uT[:, gc, tk], [LB, uT], [ps])
                    ct, stb = (cosT, sinT) if mode == "p" else (cosS, sinS)
                    f1 = ssmf_r.next(); f2 = ssmf_r.next()
                    TT("dve", f1[:, 0:Ls], ps[:, 0:Ls], ct[:, g, 0:Ls], ALU.mult, [ps, ct], [f1])
                    TT("dve", f2[:, 0:Ls], ps[:, 256:256 + Ls], stb[:, g, 0:Ls], ALU.mult, [ps, stb], [f2])
                    TT("dve", f1[:, 0:Ls], f1[:, 0:Ls], f2[:, 0:Ls], ALU.add, [f1, f2], [f1])
                    wb = ssmw_r.next()
                    if mode == "p":
                        P.op("dve", lambda e, wb=wb, f1=f1, g=g: e.tensor_tensor_scan(wb[:, 0:Ls], rhoF[:, g:g + 1].to_broadcast([128, Ls]), f1[:, 0:Ls],
                                                                                   winit[:, g:g + 1], ALU.mult, ALU.add), rs([rhoF, f1, winit]), rs([wb]))
                    else:
                        STT("dve", f1[:, 0:16:4], x0s[:, :, g], rhoF[:, g:g + 1], f1[:, 0:16:4], ALU.mult, ALU.add, [x0s, rhoF, f1], [f1])
                        P.op("dve", lambda e, wb=wb, f1=f1, g=g: e.tensor_tensor_scan(wb[:, 0:16], rhoS[:, g, :], f1[:, 0:16], 0.0, ALU.mult, ALU.add),
                             rs([rhoS, f1]), rs([wb]))
                    u1 = ssmu_r.next(); u2 = ssmu_r.next()
                    TT("dve", u1[:, 0:Ls], wb[:, 0:Ls], ct[:, g, 0:Ls], ALU.mult, [wb, ct], [u1])
                    TT("dve", u2[:, 0:Ls], wb[:, 0:Ls], stb[:, g, 0:Ls], ALU.mult, [wb, stb], [u2])
                    MM(psY[:, gc * 256:gc * 256 + Ls], C1g[:, g, :], u1[:, 0:Ls], [C1g, u1], [psY], start=(gl == 0), stop=False)
                    MM(psY[:, gc * 256:gc * 256 + Ls], C2g[:, g, :], u2[:, 0:Ls], [C2g, u2], [psY], start=False, stop=(gl == 7))
                    if mode == "p":
                        MM(psW[:, g:g + 1], RotP[:, g, :], wb[:, Ls - 1:Ls], [RotP, wb], [psW])
                    else:
                        MM(psW[:, 4 * g:4 * g + 4], RotS[:, g, :], wb[:, 3:16:4], [RotS, wb], [psW])
                if mode == "p":
                    CP("dve", winit[:], psW[:, 0:16], [psW], [winit])
                else:
                    so = f5_r.next()
                    CP("dve", so[:, 0:64].rearrange("p (b g) -> p b g", g=16), psW[:, 0:64].rearrange("p (g b) -> p b g", b=4), [psW], [so])
                    DMA("sp", ssms[l], so[:, 0:64].rearrange("p (b g) -> p b g", g=16), [so], [])
                for k2 in range(2):
                    STT("dve", yt[:, k2, 0:Ls], uT[:, k2, tk], dskb[:, k2:k2 + 1], psY[:, k2 * 256:k2 * 256 + Ls], ALU.mult, ALU.add, [uT, dskb, psY], [yt])
                ACTV(zT[:, :, 0:Ls], yt[:, :, 0:Ls], AF.Gelu, [yt], [zT])
                psG = pf.next()
                for oc in range(2):
                    for k2 in range(2):
                        MM(psG[:, oc * 256:oc * 256 + Ls], w_glu[:, k2, oc * 128:(oc + 1) * 128], zT[:, k2, 0:Ls], [w_glu, zT], [psG], start=(k2 == 0), stop=(k2 == 1))
                for oc in range(2):
                    ACTV(sgs[:, oc, 0:Ls], psG[:, oc * 256:oc * 256 + Ls], AF.Sigmoid, [psG, bglub], [sgs], bias=bglub[:, oc:oc + 1])
                TT("dve", sgs[:, :, 0:Ls], sgs[:, :, 0:Ls], zT[:, :, 0:Ls], ALU.mult, [sgs, zT], [sgs])
                TT("dve", mixT[:, 6:8, tk], sgs[:, :, 0:Ls], scg[:, :, tk], ALU.mult, [sgs, scg], [mixT])

            chk(38)
            if mode == "p":
                for i in range(nt):
                    qi = blk * NTB + i
                    k_lo = max(0, qi - 16)
                    kis = list(range(k_lo, qi + 1))
                    accs = pacc
                    for h in range(8):
                        p, hb_ = h // 2, 64 * (h % 2)
                        acc = accs[h // 4]
                        hh = h % 4
                        for g0 in range(0, len(kis), 4):
                            grp = kis[g0:g0 + 4]
                            ps = pf.next()
                            for idx, ki in enumerate(grp):
                                sl = ki % NKT
                                MM(ps[:, idx * 128:(idx + 1) * 128], kT[hb_:hb_ + 64, p, sl * 128:(sl + 1) * 128], qT[hb_:hb_ + 64, p, i * 128:(i + 1) * 128], [kT, qT], [ps])
                            w = len(grp) * 128
                            pt = pt_r.next(); pm = pm_r.next()
                            ACTV(pt[:, 0:w], ps[:, 0:w], AF.Exp, [ps], [pt])
                            m0 = grp[0] - (qi - 16)
                            TT("pool", pm[:, 0:w], pt[:, 0:w], C["c_swam"][:, m0:m0 + len(grp), :].rearrange("p a b -> p (a b)"), ALU.mult, [pt, C["c_swam"]], [pm])
                            for idx, ki in enumerate(grp):
                                sl = ki % NKT
                                MM(acc[:, hh * 65:(hh + 1) * 65], pm[:, idx * 128:(idx + 1) * 128], Vr[:, sl, h, :], [pm, Vr], [acc],
                                   start=(ki == kis[0]), stop=(ki == kis[-1]))
                    for a in range(2):
                        acc = accs[a]
                        st = st_r.next()
                        av_ = acc[:, 0:260].rearrange("p (h d) -> p h d", d=65)
                        P.op("dve", lambda e, st=st, av_=av_: e.reciprocal(st[:, 0:4], av_[:, :, 64]), rs([acc]), rs([st]))
                        ob = f5_r.next()
                        TT("dve", ob[:, 0:256].rearrange("p (h d) -> p h d", d=64), av_[:, :, 0:64], st[:, 0:4].unsqueeze(2).to_broadcast([128, 4, 64]), ALU.mult, [acc, st], [ob])
                        TT("dve", otm[:, i, 256 + a * 256:512 + a * 256], ob[:, 0:256], sbg[:, i, a * 256:(a + 1) * 256], ALU.mult, [ob, sbg], [otm])
            else:
                accs = pacc
                ckTs = {}
                for b in range(4):
                    for t in range(16):
                        ctk = ctile_k.next(); ctv = ctile_v.next()
                        DMA("sp", ctk[:], ck[l, b, t * 128:(t + 1) * 128, :], [], [ctk])
                        DMA("sp", ctv[:], cv[l, b, t * 128:(t + 1) * 128, :], [], [ctv])
                        if t == 0:
                            DMA("sp", ks[l, b, 0:124, :], ctk[4:128, :], [ctk], [])
                            DMA("sp", vs[l, b, 0:124, :], ctv[4:128, :], [ctv], [])
                        else:
                            DMA("sp", ks[l, b, t * 128 - 4:t * 128 + 124, :], ctk[:], [ctk], [])
                            DMA("sp", vs[l, b, t * 128 - 4:t * 128 + 124, :], ctv[:], [ctv], [])
                        ckb = ckb_r.next(); cvb = cvb_r.next()
                        CP("act", ckb[:], ctk[:], [ctk], [ckb])
                        CP("pool", cvb[:, :, 0:64], ctv[:].rearrange("p (h d) -> p h d", d=64), [ctv], [cvb])
                        pb = pbr.next()
                        for p in range(4):
                            TRN(pb[:, p * 128:(p + 1) * 128], ckb[:, p * 128:(p + 1) * 128], ident[:], [ckb, ident], [pb])
                        ckT = ckT_r.next()
                        CP("dve", ckT[:], pb[:, 0:512].rearrange("p (a b) -> p a b", b=128), [pb], [ckT])
                        pse = pf.next(); pso_ = pf.next()
                        for h in range(8):
                            p, hb_ = h // 2, 64 * (h % 2)
                            pst = pse if h % 2 == 0 else pso_
                            MM(pst[:, p * 16:(p + 1) * 16], ckT[hb_:hb_ + 64, p, :], qT[hb_:hb_ + 64, p, 0:16], [ckT, qT], [pst])
                        pt = pt_r.next(); pm = pm_r.next()
                        ACTV(pt[:, 0:64], pse[:, 0:64], AF.Exp, [pse], [pt])
                        ACTV(pt[:, 64:128], pso_[:, 0:64], AF.Exp, [pso_], [pt])
                        TT("pool", pm[:, 0:128].rearrange("p (h q) -> p h q", q=16), pt[:, 0:128].rearrange("p (h q) -> p h q", q=16),
                           C["c_swam_s"][:, b, t, :].unsqueeze(1).to_broadcast([128, 8, 16]), ALU.mult, [pt, C["c_swam_s"]], [pm])
                        for h in range(8):
                            cbk = (h % 2) * 4 + h // 2
                            MM(accs[h // 4][0:16, (h % 4) * 65:(h % 4 + 1) * 65], pm[:, cbk * 16:(cbk + 1) * 16], cvb[:, h, :], [pm, cvb], [accs[h // 4]],
                               start=(b == 0 and t == 0), stop=False)
                pse = pf.next(); pso_ = pf.next()
                for h in range(8):
                    p, hb_ = h // 2, 64 * (h % 2)
                    pst = pse if h % 2 == 0 else pso_
                    MM(pst[0:16, p * 16:(p + 1) * 16], kTs[hb_:hb_ + 64, p, :], qT[hb_:hb_ + 64, p, 0:16], [kTs, qT], [pst])
                pt = pt_r.next(); pm = pm_r.next()
                ACTV(pt[0:16, 0:64], pse[0:16, 0:64], AF.Exp, [pse], [pt])
                ACTV(pt[0:16, 64:128], pso_[0:16, 0:64], AF.Exp, [pso_], [pt])
                TT("pool", pm[0:16, 0:128].rearrange("p (h q) -> p h q", q=16), pt[0:16, 0:128].rearrange("p (h q) -> p h q", q=16),
                   C["c_swam_n"][:].unsqueeze(1).to_broadcast([16, 8, 16]), ALU.mult, [pt, C["c_swam_n"]], [pm])
                for h in range(8):
                    cbk = (h % 2) * 4 + h // 2
                    MM(accs[h // 4][0:16, (h % 4) * 65:(h % 4 + 1) * 65], pm[0:16, cbk * 16:(cbk + 1) * 16], Vsn[:, h, :], [pm, Vsn], [accs[h // 4]], start=False, stop=True)
                for a in range(2):
                    acc = accs[a]
                    st = st_r.next()
                    av_ = acc[0:16, 0:260].rearrange("p (h d) -> p h d", d=65)
                    P.op("dve", lambda e, st=st, av_=av_: e.reciprocal(st[0:16, 0:4], av_[:, :, 64]), rs([acc]), rs([st]))
                    ob = f5_r.next()
                    TT("dve", ob[0:16, 0:256].rearrange("p (h d) -> p h d", d=64), av_[:, :, 0:64], st[0:16, 0:4].unsqueeze(2).to_broadcast([16, 4, 64]), ALU.mult, [acc, st], [ob])
                    TT("dve", otm[0:16, 0, 256 + a * 256:512 + a * 256], ob[0:16, 0:256], sbg[0:16, 0, a * 256:(a + 1) * 256], ALU.mult, [ob, sbg], [otm])

            chk(39)
            for i in range(nt):
                pb = pbr.next()
                for c6 in range(6):
                    TRN(pb[:, c6 * 128:c6 * 128 + n], otm[0:n, i, c6 * 128:(c6 + 1) * 128], ident[0:n, 0:n], [otm, ident], [pb])
                CP("act", mixT[:, 0:6, i * n:(i + 1) * n], pb[:, 0:768].rearrange("p (k t) -> p k t", t=128)[:, :, 0:n], [pb], [mixT])
            pss = [[pf.next(), pf.next()] for _ in range(nt)]
            for half in range(2):
                wo_ = wst_r.next()
                DMA("sp", wo_[:], wosc[l % 2][:, half * 4096:(half + 1) * 4096].rearrange("p (k m) -> p k m", m=512), [r_wosc[l % 2][half]], [wo_])
                for i in range(nt):
                    ps = pss[i][half]
                    for k in range(8):
                        MM(ps[0:n, :], mixT[:, k, i * n:(i + 1) * n], wo_[:, k, :], [mixT, wo_], [ps], start=(k == 0), stop=(k == 7))
            for i in range(nt):
                psa_, psb_ = pss[i]
                yo = yo_r.next(); jk = junk_r.next(); st = st_r.next()
                CP("act", yo[0:n, 0:512], psa_[0:n, :], [psa_], [yo])
                CP("act", yo[0:n, 512:1024], psb_[0:n, :], [psb_], [yo])
                ACTV(jk[0:n, :], yo[0:n, :], AF.Square, [yo], [jk, st], accum=st[0:n, 0:1])
                ACTV(st[0:n, 1:2], st[0:n, 0:1], AF.Sqrt, [st, eps_c], [st], bias=eps_c[0:n, 0:1], scale=1.0 / D)
                P.op("dve", lambda e, st=st: e.reciprocal(st[0:n, 2:3], st[0:n, 1:2]), rs([st]), rs([st]))
                STT("dve", yo[0:n, :], yo[0:n, :], st[0:n, 2:3], mod[2][0:n, :], ALU.mult, ALU.mult, [yo, st, mod[2]], [yo])
                xt = xt_r.next()
                src, rsrc = xsrc(i)
                DMA("sp", xt[0:n, :], src, rsrc, [xt])
                TT("dve", yo[0:n, :], yo[0:n, :], xt[0:n, :], ALU.add, [yo, xt], [yo])
                dst, rdst = xdst(i)
                DMA("sp", dst, yo[0:n, :], [yo], rdst, owner=yo)

        P.maxops = int(os.environ.get("MK_MAXOPS", "0"))
        try:
          for l in range(DEPTH):
            layer_setup(l)
            chk(1)
            layer_setup2(l)
            chk(2)
            MS("dve", Sg[:], 0.0, [Sg]); MS("pool", Sgb[:], 0.0, [Sgb]); MS("dve", winit[:], 0.0, [winit])
            for blk in range(NB):
                do_block(l, "p", blk)
                chk(3)
            chk(4)
            go = f5_r.next()
            CP("dve", go[:, 0:64], Sg[:], [Sg], [go])
            DMA("sp", glap[l], go[:, 0:64], [go], [])
            wo = f5_r.next()
            CP("dve", wo[:, 0:16], winit[:], [winit], [wo])
            DMA("sp", ssmp[l], wo[:, 0:16], [wo], [])
            DMA("sp", Sgs[:], sgla[l].rearrange("b p e -> p b e"), [], [Sgs])
            CP("dve", Sgsb[:], Sgs[:], [Sgs], [Sgsb])
            DMA("sp", x0s[:], sssm[l], [], [x0s])
            build_rot(4.0)
            do_block(l, "s", 0)
            DMA("sp", glas[l].rearrange("b p e -> p b e"), Sgs[:], [Sgs], [])
        except (Stop, StopBuild):
            pass
        print("MK nops", P.nops, "sbuf_free", nc.sbuf_bytes_remaining, flush=True)
        P.emit()
    return nc


def _host_inputs(inp, core, T, DEPTH):
    f = np.float32
    pb = core // 2
    sbs = slice(4 * core, 4 * core + 4)
    m = {}
    m["xp"] = np.ascontiguousarray(inp["x_prompt"][pb, :T]).astype(f)
    m["xs"] = np.ascontiguousarray(inp["x_sample"][sbs].reshape(16, D)).astype(f)
    c5 = np.concatenate([inp["c_prompt"][pb:pb + 1], inp["c_sample"][sbs]], 0)
    m["cT"] = np.ascontiguousarray(c5.T.reshape(8, 128, 5).transpose(1, 0, 2)).astype(f)
    L = DEPTH
    m["w_in_r"] = np.ascontiguousarray(inp["w_in"][:L].reshape(L, 8, 128, NCOL).transpose(0, 2, 1, 3))
    m["w_out_r"] = np.ascontiguousarray(inp["w_out"][:L].reshape(L, 8, 128, D).transpose(0, 2, 1, 3))
    m["w_ada_r"] = np.ascontiguousarray(inp["w_ada"][:L].reshape(L, 8, 128, 3 * D).transpose(0, 2, 1, 3))
    bc = lambda a: np.ascontiguousarray(np.broadcast_to(a[:L, None, :], (L, 128, a.shape[-1]))).astype(f)
    m["b_ada_b"] = bc(inp["b_ada"]); m["g_pre_b"] = bc(inp["g_pre"]); m["g_post_b"] = bc(inp["g_post"]); m["g_gla_b"] = bc(inp["g_gla"])
    m["w_lr"] = np.ascontiguousarray(inp["w_gla_lr"][:L]); m["b_lr"] = np.ascontiguousarray(inp["b_gla_lr"][:L, :, None])
    lam_re, lam_im, ldt = inp["ssm_lambda_re"][:L], inp["ssm_lambda_im"][:L], inp["ssm_log_dt"][:L]
    tF = lambda a: np.ascontiguousarray(np.concatenate([a.transpose(0, 2, 1)] * 2, 1)).astype(f)
    m["lamF_re"] = tF(lam_re); m["lamF_im"] = tF(lam_im)
    m["logdtF"] = np.ascontiguousarray(np.broadcast_to(ldt[:, None, :], (L, 128, 16))).astype(f)

    def tB(a):
        a4 = a.reshape(L, 2, 8, 64)
        o = np.broadcast_to(a4.transpose(0, 2, 1, 3)[:, :, None, :, :], (L, 8, 16, 2, 64))
        return np.ascontiguousarray(o.reshape(L, 128, 128)).astype(f)
    m["lamB_re"] = tB(lam_re); m["lamB_im"] = tB(lam_im)
    m["logdtB"] = tB(np.broadcast_to(ldt[:, :, None], (L, 16, 64)))

    def tBt(bb):
        b5 = bb.reshape(L, 2, 8, 64, 16)
        return np.ascontiguousarray(b5.transpose(0, 2, 4, 1, 3).reshape(L, 128, 128)).astype(f)
    m["Bt_re"] = tBt(inp["ssm_b_re"][:L]); m["Bt_im"] = tBt(inp["ssm_b_im"][:L])
    tC = lambda cc: np.ascontiguousarray(cc.transpose(0, 3, 1, 2).reshape(L, 64, 256)).astype(f)
    m["Ct_re"] = tC(inp["ssm_c_re"][:L]); m["Ct_im"] = tC(inp["ssm_c_im"][:L])
    kp_ = lambda a: np.ascontiguousarray(a[:L].reshape(L, 2, 128).transpose(0, 2, 1)).astype(f)
    m["dsk"] = kp_(inp["ssm_d"]); m["bglu"] = kp_(inp["b_glu"])
    m["w_glu_r"] = np.ascontiguousarray(inp["w_glu"][:L].reshape(L, 2, 128, 256).transpose(0, 2, 1, 3))
    m["sgla"] = np.ascontiguousarray(inp["state_gla"][:L, sbs].reshape(L, 4, 128, 64))
    sre = inp["state_ssm_re"][:L, sbs]; sim = inp["state_ssm_im"][:L, sbs]
    s2 = np.concatenate([sre.transpose(0, 3, 1, 2), sim.transpose(0, 3, 1, 2)], 1)
    m["sssm"] = np.ascontiguousarray(s2).astype(f)
    m["ck"] = np.ascontiguousarray(inp["cache_swa_k"][:L, sbs].reshape(L, 4, LC, 512))
    m["cv"] = np.ascontiguousarray(inp["cache_swa_v"][:L, sbs].reshape(L, 4, LC, 512))
    return m


_NC_CACHE = {}


def run(inputs, T=4096, DEPTH=4, ncores=8):
    inp = {k: np.asarray(v) for k, v in inputs.items()}
    key = (T, DEPTH)
    if key not in _NC_CACHE:
        _NC_CACHE[key] = build(T, DEPTH)
    nc = _NC_CACHE[key]
    cst = host_consts(T)
    in_maps = []
    for c in range(ncores):
        m = _host_inputs(inp, c, T, DEPTH)
        m.update(cst)
        in_maps.append(m)
    res = run_bass_kernel_spmd(nc, in_maps, core_ids=list(range(ncores)))
    R = res.results
    KEEP = min(LC, T)
    nb = ncores // 2
    f = np.float32
    y_p = np.stack([R[2 * b]["yp"] for b in range(nb)]).astype(f)
    y_s = np.concatenate([R[c]["ys"].reshape(4, 4, D) for c in range(ncores)]).astype(f)
    gla_p = np.stack([R[2 * b]["glap"].reshape(DEPTH, 4, 32, 64) for b in range(nb)], 1).astype(f)
    gla_s = np.concatenate([R[c]["glas"].reshape(DEPTH, 4, 4, 32, 64) for c in range(ncores)], 1).astype(f)
    k_p = np.stack([R[2 * b]["kp"].reshape(DEPTH, KEEP, 8, 64) for b in range(nb)], 1).astype(f)
    v_p = np.stack([R[2 * b]["vp"].reshape(DEPTH, KEEP, 8, 64) for b in range(nb)], 1).astype(f)
    k_s = np.concatenate([R[c]["ks"].reshape(DEPTH, 4, LC, 8, 64) for c in range(ncores)], 1).astype(f)
    v_s = np.concatenate([R[c]["vs"].reshape(DEPTH, 4, LC, 8, 64) for c in range(ncores)], 1).astype(f)

    def unp(a):
        return a[:, 0:64, :].transpose(0, 2, 1), a[:, 64:128, :].transpose(0, 2, 1)
    rp = [unp(R[2 * b]["ssmp"]) for b in range(nb)]
    re_p = np.stack([x[0] for x in rp], 1).astype(f); im_p = np.stack([x[1] for x in rp], 1).astype(f)

    def unps(a):
        return a[:, 0:64].transpose(0, 2, 3, 1), a[:, 64:128].transpose(0, 2, 3, 1)
    rsx = [unps(R[c]["ssms"]) for c in range(ncores)]
    re_s = np.concatenate([x[0] for x in rsx], 1).astype(f); im_s = np.concatenate([x[1] for x in rsx], 1).astype(f)
    return (y_p, y_s, gla_p, gla_s, k_p, v_p, k_s, v_s, re_p, im_p, re_s, im_s)


def kernel(**inputs):
    return run(inputs, 4096, 4, 8)
```

```python
import math
import numpy as np
import ml_dtypes
from contextlib import ExitStack
import concourse.bass as bass
import concourse.mybir as mybir
from concourse.bass_utils import run_bass_kernel_spmd

F32 = mybir.dt.float32
BF16 = mybir.dt.bfloat16
AF = mybir.ActivationFunctionType
ALU = mybir.AluOpType
AX = mybir.AxisListType
PI = math.pi

D = 1024
NCOL = 3344
LC = 2048
SSM_L = 128
NKT = 18
BLK = 256
NTB = BLK // 128
EPS = 1e-6


class Res:
    __slots__ = ("name", "last_w", "reads", "sem", "cnt", "excl")

    def __init__(self, name, excl=False):
        self.excl = excl
        self.name = name
        self.last_w = None
        self.reads = []
        self.sem = None
        self.cnt = 0


class StopBuild(Exception):
    pass


class Prog:
    nops = 0
    maxops = 0
    log = None

    def _tick(self, eng, tag):
        self.nops += 1
        if self.log is not None:
            self.log.append((self.nops, eng, tag))
        if self.maxops and self.nops > self.maxops:
            raise StopBuild()

    ENGS = ("pe", "act", "dve", "pool", "sp")
    COMPUTE = ("pe", "act", "dve", "pool")

    def __init__(self, nc):
        self.nc = nc
        self.streams = {e: [] for e in self.ENGS}
        self.needed = {e: set() for e in self.COMPUTE}
        self.known_c = {e: {x: -1 for x in self.COMPUTE} for e in self.ENGS}
        self.known_d = {e: {} for e in self.ENGS}
        self.dma_sems = []

    def _collect(self, eng, reads, writes, is_dma):
        deps = []
        for r in reads:
            if r.last_w is not None:
                deps.append(r.last_w)
            if r.excl:
                deps.extend(r.reads)
        for w in writes:
            if w.last_w is not None:
                deps.append(w.last_w)
            deps.extend(w.reads)
        waits = []
        for d in deps:
            if d[0] == "c":
                _, x, idx = d
                if x == eng and not is_dma and eng == "pe":
                    continue
                if self.known_c[eng][x] >= idx:
                    continue
                self.known_c[eng][x] = idx
                self.needed[x].add(idx)
                waits.append(d)
            else:
                _, owner, cnt = d
                k = self.known_d[eng].get(id(owner), 0)
                if k >= cnt:
                    continue
                self.known_d[eng][id(owner)] = cnt
                waits.append(d)
        return waits

    def op(self, eng, fn, reads=(), writes=()):
        self._tick(eng, "op")
        waits = self._collect(eng, reads, writes, False)
        idx = len(self.streams[eng])
        ev = ("c", eng, idx)
        self.streams[eng].append({"fn": fn, "waits": waits, "ev": ev})
        for r in reads:
            r.reads.append(ev)
        for w in writes:
            w.last_w = ev
            w.reads = []
        return ev

    def dma(self, eng, fn, reads=(), writes=(), owner=None):
        self._tick(eng, "dma")
        waits = self._collect(eng, reads, writes, True)
        if owner is None:
            owner = writes[0] if writes else reads[0]
        if owner.sem is None:
            owner.sem = True
            self.dma_sems.append(owner)
        owner.cnt += 16
        ev = ("d", owner, owner.cnt)
        self.streams[eng].append({"fn": fn, "waits": waits, "ev": ev})
        for r in reads:
            r.reads.append(ev)
        for w in writes:
            w.last_w = ev
            w.reads = []
        return ev

    def emit(self):
        nc = self.nc
        es = ExitStack()
        with es:
            EPOCH = 12000
            for i, o in enumerate(self.dma_sems):
                o.sem = es.enter_context(nc.semaphore(f"d_{i}"))
            rank = {}
            csem = {}
            for e in self.COMPUTE:
                rank[e] = {idx: (n // EPOCH, n % EPOCH + 1) for n, idx in enumerate(sorted(self.needed[e]))}
                nep = max(1, (len(self.needed[e]) + EPOCH - 1) // EPOCH)
                csem[e] = [es.enter_context(nc.semaphore(f"s_{e}{k}")) for k in range(nep)]
            block = es.enter_context(nc.Block())
            me = self

            def replay(e, name, final=False):
                for o in me.streams[name]:
                    for d in o["waits"]:
                        if d[0] == "c":
                            ep, rv = rank[d[1]][d[2]]
                            e.wait_ge(csem[d[1]][ep], rv)
                        else:
                            e.wait_ge(d[1].sem, d[2])
                    ins = o["fn"](e)
                    ev = o["ev"]
                    if ev[0] == "c":
                        if ev[2] in rank[ev[1]]:
                            ins.then_inc(csem[ev[1]][rank[ev[1]][ev[2]][0]], 1)
                    else:
                        ins.then_inc(ev[1].sem, 16)
                if final:
                    for o in me.dma_sems:
                        e.wait_ge(o.sem, o.cnt)

            @block.tensor
            def _(e):
                replay(e, "pe")

            @block.scalar
            def _(e):
                replay(e, "act")

            @block.vector
            def _(e):
                replay(e, "dve")

            @block.gpsimd
            def _(e):
                replay(e, "pool")

            @block.sync
            def _(e):
                replay(e, "sp", final=True)


class Buf:
    def __init__(self, t, r):
        self.t = t
        self.r = r

    def __getitem__(self, k):
        return self.t[k]


class Ring:
    def __init__(self, bufs):
        self.bufs = bufs
        self.i = 0

    def next(self):
        b = self.bufs[self.i % len(self.bufs)]
        self.i += 1
        return b


def mult_mask(o):
    o = np.asarray(o)
    m = ((o >= 0) & (o <= 128)).astype(np.float32)
    m += ((o >= 0) & (o <= 512) & (o % 4 == 0))
    m += ((o >= 0) & (o <= 2048) & (o % 16 == 0))
    return m


def host_consts(T):
    c = {}
    bf = ml_dtypes.bfloat16
    c["c_ident_bf"] = np.eye(128, dtype=np.float32).astype(bf)
    c["c_ident_f"] = np.eye(128, dtype=np.float32)
    ps = np.zeros((128, 128), np.float32)
    for k in range(128):
        ps[k, (k + 64) % 128] = 1.0
    c["c_pswap"] = ps
    sg = np.ones((128, 1), np.float32)
    sg[64:] = -1.0
    c["c_sgn"] = sg
    half = 8
    inv = (500000.0 ** (-np.arange(half, dtype=np.float32) * (2.0 / 16))).astype(np.float32)

    def rope_tab(pos):
        ang = pos.astype(np.float32)[None, :] * inv[:, None]
        cosd = np.ones((64, len(pos)), np.float32)
        sind = np.zeros((64, len(pos)), np.float32)
        cosd[0:8] = np.cos(ang)
        cosd[8:16] = np.cos(ang)
        sind[0:8] = np.sin(ang)
        sind[8:16] = np.sin(ang)
        return np.concatenate([cosd, cosd], 0), np.concatenate([sind, sind], 0)

    c["c_cos_p"], c["c_sin_p"] = [a.astype(bf) for a in rope_tab(np.arange(T))]
    c["c_cos_s"], c["c_sin_s"] = rope_tab(8192 + (np.arange(16) % 4))
    rot = np.zeros((128, 128), np.float32)
    for hb in (0, 64):
        for j in range(8):
            rot[hb + j + 8, hb + j] = -1.0
            rot[hb + j, hb + j + 8] = 1.0
    c["c_rot"] = rot.astype(bf)
    k = np.arange(128)[:, None]
    q = np.arange(128)[None, :]
    sw = np.zeros((128, 17, 128), np.float32)
    for m in range(17):
        sw[:, m, :] = mult_mask((16 - m) * 128 + q - k)
    c["c_swam"] = sw.astype(bf)
    sws = np.zeros((128, 4, 16, 16), np.float32)
    for b in range(4):
        for t in range(16):
            for i in range(4):
                sws[:, b, t, 4 * b + i] = mult_mask(2048 + i - 128 * t - np.arange(128))
    c["c_swam_s"] = sws.astype(bf)
    swn = np.zeros((16, 16), np.float32)
    for kk in range(16):
        for qq in range(16):
            if kk // 4 == qq // 4:
                swn[kk, qq] = mult_mask(qq - kk)
    c["c_swam_n"] = swn.astype(bf)
    j = np.arange(128)[:, None]
    i = np.arange(128)[None, :]
    cz = (j <= i).astype(np.float32)
    c["c_causal_p"] = np.tile(cz, (1, 4)).astype(bf)
    czs = ((j[:16, :] <= i[:, :16]) & (j[:16, :] // 4 == i[:, :16] // 4)).astype(np.float32)
    c["c_causal_s"] = np.tile(czs, (1, 4)).astype(np.float32)
    rp = np.ones((128, BLK), np.float32)
    rp[:, ::128] = 0.0
    c["c_reset_p"] = rp
    rsx = np.ones((128, 16), np.float32)
    rsx[:, ::4] = 0.0
    c["c_reset_s"] = rsx
    hm = np.zeros((128, 4), np.float32)
    for h in range(4):
        hm[32 * h:32 * h + 32, h] = 1.0
    c["c_hm"] = hm
    bm = np.zeros((16, 4), np.float32)
    for b in range(4):
        bm[4 * b:4 * b + 4, b] = 1.0
    c["c_bm"] = bm
    bcol = np.zeros((128, 4, 16), np.float32)
    for b in range(4):
        bcol[:, b, 4 * b:4 * b + 4] = 1.0
    c["c_bcol"] = bcol
    gm = np.zeros((128, 8), np.float32)
    for g in range(8):
        gm[16 * g:16 * g + 16, g] = 1.0
    c["c_gm"] = gm
    c["c_iota"] = np.tile(np.arange(1, SSM_L + 1, dtype=np.float32)[None, :], (128, 1))
    return c


def build(T, DEPTH):
    NB = T // BLK
    NT = T // 128
    KEEP = min(LC, T)
    nc = bass.Bass("TRN2", target_bir_lowering=False)
    P = Prog(nc)
    es = ExitStack()
    cst = host_consts(T)

    def DI(name, shape, dt=F32):
        return nc.dram_tensor(name, list(shape), dt, kind="ExternalInput").ap()

    def DO(name, shape):
        return nc.dram_tensor(name, list(shape), F32, kind="ExternalOutput").ap()

    xp = DI("xp", [T, D]); xs = DI("xs", [16, D]); cT = DI("cT", [128, 8, 5])
    w_in_r = DI("w_in_r", [DEPTH, 128, 8, NCOL]); w_out_r = DI("w_out_r", [DEPTH, 128, 8, D])
    w_ada_r = DI("w_ada_r", [DEPTH, 128, 8, 3 * D]); b_ada_b = DI("b_ada_b", [DEPTH, 128, 3 * D])
    g_pre_b = DI("g_pre_b", [DEPTH, 128, D]); g_post_b = DI("g_post_b", [DEPTH, 128, D])
    g_gla_b = DI("g_gla_b", [DEPTH, 128, 256]); w_lr = DI("w_lr", [DEPTH, 16, 128]); b_lr = DI("b_lr", [DEPTH, 128, 1])
    lamF_re = DI("lamF_re", [DEPTH, 128, 16]); lamF_im = DI("lamF_im", [DEPTH, 128, 16]); logdtF = DI("logdtF", [DEPTH, 128, 16])
    lamB_re = DI("lamB_re", [DEPTH, 128, 128]); lamB_im = DI("lamB_im", [DEPTH, 128, 128]); logdtB = DI("logdtB", [DEPTH, 128, 128])
    Bt_re = DI("Bt_re", [DEPTH, 128, 128]); Bt_im = DI("Bt_im", [DEPTH, 128, 128])
    Ct_re = DI("Ct_re", [DEPTH, 64, 256]); Ct_im = DI("Ct_im", [DEPTH, 64, 256])
    dsk = DI("dsk", [DEPTH, 128, 2]); bglu = DI("bglu", [DEPTH, 128, 2]); w_glu_r = DI("w_glu_r", [DEPTH, 128, 2, 256])
    sgla = DI("sgla", [DEPTH, 4, 128, 64]); sssm = DI("sssm", [DEPTH, 128, 4, 16])
    ck = DI("ck", [DEPTH, 4, LC, 512]); cv = DI("cv", [DEPTH, 4, LC, 512])
    cin = {}
    for name, arr in cst.items():
        cin[name] = DI(name, arr.shape, BF16 if arr.dtype == ml_dtypes.bfloat16 else F32)

    yp = DO("yp", [T, D]); ys = DO("ys", [16, D])
    glap = DO("glap", [DEPTH, 128, 64]); glas = DO("glas", [DEPTH, 4, 128, 64])
    kp = DO("kp", [DEPTH, KEEP, 512]); vp = DO("vp", [DEPTH, KEEP, 512])
    ks = DO("ks", [DEPTH, 4, LC, 512]); vs = DO("vs", [DEPTH, 4, LC, 512])
    ssmp = DO("ssmp", [DEPTH, 128, 16]); ssms = DO("ssms", [DEPTH, 128, 4, 16])
    xsc = [nc.dram_tensor(f"xsc{i}", [T, D], F32, kind="Internal").ap() for i in range(2)]
    xss = [nc.dram_tensor(f"xss{i}", [16, D], F32, kind="Internal").ap() for i in range(2)]
    r_xsc = [[Res(f"xsc{i}_{b}") for b in range(NB)] for i in range(3)]
    r_xss = [Res(f"xss{i}") for i in range(3)]
    WBLOCKS = [(1296 + 128 * p, 128) for p in range(4)] + [(784 + 128 * p, 128) for p in range(4)] + \
              [(1808, 512), (2320, 512), (256, 256), (528, 256)] + [(2832 + 128 * k, 128) for k in range(2)] + \
              [(3088 + 128 * k, 128) for k in range(2)] + [(0, 128), (128, 128), (512, 16)]
    wsc = [nc.dram_tensor(f"wsc{i}", [128, 8 * NCOL], BF16, kind="Internal").ap() for i in range(2)]
    wosc = [nc.dram_tensor(f"wosc{i}", [128, 8 * D], BF16, kind="Internal").ap() for i in range(2)]
    r_wsc = [{c0: Res(f"wsc{i}_{c0}") for (c0, m) in WBLOCKS} for i in range(2)]
    r_wosc = [[Res(f"wosc{i}_{h}") for h in range(2)] for i in range(2)]

    with es:
        def sb(name, shape, dt=F32):
            return Buf(es.enter_context(nc.sbuf_tensor("sb_" + name, list(shape), dt)), Res(name))

        def ring(name, n, shape, dt=F32):
            return Ring([sb(f"{name}{i}", shape, dt) for i in range(n)])

        def rs(bufs):
            return [b.r if isinstance(b, Buf) else b for b in bufs]

        def MM(out, lhsT, rhs, R, W, start=True, stop=True):
            P.op("pe", lambda e: e.matmul(out, lhsT, rhs, start=start, stop=stop), rs(R), rs(W))

        def TRN(out, in_, ident, R, W):
            P.op("pe", lambda e: e.transpose(out, in_, ident), rs(R), rs(W))

        def ACTV(out, in_, func, R, W, bias=None, scale=None, accum=None):
            kw = {}
            if bias is not None:
                kw["bias"] = bias
            if scale is not None:
                kw["scale"] = scale
            if accum is not None:
                kw["accum_out"] = accum
            P.op("act", lambda e: e.activation(out, in_, func, **kw), rs(R), rs(W))

        def TT(eng, out, a, b, op, R, W):
            P.op(eng, lambda e: e.tensor_tensor(out, a, b, op), rs(R), rs(W))

        def TS(eng, out, a, s1, s2, op0, op1, R, W):
            if op1 is None:
                P.op(eng, lambda e: e.tensor_scalar(out, a, s1, None, op0), rs(R), rs(W))
            else:
                P.op(eng, lambda e: e.tensor_scalar(out, a, s1, s2, op0, op1), rs(R), rs(W))

        def STT(eng, out, a, s, b, op0, op1, R, W):
            P.op(eng, lambda e: e.scalar_tensor_tensor(out, a, s, b, op0, op1), rs(R), rs(W))

        def CP(eng, out, in_, R, W):
            if eng == "act":
                P.op(eng, lambda e: e.copy(out, in_), rs(R), rs(W))
            else:
                P.op(eng, lambda e: e.tensor_copy(out, in_), rs(R), rs(W))

        def MS(eng, out, val, W):
            P.op(eng, lambda e: e.memset(out, val), [], rs(W))

        def DMA(q, out, in_, R, W, owner=None):
            P.dma(q, lambda e: e.dma_start(out=out, in_=in_), rs(R), rs(W), owner.r if isinstance(owner, Buf) else owner)

        pf = Ring([Buf(es.enter_context(nc.psum_tensor(f"pf{i}", [128, 512], F32)), Res(f"pf{i}", True)) for i in range(4)])
        pacc = [Buf(es.enter_context(nc.psum_tensor(f"pacc{i}", [128, 512], F32)), Res(f"pacc{i}", True)) for i in range(2)]
        pbr = Ring([Buf(es.enter_context(nc.psum_tensor(f"pb{i}", [128, 1024], BF16)), Res(f"pb{i}", True)) for i in range(2)])

        C = {}
        for name, arr in cst.items():
            if name in ("c_cos_p", "c_sin_p"):
                continue
            C[name] = sb(name, arr.shape, BF16 if arr.dtype == ml_dtypes.bfloat16 else F32)
        for name, arr in cst.items():
            if name in ("c_cos_p", "c_sin_p"):
                continue
            DMA("sp", C[name][:], cin[name], [], [C[name]])
        ident = C["c_ident_bf"]

        w_glu = sb("w_glu", [128, 2, 256], BF16)
        wlr = sb("wlr", [16, 128], BF16)
        nblr = sb("nblr", [128, 1])
        ggla = sb("ggla", [128, 256])
        dskb = sb("dskb", [128, 2]); bglub = sb("bglub", [128, 2])
        kT = sb("kT", [128, 4, NKT * 128], BF16)
        Vr = sb("Vr", [128, NKT, 8, 65], BF16)
        kTs = sb("kTs", [128, 4, 16], BF16)
        Vsn = sb("Vsn", [16, 8, 65], BF16)
        MS("pool", Vr[:], 1.0, [Vr])
        MS("pool", Vsn[:], 1.0, [Vsn])
        modp = [sb(f"modp{i}", [128, D], BF16) for i in range(3)]
        mods = [sb(f"mods{i}", [16, D], BF16) for i in range(3)]
        scb_p = sb("scb_p", [128, 8, 128], BF16)
        scb_s = sb("scb_s", [128, 8, 16], BF16)
        cosT = sb("cosT", [128, 16, SSM_L], BF16); sinT = sb("sinT", [128, 16, SSM_L], BF16)
        cosS = sb("cosS", [128, 16, 16], BF16); sinS = sb("sinS", [128, 16, 16], BF16)
        rhoF = sb("rhoF", [128, 16]); rhoS = sb("rhoS", [128, 16, 16])
        LA = sb("LA", [128, 16, 128], BF16); LB = sb("LB", [128, 16, 128], BF16)
        C1g = sb("C1g", [128, 16, 128], BF16); C2g = sb("C2g", [128, 16, 128], BF16)
        MS("pool", C1g[:], 0.0, [C1g]); MS("pool", C2g[:], 0.0, [C2g])
        RotP = sb("RotP", [128, 16, 128]); RotS = RotP
        Sg = sb("Sg", [128, 64]); Sgb = sb("Sgb", [128, 64], BF16)
        Sgs = sb("Sgs", [128, 4, 64]); Sgsb = sb("Sgsb", [128, 4, 64], BF16)
        winit = sb("winit", [128, 16])
        x0s = sb("x0s", [128, 4, 16])

        xt_r = ring("xt", 1, [128, D])
        st_r = ring("st", 4, [128, 4])
        hb_r = ring("hb", 1, [128, D], BF16)
        junk_r = hb_r
        hT = sb("hT", [128, 8, BLK], BF16)
        qT = sb("qT", [128, 4, BLK], BF16)
        xb_r = ring("xb", 1, [128, BLK], BF16)
        f5_r = ring("f5", 2, [128, 512])
        sbg = sb("sbg", [128, NTB, 512], BF16)
        sag = sb("sag", [128, NTB, 256], BF16)
        vg = sb("vg", [128, NTB, 256], BF16)
        uT = sb("uT", [128, 2, BLK], BF16)
        scg = sb("scg", [128, 2, BLK], BF16)
        mixT = sb("mixT", [128, 8, BLK], BF16)
        otm = sb("otm", [128, NTB, 768], BF16)
        qd = sb("qd", [128, BLK], BF16); kd = sb("kd", [128, BLK], BF16); kdec = sb("kdec", [128, BLK], BF16)
        qdh = [sb(f"qdh{h}", [128, BLK], BF16) for h in range(4)]
        qdhb = sb("qdhb", [128, 16, 16], BF16)
        E1 = sb("E1", [128, BLK]); E2 = sb("E2", [128, BLK]); csb = sb("csb", [128, BLK])
        alrT = sb("alrT", [16, BLK], BF16)
        kdTh = ring("kdTh", 1, [128, 4, 128], BF16)
        for b_ in kdTh.bufs:
            MS("pool", b_[:], 0.0, [b_])
        kdThs = sb("kdThs", [16, 16, 128], BF16)
        MS("pool", kdThs[:], 0.0, [kdThs])
        attT_r = ring("attT", 1, [128, 512], BF16)
        b5_r = attT_r
        osb_r = ring("osb", 1, [128, 256]); osq_r = ring("osq", 1, [128, 256])
        kout_r = ring("kout", 1, [128, 512]); vout_r = kout_r
        yo_r = ring("yo", 1, [128, D])
        wst_r = ring("wst", 2, [128, 8, 512], BF16)
        pt_r = ring("ptr", 2, [128, 512], BF16); pm_r = ring("pmr", 3, [128, 512], BF16)
        ssmf_r = ring("ssmf", 4, [128, SSM_L]); ssmw_r = ring("ssmw", 2, [128, SSM_L])
        ssmu_r = ring("ssmu", 4, [128, SSM_L], BF16)
        yt = sb("yt", [128, 2, SSM_L]); zT = sb("zT", [128, 2, SSM_L], BF16); sgs = sb("sgs", [128, 2, SSM_L], BF16)
        ctile_k = ring("ctk", 1, [128, 512]); ctile_v = ring("ctv", 1, [128, 512])
        ckb_r = ring("ckb", 1, [128, 512], BF16); cvb_r = ring("cvb", 2, [128, 8, 65], BF16)
        for b_ in cvb_r.bufs:
            MS("pool", b_[:], 1.0, [b_])
        ckT_r = ring("ckT", 2, [128, 4, 128], BF16)
        sm = {}
        for nm in ("dtF", "aF", "thF", "t1", "t2", "cL", "sL", "sLs"):
            sm[nm] = sb("sm_" + nm, [128, 16])
        sB = {}
        for nm in ("lre", "lim", "ldt", "dt", "a", "th", "rho", "c", "s", "lbr", "lbi", "nr", "ni", "den", "kr", "ki", "bre", "bim", "t1", "t2", "t3"):
            sB[nm] = sb("sB_" + nm, [128, 128])
        BbA = sb("BbA", [128, 2, 128]); BbB = sb("BbB", [128, 2, 128])
        A1 = sb("A1", [128, 256]); A2 = sb("A2", [128, 256])
        angt = ring("angt", 2, [128, SSM_L])
        rtab_r = ring("rtab", 1, [128, 2, BLK], BF16)


        I32 = mybir.dt.int32
        sc_f = [sb(f"sc_f{i}", [128, 128]) for i in range(3)]
        sc_i = sb("sc_i", [128, 128], I32)

        def sincos(sin_out, cos_out, ang, w, Rin, Wout):
            tA, tB, tC = sc_f
            TS("dve", tA[:, 0:w], ang, 1.0 / (2 * PI), None, ALU.mult, None, Rin, [tA])
            CP("dve", sc_i[:, 0:w], tA[:, 0:w], [tA], [sc_i])
            CP("dve", tB[:, 0:w], sc_i[:, 0:w], [sc_i], [tB])
            TT("dve", tA[:, 0:w], tA[:, 0:w], tB[:, 0:w], ALU.subtract, [tA, tB], [tA])
            ACTV(tB[:, 0:w], tA[:, 0:w], AF.Sin, [tA], [tB], scale=PI)
            ACTV(tC[:, 0:w], tA[:, 0:w], AF.Sin, [tA], [tC], scale=PI / 2)
            TT("dve", tA[:, 0:w], tB[:, 0:w], tB[:, 0:w], ALU.mult, [tB], [tA])
            TS("dve", cos_out, tA[:, 0:w], -2.0, 1.0, ALU.mult, ALU.add, [tA], Wout)
            TT("dve", tC[:, 0:w], tC[:, 0:w], tC[:, 0:w], ALU.mult, [tC], [tC])
            TS("dve", tC[:, 0:w], tC[:, 0:w], -2.0, 1.0, ALU.mult, ALU.add, [tC], [tC])
            TT("dve", tC[:, 0:w], tC[:, 0:w], tB[:, 0:w], ALU.mult, [tC, tB], [tC])
            TS("dve", sin_out, tC[:, 0:w], 2.0, None, ALU.mult, None, [tC], Wout)

        nPI = sb("nPI", [128, 1])
        MS("dve", nPI[:], -PI, [nPI])
        one_c = sb("one_c", [128, 1])
        MS("dve", one_c[:], 1.0, [one_c])
        eps_c = sb("eps_c", [128, 1])
        MS("dve", eps_c[:], EPS, [eps_c])

        def layer_setup(l):
            for (c0, m) in WBLOCKS:
                wt_ = wst_r.next()
                DMA("pool", wt_[:, :, 0:m], w_in_r[l, :, :, c0:c0 + m], [], [wt_])
                DMA("sp", wsc[l % 2][:, 8 * c0:8 * c0 + 8 * m].rearrange("p (k m) -> p k m", m=m), wt_[:, :, 0:m], [wt_], [r_wsc[l % 2][c0]], owner=wt_)
            for half in range(2):
                wt_ = wst_r.next()
                DMA("pool", wt_[:], w_out_r[l, :, :, half * 512:(half + 1) * 512], [], [wt_])
                DMA("sp", wosc[l % 2][:, half * 4096:(half + 1) * 4096].rearrange("p (k m) -> p k m", m=512), wt_[:], [wt_], [r_wosc[l % 2][half]], owner=wt_)
            DMA("pool", w_glu[:], w_glu_r[l], [], [w_glu])
            DMA("pool", wlr[:], w_lr[l], [], [wlr])
            DMA("sp", nblr[:], b_lr[l], [], [nblr])
            TS("dve", nblr[:], nblr[:], -1.0, None, ALU.mult, None, [nblr], [nblr])
            DMA("sp", ggla[:], g_gla_b[l], [], [ggla])
            gpost = xt_r.next()
            DMA("sp", gpost[:], g_post_b[l], [], [gpost])
            DMA("sp", dskb[:], dsk[l], [], [dskb])
            DMA("sp", bglub[:], bglu[l], [], [bglub])
            if l == 0:
                sc = f5_r.next()
                scv = sc[:, 0:40].rearrange("p (k r) -> p k r", r=5)
                DMA("sp", scv, cT[:, :, :], [], [sc])
                ACTV(scv, scv, AF.Silu, [sc], [sc])
                CP("dve", scb_p[:], scv[:, :, 0:1].to_broadcast([128, 8, 128]), [sc], [scb_p])
                for b in range(4):
                    CP("dve", scb_s[:, :, 4 * b:4 * b + 4], scv[:, :, 1 + b:2 + b].to_broadcast([128, 8, 4]), [sc], [scb_s])
            gp2 = yo_r.next()
            DMA("sp", gp2[:], g_pre_b[l], [], [gp2])
            for cb in range(6):
                wst = wst_r.next()
                DMA("pool", wst[:], w_ada_r[l, :, :, cb * 512:(cb + 1) * 512], [], [wst])
                bad = f5_r.next()
                DMA("sp", bad[:], b_ada_b[l, :, cb * 512:(cb + 1) * 512], [], [bad])
                part = cb // 2
                co = (cb % 2) * 512
                for (lhs, npart, mod) in ((scb_p, 128, modp), (scb_s, 16, mods)):
                    ps = pf.next()
                    for k in range(8):
                        MM(ps[0:npart, :], lhs[:, k, 0:npart], wst[:, k, :], [lhs, wst], [ps], start=(k == 0), stop=(k == 7))
                    tmp = f5_r.next()
                    TT("dve", tmp[0:npart, :], ps[0:npart, :], bad[0:npart, :], ALU.add, [ps, bad], [tmp])
                    if part == 0:
                        CP("pool", mod[1][0:npart, co:co + 512], tmp[0:npart, :], [tmp], [mod[1]])
                    elif part == 1:
                        STT("dve", mod[0][0:npart, co:co + 512], tmp[0:npart, :], 1.0, gp2[0:npart, co:co + 512], ALU.add, ALU.mult, [tmp, gp2], [mod[0]])
                    else:
                        TT("dve", mod[2][0:npart, co:co + 512], tmp[0:npart, :], gpost[0:npart, co:co + 512], ALU.mult, [tmp, gpost], [mod[2]])
            lre = sm["t1"]; lim = sm["t2"]
            DMA("sp", lre[:], lamF_re[l], [], [lre]); DMA("sp", lim[:], lamF_im[l], [], [lim])
            DMA("sp", sm["dtF"][:], logdtF[l], [], [sm["dtF"]])
            ACTV(sm["dtF"][:], sm["dtF"][:], AF.Exp, [sm["dtF"]], [sm["dtF"]])
            TT("dve", sm["aF"][:], lre[:], sm["dtF"][:], ALU.mult, [lre, sm["dtF"]], [sm["aF"]])
            TT("dve", sm["thF"][:], lim[:], sm["dtF"][:], ALU.mult, [lim, sm["dtF"]], [sm["thF"]])
            ACTV(rhoF[:], sm["aF"][:], AF.Exp, [sm["aF"]], [rhoF])
            TT("dve", rhoS[:], rhoF[:].unsqueeze(2).to_broadcast([128, 16, 16]),
               C["c_reset_s"][:].unsqueeze(1).to_broadcast([128, 16, 16]), ALU.mult, [rhoF, C["c_reset_s"]], [rhoS])
            for g in range(16):
                a1 = angt.next()
                TS("dve", a1[:], C["c_iota"][:], sm["thF"][:, g:g + 1], None, ALU.mult, None, [C["c_iota"], sm["thF"]], [a1])
                sincos(sinT[:, g, :], cosT[:, g, :], a1[:], SSM_L, [a1], [sinT, cosT])
            for b in range(4):
                CP("pool", cosS[:, :, 4 * b:4 * b + 4], cosT[:, :, 0:4], [cosT], [cosS])
                CP("pool", sinS[:, :, 4 * b:4 * b + 4], sinT[:, :, 0:4], [sinT], [sinS])
            build_rot(float(SSM_L))

        def build_rot(Lr):
            Rot = RotP
            if True:
                TS("dve", sm["t1"][:], sm["thF"][:], Lr, None, ALU.mult, None, [sm["thF"]], [sm["t1"]])
                sincos(sm["sL"][:], sm["cL"][:], sm["t1"][:], 16, [sm["t1"]], [sm["sL"], sm["cL"]])
                TS("dve", sm["sLs"][:], sm["sL"][:], C["c_sgn"][:, 0:1], None, ALU.mult, None, [sm["sL"], C["c_sgn"]], [sm["sLs"]])
                for g in range(16):
                    TS("dve", Rot[:, g, :], C["c_ident_f"][:], sm["cL"][:, g:g + 1], None, ALU.mult, None, [C["c_ident_f"], sm["cL"]], [Rot])
                    STT("dve", Rot[:, g, :], C["c_pswap"][:], sm["sLs"][:, g:g + 1], Rot[:, g, :], ALU.mult, ALU.add, [C["c_pswap"], sm["sLs"], Rot], [Rot])

        def layer_setup2(l):
            q = sB
            DMA("sp", q["lre"][:], lamB_re[l], [], [q["lre"]]); DMA("sp", q["lim"][:], lamB_im[l], [], [q["lim"]])
            DMA("sp", q["ldt"][:], logdtB[l], [], [q["ldt"]])
            DMA("sp", q["bre"][:], Bt_re[l], [], [q["bre"]]); DMA("sp", q["bim"][:], Bt_im[l], [], [q["bim"]])
            ACTV(q["dt"][:], q["ldt"][:], AF.Exp, [q["ldt"]], [q["dt"]])
            TT("dve", q["a"][:], q["lre"][:], q["dt"][:], ALU.mult, [q["lre"], q["dt"]], [q["a"]])
            TT("dve", q["th"][:], q["lim"][:], q["dt"][:], ALU.mult, [q["lim"], q["dt"]], [q["th"]])
            ACTV(q["rho"][:], q["a"][:], AF.Exp, [q["a"]], [q["rho"]])
            sincos(q["s"][:], q["c"][:], q["th"][:], 128, [q["th"]], [q["s"], q["c"]])
            STT("dve", q["lbr"][:], q["rho"][:], 1.0, q["c"][:], ALU.mult, ALU.mult, [q["rho"], q["c"]], [q["lbr"]])
            TS("dve", q["lbr"][:], q["lbr"][:], -1.0, None, ALU.add, None, [q["lbr"]], [q["lbr"]])
            TT("dve", q["lbi"][:], q["rho"][:], q["s"][:], ALU.mult, [q["rho"], q["s"]], [q["lbi"]])
            TT("dve", q["t1"][:], q["lbr"][:], q["lre"][:], ALU.mult, [q["lbr"], q["lre"]], [q["t1"]])
            TT("dve", q["t2"][:], q["lbi"][:], q["lim"][:], ALU.mult, [q["lbi"], q["lim"]], [q["t2"]])
            TT("dve", q["nr"][:], q["t1"][:], q["t2"][:], ALU.add, [q["t1"], q["t2"]], [q["nr"]])
            TT("dve", q["t1"][:], q["lbi"][:], q["lre"][:], ALU.mult, [q["lbi"], q["lre"]], [q["t1"]])
            TT("dve", q["t2"][:], q["lbr"][:], q["lim"][:], ALU.mult, [q["lbr"], q["lim"]], [q["t2"]])
            TT("dve", q["ni"][:], q["t1"][:], q["t2"][:], ALU.subtract, [q["t1"], q["t2"]], [q["ni"]])
            TT("dve", q["t1"][:], q["lre"][:], q["lre"][:], ALU.mult, [q["lre"]], [q["t1"]])
            TT("dve", q["t2"][:], q["lim"][:], q["lim"][:], ALU.mult, [q["lim"]], [q["t2"]])
            TT("dve", q["den"][:], q["t1"][:], q["t2"][:], ALU.add, [q["t1"], q["t2"]], [q["den"]])
            P.op("dve", lambda e: e.reciprocal(q["den"][:], q["den"][:]), rs([q["den"]]), rs([q["den"]]))
            TT("dve", q["kr"][:], q["nr"][:], q["den"][:], ALU.mult, [q["nr"], q["den"]], [q["kr"]])
            TT("dve", q["ki"][:], q["ni"][:], q["den"][:], ALU.mult, [q["ni"], q["den"]], [q["ki"]])
            TT("dve", q["t1"][:], q["kr"][:], q["bre"][:], ALU.mult, [q["kr"], q["bre"]], [q["t1"]])
            TT("dve", q["t2"][:], q["ki"][:], q["bim"][:], ALU.mult, [q["ki"], q["bim"]], [q["t2"]])
            TT("dve", q["t3"][:], q["t1"][:], q["t2"][:], ALU.subtract, [q["t1"], q["t2"]], [q["t3"]])
            TT("dve", q["t1"][:], q["kr"][:], q["bim"][:], ALU.mult, [q["kr"], q["bim"]], [q["t1"]])
            TT("dve", q["t2"][:], q["ki"][:], q["bre"][:], ALU.mult, [q["ki"], q["bre"]], [q["t2"]])
            TT("dve", q["t1"][:], q["t1"][:], q["t2"][:], ALU.add, [q["t1"], q["t2"]], [q["t1"]])
            for gc in range(2):
                CP("dve", BbA[:, gc, 0:64], q["t3"][:, gc * 64:(gc + 1) * 64], [q["t3"]], [BbA])
                CP("dve", BbA[:, gc, 64:128], q["t1"][:, gc * 64:(gc + 1) * 64], [q["t1"]], [BbA])
                CP("dve", BbB[:, gc, 0:64], q["t1"][:, gc * 64:(gc + 1) * 64], [q["t1"]], [BbB])
                TS("dve", BbB[:, gc, 64:128], q["t3"][:, gc * 64:(gc + 1) * 64], -1.0, None, ALU.mult, None, [q["t3"]], [BbB])
            for g in range(16):
                gc, gl = g // 8, g % 8
                TS("dve", LA[:, g, :], BbA[:, gc, :], C["c_gm"][:, gl:gl + 1], None, ALU.mult, None, [BbA, C["c_gm"]], [LA])
                TS("dve", LB[:, g, :], BbB[:, gc, :], C["c_gm"][:, gl:gl + 1], None, ALU.mult, None, [BbB, C["c_gm"]], [LB])
            DMA("sp", A1[0:64, :], Ct_re[l], [], [A1]); DMA("sp", A1[64:128, :], Ct_im[l], [], [A1])
            DMA("sp", A2[0:64, :], Ct_im[l], [], [A2]); DMA("sp", A2[64:128, :], Ct_re[l], [], [A2])
            TS("dve", A1[64:128, :], A1[64:128, :], -1.0, None, ALU.mult, None, [A1], [A1])
            TS("dve", A2[:], A2[:], -1.0, None, ALU.mult, None, [A2], [A2])
            for g in range(16):
                gl = g % 8
                CP("pool", C1g[:, g, 16 * gl:16 * gl + 16], A1[:, 16 * g:16 * g + 16], [A1], [C1g])
                CP("pool", C2g[:, g, 16 * gl:16 * gl + 16], A2[:, 16 * g:16 * g + 16], [A2], [C2g])

        import os
        STAGE = int(os.environ.get("MK_STAGE", "99"))

        class Stop(Exception):
            pass

        def chk(k):
            if STAGE == k:
                raise Stop()

        ROPES = [0]
        SWA_CNT = [0]

        def do_block(l, mode, blk):
            if mode == "p":
                n, nt = 128, NTB
                mod = modp
            else:
                n, nt = 16, 1
                mod = mods
            NTOK = n * nt
            last = (l == DEPTH - 1)

            def chk(k):
                if STAGE >= 100:
                    if mode == "s" and STAGE - 100 == k:
                        raise Stop()
                elif mode == "p" and STAGE == k:
                    raise Stop()

            def xsrc(i):
                if mode == "p":
                    t0 = blk * BLK + i * 128
                    if l == 0:
                        return xp[t0:t0 + 128, :], []
                    return xsc[(l - 1) % 2][t0:t0 + 128, :], [r_xsc[(l - 1) % 2][blk]]
                if l == 0:
                    return xs[:, :], []
                return xss[(l - 1) % 2][:, :], [r_xss[(l - 1) % 2]]

            def xdst(i):
                if mode == "p":
                    t0 = blk * BLK + i * 128
                    if last:
                        return yp[t0:t0 + 128, :], []
                    return xsc[l % 2][t0:t0 + 128, :], [r_xsc[l % 2][blk]]
                if last:
                    return ys[:, :], []
                return xss[l % 2][:, :], [r_xss[l % 2]]

            for i in range(nt):
                xt = xt_r.next()
                src, rsrc = xsrc(i)
                DMA("sp", xt[0:n, :], src, rsrc, [xt])
                jk = junk_r.next(); st = st_r.next()
                ACTV(jk[0:n, :], xt[0:n, :], AF.Square, [xt], [jk, st], accum=st[0:n, 0:1])
                ACTV(st[0:n, 1:2], st[0:n, 0:1], AF.Sqrt, [st, eps_c], [st], bias=eps_c[0:n, 0:1], scale=1.0 / D)
                P.op("dve", lambda e, st=st: e.reciprocal(st[0:n, 2:3], st[0:n, 1:2]), rs([st]), rs([st]))
                hf = yo_r.next()
                STT("dve", hf[0:n, :], xt[0:n, :], st[0:n, 2:3], mod[0][0:n, :], ALU.mult, ALU.mult, [xt, st, mod[0]], [hf])
                hb = hb_r.next()
                TT("dve", hb[0:n, :], hf[0:n, :], mod[1][0:n, :], ALU.add, [hf, mod[1]], [hb])
                pb = pbr.next()
                for k in range(8):
                    TRN(pb[:, k * 128:k * 128 + n], hb[0:n, k * 128:(k + 1) * 128], ident[0:n, 0:n], [hb, ident], [pb])
                CP("act", hT[:, :, i * n:(i + 1) * n], pb[:].rearrange("p (k t) -> p k t", t=128)[:, :, 0:n], [pb], [hT])

            def wload(c0, m):
                wb_ = wst_r.next()
                DMA("sp", wb_[:, :, 0:m], wsc[l % 2][:, 8 * c0:8 * c0 + 8 * m].rearrange("p (k m) -> p k m", m=m), [r_wsc[l % 2][c0]], [wb_])
                return wb_

            chk(31)
            def fm(c0, m):
                wb_ = wload(c0, m)
                ps = pf.next()
                for k in range(8):
                    MM(ps[0:m, 0:NTOK], wb_[:, k, 0:m], hT[:, k, 0:NTOK], [wb_, hT], [ps], start=(k == 0), stop=(k == 7))
                return ps

            def tm(i, wb_, wd, ps, o0):
                for k in range(8):
                    MM(ps[0:n, o0:o0 + wd], hT[:, k, i * n:(i + 1) * n], wb_[:, k, 0:wd], [wb_, hT], [ps], start=(k == 0), stop=(k == 7))

            if mode == "p":
                rt = rtab_r.next()
                DMA("sp", rt[:, 0, 0:BLK], cin["c_cos_p"][:, blk * BLK:(blk + 1) * BLK], [], [rt])
                DMA("sp", rt[:, 1, 0:BLK], cin["c_sin_p"][:, blk * BLK:(blk + 1) * BLK], [], [rt])
                cosb, sinb, rtb = rt[:, 0, 0:BLK], rt[:, 1, 0:BLK], rt
                s0 = (blk * NTB) % NKT
                kdst = lambda p: kT[:, p, s0 * 128:s0 * 128 + BLK]
                kdst_b = kT
            else:
                cosb, sinb, rtb = C["c_cos_s"][:], C["c_sin_s"][:], C["c_cos_s"]
                kdst = lambda p: kTs[:, p, :]
                kdst_b = kTs

            def rope(ps, dst, dstb, scale):
                xb = xb_r.next()
                P.op("act", lambda e, xb=xb, ps=ps: e.mul(xb[:, 0:NTOK], ps[:, 0:NTOK], float(scale)), rs([ps]), rs([xb]))
                ps2 = pf.next()
                MM(ps2[:, 0:NTOK], C["c_rot"][:], xb[:, 0:NTOK], [C["c_rot"], xb], [ps2])
                chk(321)
                t1 = f5_r.next(); t2 = f5_r.next()
                STT("dve", t1[:, 0:NTOK], ps[:, 0:NTOK], scale, cosb, ALU.mult, ALU.mult, [ps, rtb, C["c_sin_s"]], [t1])
                TT("dve", t2[:, 0:NTOK], ps2[:, 0:NTOK], sinb, ALU.mult, [ps2, rtb, C["c_sin_s"]], [t2])
                chk(322)
                TT("dve", dst, t1[:, 0:NTOK], t2[:, 0:NTOK], ALU.add, [t1, t2], [dstb])
                ROPES[0] += 1
                if ROPES[0] == int(os.environ.get("MK_ROPES", "0")):
                    raise Stop()
                chk(323)

            for p in range(4):
                ps = fm(1296 + 128 * p, 128)
                chk(32)
                rope(ps, kdst(p), kdst_b, 1.0)
            for p in range(4):
                ps = fm(784 + 128 * p, 128)
                rope(ps, qT[:, p, 0:NTOK], qT, 0.125)
            chk(33)
            if mode == "p":
                for i in range(nt):
                    tok0 = blk * BLK + i * 128
                    if tok0 >= T - KEEP:
                        pb = pbr.next()
                        sl = (blk * NTB + i) % NKT
                        for p in range(4):
                            TRN(pb[:, p * 128:(p + 1) * 128], kT[:, p, sl * 128:(sl + 1) * 128], ident[:], [kT, ident], [pb])
                        ko = kout_r.next()
                        CP("act", ko[:], pb[:, 0:512], [pb], [ko])
                        DMA("sp", kp[l, tok0 - (T - KEEP):tok0 - (T - KEEP) + 128, :], ko[:], [ko], [])
            else:
                pb = pbr.next()
                for p in range(4):
                    TRN(pb[0:16, p * 128:(p + 1) * 128], kTs[:, p, :], ident[:], [kTs, ident], [pb])
                ko = kout_r.next()
                CP("act", ko[0:16, :], pb[0:16, 0:512], [pb], [ko])
                for b in range(4):
                    DMA("sp", ks[l, b, LC - 4:LC, :], ko[4 * b:4 * b + 4, :], [ko], [])
            chk(34)
            wv = wload(1808, 512)
            for i in range(nt):
                ps = pf.next()
                tm(i, wv, 512, ps, 0)
                vo = vout_r.next()
                if mode == "p":
                    sl = (blk * NTB + i) % NKT
                    CP("act", Vr[:, sl, :, 0:64], ps[:, :].rearrange("p (h d) -> p h d", d=64), [ps], [Vr])
                    tok0 = blk * BLK + i * 128
                    if tok0 >= T - KEEP:
                        CP("dve", vo[:], ps[:, :], [ps], [vo])
                        DMA("sp", vp[l, tok0 - (T - KEEP):tok0 - (T - KEEP) + 128, :], vo[:], [vo], [])
                else:
                    CP("act", Vsn[:, :, 0:64], ps[0:16, :].rearrange("p (h d) -> p h d", d=64), [ps], [Vsn])
                    CP("dve", vo[0:16, :], ps[0:16, :], [ps], [vo])
                    for b in range(4):
                        DMA("sp", vs[l, b, LC - 4:LC, :], vo[4 * b:4 * b + 4, :], [vo], [])
            wg = wload(2320, 512)
            for i in range(nt):
                ps = pf.next()
                tm(i, wg, 512, ps, 0)
                ACTV(sbg[0:n, i, :], ps[0:n, :], AF.Silu, [ps], [sbg])
            wa1 = wload(256, 256)
            wa2 = wload(528, 256)
            for i in range(nt):
                ps = pf.next()
                tm(i, wa1, 256, ps, 0)
                tm(i, wa2, 256, ps, 256)
                CP("dve", vg[0:n, i, :], ps[0:n, 0:256], [ps], [vg])
                ACTV(sag[0:n, i, :], ps[0:n, 256:512], AF.Silu, [ps], [sag])
            chk(35)
            for k2 in range(2):
                ps = fm(2832 + 128 * k2, 128)
                CP("act", uT[:, k2, 0:NTOK], ps[:, 0:NTOK], [ps], [uT])
                ps = fm(3088 + 128 * k2, 128)
                ACTV(scg[:, k2, 0:NTOK], ps[:, 0:NTOK], AF.Silu, [ps], [scg])

            chk(36)
            psq = fm(0, 128)
            psk = fm(128, 128)
            psa = fm(512, 16)
            CP("act", alrT[:, 0:NTOK], psa[0:16, 0:NTOK], [psa], [alrT])
            psl = pf.next()
            MM(psl[:, 0:NTOK], wlr[:], alrT[:, 0:NTOK], [wlr, alrT], [psl])
            t1 = f5_r.next()
            ACTV(t1[:, 0:NTOK], psl[:, 0:NTOK], AF.Exp, [psl, nblr], [t1], bias=nblr[:, 0:1], scale=-1.0)
            ACTV(t1[:, 0:NTOK], t1[:, 0:NTOK], AF.Ln, [t1, one_c], [t1], bias=one_c[:, 0:1])
            rmask = C["c_reset_p"] if mode == "p" else C["c_reset_s"]
            P.op("dve", lambda e, t1=t1: e.tensor_tensor_scan(csb[:, 0:NTOK], rmask[:, 0:NTOK], t1[:, 0:NTOK], 0.0, ALU.mult, ALU.add),
                 rs([rmask, t1]), rs([csb]))
            ACTV(E1[:, 0:NTOK], csb[:, 0:NTOK], AF.Exp, [csb], [E1], scale=-1.0 / 16)
            ACTV(E2[:, 0:NTOK], csb[:, 0:NTOK], AF.Exp, [csb], [E2], scale=1.0 / 16)
            STT("dve", qd[:, 0:NTOK], psq[:, 0:NTOK], 32 ** -0.5, E1[:, 0:NTOK], ALU.mult, ALU.mult, [psq, E1], [qd])
            TT("dve", kd[:, 0:NTOK], psk[:, 0:NTOK], E2[:, 0:NTOK], ALU.mult, [psk, E2], [kd])
            cl = 128 if mode == "p" else 4
            nch = NTOK // cl
            TT("dve", kdec[:, 0:NTOK].rearrange("p (c t) -> p c t", t=cl), kd[:, 0:NTOK].rearrange("p (c t) -> p c t", t=cl),
               E1[:, cl - 1:NTOK:cl].unsqueeze(2).to_broadcast([128, nch, cl]), ALU.mult, [kd, E1], [kdec])
            for h in range(4):
                TS("dve", qdh[h][:, 0:NTOK], qd[:, 0:NTOK], C["c_hm"][:, h:h + 1], None, ALU.mult, None, [qd, C["c_hm"]], [qdh[h]])
            if mode == "s":
                for h in range(4):
                    for b in range(4):
                        TT("dve", qdhb[:, h * 4 + b, :], qdh[h][:, 0:16], C["c_bcol"][:, b, :], ALU.mult, [qdh[h], C["c_bcol"]], [qdhb])
            causal = C["c_causal_p"] if mode == "p" else C["c_causal_s"]
            for i in range(nt):
                tk = slice(i * n, (i + 1) * n)
                pb = pbr.next()
                TRN(pb[0:n, 0:128], kdec[:, tk], ident[:], [kdec, ident], [pb])
                if mode == "p":
                    kth = kdTh.next()
                    for h in range(4):
                        CP("dve", kth[0:n, h, 32 * h:32 * h + 32], pb[0:n, 32 * h:32 * h + 32], [pb], [kth])
                else:
                    kth = kdThs
                    kraw = b5_r.next()
                    CP("dve", kraw[0:16, 0:128], pb[0:16, 0:128], [pb], [kraw])
                    for h in range(4):
                        for b in range(4):
                            TS("dve", kdThs[0:16, h * 4 + b, 32 * h:32 * h + 32], kraw[0:16, 32 * h:32 * h + 32], C["c_bm"][0:16, b:b + 1], None,
                               ALU.mult, None, [kraw, C["c_bm"]], [kdThs])
                psA = pf.next()
                for h in range(4):
                    MM(psA[0:n, h * n:(h + 1) * n], kd[:, tk], qdh[h][:, tk], [kd, qdh[h]], [psA])
                att = attT_r.next()
                TT("dve", att[0:n, 0:4 * n], psA[0:n, 0:4 * n], causal[0:n, 0:4 * n], ALU.mult, [psA, causal], [att])
                pso = pf.next()
                for h in range(4):
                    MM(pso[0:n, 64 * h:64 * h + 64], att[0:n, h * n:(h + 1) * n], vg[0:n, i, 64 * h:64 * h + 64], [att, vg], [pso], start=True, stop=False)
                    if mode == "p":
                        MM(pso[0:n, 64 * h:64 * h + 64], qdh[h][:, tk], Sgb[:], [qdh[h], Sgb], [pso], start=False, stop=True)
                    else:
                        for b in range(4):
                            MM(pso[0:n, 64 * h:64 * h + 64], qdhb[:, h * 4 + b, :], Sgsb[:, b, :], [qdhb, Sgsb], [pso], start=False, stop=(b == 3))
                if mode == "p":
                    psd = pf.next()
                    for h in range(4):
                        MM(psd[:, 0:64], kth[0:n, h, :], vg[0:n, i, 64 * h:64 * h + 64], [kth, vg], [psd], start=(h == 0), stop=(h == 3))
                    STT("dve", Sg[:], Sg[:], E1[:, i * 128 + 127:i * 128 + 128], psd[:, 0:64], ALU.mult, ALU.add, [Sg, E1, psd], [Sg])
                    CP("pool", Sgb[:], Sg[:], [Sg], [Sgb])
                else:
                    psd = pf.next()
                    for b in range(4):
                        for h in range(4):
                            MM(psd[:, 64 * b:64 * b + 64], kdThs[0:16, h * 4 + b, :], vg[0:16, 0, 64 * h:64 * h + 64], [kdThs, vg], [psd], start=(h == 0), stop=(h == 3))
                    for b in range(4):
                        STT("dve", Sgs[:, b, :], Sgs[:, b, :], E1[:, 4 * b + 3:4 * b + 4], psd[:, 64 * b:64 * b + 64], ALU.mult, ALU.add, [Sgs, E1, psd], [Sgs])
                osb = osb_r.next(); osq = osq_r.next(); st = st_r.next()
                CP("act", osb[0:n, :], pso[0:n, 0:256], [pso], [osb])
                TT("dve", osq[0:n, :], osb[0:n, :], osb[0:n, :], ALU.mult, [osb], [osq])
                P.op("dve", lambda e, st=st, osq=osq: e.tensor_reduce(st[0:n, 0:4], osq[0:n, :].rearrange("p (h d) -> p h d", d=64), AX.X, ALU.add),
                     rs([osq]), rs([st]))
                ACTV(st[0:n, 0:4], st[0:n, 0:4], AF.Sqrt, [st, eps_c], [st], bias=eps_c[0:n, 0:1], scale=1.0 / 64)
                P.op("dve", lambda e, st=st: e.reciprocal(st[0:n, 0:4], st[0:n, 0:4]), rs([st]), rs([st]))
                TT("dve", osb[0:n, :].rearrange("p (h d) -> p h d", d=64), osb[0:n, :].rearrange("p (h d) -> p h d", d=64),
                   st[0:n, 0:4].unsqueeze(2).to_broadcast([n, 4, 64]), ALU.mult, [osb, st], [osb])
                TT("dve", osb[0:n, :], osb[0:n, :], ggla[0:n, :], ALU.mult, [osb, ggla], [osb])
                TT("dve", otm[0:n, i, 0:256], osb[0:n, :], sag[0:n, i, :], ALU.mult, [osb, sag], [otm])

            chk(37)
            Ls = SSM_L if mode == "p" else 16
            for c0 in range(0, NTOK, Ls):
                tk = slice(c0, c0 + Ls)
                psY, psW = pacc
                ct, stb = (cosT, sinT) if mode == "p" else (cosS, sinS)

                def ssm_s1(g):
                    gc = g // 8
                    ps = pf.next()
                    MM(ps[:, 0:Ls], LA[:, g, :], uT[:, gc, tk], [LA, uT], [ps])
                    MM(ps[:, 256:256 + Ls], LB[:, g, :], uT[:, gc, tk], [LB, uT], [ps])
                    f1 = ssmf_r.next(); f2 = ssmf_r.next()
                    TT("dve", f1[:, 0:Ls], ps[:, 0:Ls], ct[:, g, 0:Ls], ALU.mult, [ps, ct], [f1])
                    TT("dve", f2[:, 0:Ls], ps[:, 256:256 + Ls], stb[:, g, 0:Ls], ALU.mult, [ps, stb], [f2])
                    TT("pool", f1[:, 0:Ls], f1[:, 0:Ls], f2[:, 0:Ls], ALU.add, [f1, f2], [f1])
                    wb = ssmw_r.next()
                    if mode == "p":
                        P.op("dve", lambda e, wb=wb, f1=f1, g=g: e.tensor_tensor_scan(wb[:, 0:Ls], rhoF[:, g:g + 1].to_broadcast([128, Ls]), f1[:, 0:Ls],
                                                                                   winit[:, g:g + 1], ALU.mult, ALU.add), rs([rhoF, f1, winit]), rs([wb]))
                    else:
                        STT("dve", f1[:, 0:16:4], x0s[:, :, g], rhoF[:, g:g + 1], f1[:, 0:16:4], ALU.mult, ALU.add, [x0s, rhoF, f1], [f1])
                        P.op("dve", lambda e, wb=wb, f1=f1, g=g: e.tensor_tensor_scan(wb[:, 0:16], rhoS[:, g, :], f1[:, 0:16], 0.0, ALU.mult, ALU.add),
                             rs([rhoS, f1]), rs([wb]))
                    u1 = ssmu_r.next(); u2 = ssmu_r.next()
                    TT("pool", u1[:, 0:Ls], wb[:, 0:Ls], ct[:, g, 0:Ls], ALU.mult, [wb, ct], [u1])
                    TT("pool", u2[:, 0:Ls], wb[:, 0:Ls], stb[:, g, 0:Ls], ALU.mult, [wb, stb], [u2])
                    return (g, wb, u1, u2)

                def ssm_s2(g, wb, u1, u2):
                    gc, gl = g // 8, g % 8
                    MM(psY[:, gc * 256:gc * 256 + Ls], C1g[:, g, :], u1[:, 0:Ls], [C1g, u1], [psY], start=(gl == 0), stop=False)
                    MM(psY[:, gc * 256:gc * 256 + Ls], C2g[:, g, :], u2[:, 0:Ls], [C2g, u2], [psY], start=False, stop=(gl == 7))
                    if mode == "p":
                        MM(psW[:, g:g + 1], RotP[:, g, :], wb[:, Ls - 1:Ls], [RotP, wb], [psW])
                    else:
                        MM(psW[:, 4 * g:4 * g + 4], RotS[:, g, :], wb[:, 3:16:4], [RotS, wb], [psW])

                spend = []
                for g in range(16):
                    spend.append(ssm_s1(g))
                    if len(spend) > 1:
                        ssm_s2(*spend.pop(0))
                while spend:
                    ssm_s2(*spend.pop(0))
                if mode == "p":
                    CP("dve", winit[:], psW[:, 0:16], [psW], [winit])
                else:
                    so = f5_r.next()
                    CP("dve", so[:, 0:64].rearrange("p (b g) -> p b g", g=16), psW[:, 0:64].rearrange("p (g b) -> p b g", b=4), [psW], [so])
                    DMA("sp", ssms[l], so[:, 0:64].rearrange("p (b g) -> p b g", g=16), [so], [])
                for k2 in range(2):
                    STT("dve", yt[:, k2, 0:Ls], uT[:, k2, tk], dskb[:, k2:k2 + 1], psY[:, k2 * 256:k2 * 256 + Ls], ALU.mult, ALU.add, [uT, dskb, psY], [yt])
                ACTV(zT[:, :, 0:Ls], yt[:, :, 0:Ls], AF.Gelu, [yt], [zT])
                psG = pf.next()
                for oc in range(2):
                    for k2 in range(2):
                        MM(psG[:, oc * 256:oc * 256 + Ls], w_glu[:, k2, oc * 128:(oc + 1) * 128], zT[:, k2, 0:Ls], [w_glu, zT], [psG], start=(k2 == 0), stop=(k2 == 1))
                for oc in range(2):
                    ACTV(sgs[:, oc, 0:Ls], psG[:, oc * 256:oc * 256 + Ls], AF.Sigmoid, [psG, bglub], [sgs], bias=bglub[:, oc:oc + 1])
                TT("dve", sgs[:, :, 0:Ls], sgs[:, :, 0:Ls], zT[:, :, 0:Ls], ALU.mult, [sgs, zT], [sgs])
                TT("dve", mixT[:, 6:8, tk], sgs[:, :, 0:Ls], scg[:, :, tk], ALU.mult, [sgs, scg], [mixT])

            chk(38)
            if mode == "p":
                for i in range(nt):
                    qi = blk * NTB + i
                    k_lo = max(0, qi - 16)
                    kis = list(range(k_lo, qi + 1))
                    accs = pacc
                    groups = [kis[g0:g0 + 4] for g0 in range(0, len(kis), 4)]

                    def swa_s1(h, grp, i=i, qi=qi):
                        p, hb_ = h // 2, 64 * (h % 2)
                        ps = pf.next()
                        for idx, ki in enumerate(grp):
                            sl = ki % NKT
                            MM(ps[:, idx * 128:(idx + 1) * 128], kT[hb_:hb_ + 64, p, sl * 128:(sl + 1) * 128], qT[hb_:hb_ + 64, p, i * 128:(i + 1) * 128], [kT, qT], [ps])
                        w = len(grp) * 128
                        pt = pt_r.next(); pm = pm_r.next()
                        ACTV(pt[:, 0:w], ps[:, 0:w], AF.Exp, [ps], [pt])
                        m0 = grp[0] - (qi - 16)
                        SWA_CNT[0] += 1
                        meng = "pool" if SWA_CNT[0] % 2 == 0 else "dve"
                        TT(meng, pm[:, 0:w], pt[:, 0:w], C["c_swam"][:, m0:m0 + len(grp), :].rearrange("p a b -> p (a b)"), ALU.mult, [pt, C["c_swam"]], [pm])
                        return (h, grp, pm)

                    def swa_s2(h, grp, pm, kis=kis):
                        acc = accs[h // 4]
                        hh = h % 4
                        for idx, ki in enumerate(grp):
                            sl = ki % NKT
                            MM(acc[:, hh * 65:(hh + 1) * 65], pm[:, idx * 128:(idx + 1) * 128], Vr[:, sl, h, :], [pm, Vr], [acc],
                               start=(ki == kis[0]), stop=(ki == kis[-1]))

                    pend = []
                    for h in range(8):
                        for grp in groups:
                            pend.append(swa_s1(h, grp))
                            if len(pend) > 2:
                                swa_s2(*pend.pop(0))
                    while pend:
                        swa_s2(*pend.pop(0))
                    for a in range(2):
                        acc = accs[a]
                        st = st_r.next()
                        av_ = acc[:, 0:260].rearrange("p (h d) -> p h d", d=65)
                        P.op("dve", lambda e, st=st, av_=av_: e.reciprocal(st[:, 0:4], av_[:, :, 64]), rs([acc]), rs([st]))
                        ob = f5_r.next()
                        TT("dve", ob[:, 0:256].rearrange("p (h d) -> p h d", d=64), av_[:, :, 0:64], st[:, 0:4].unsqueeze(2).to_broadcast([128, 4, 64]), ALU.mult, [acc, st], [ob])
                        TT("dve", otm[:, i, 256 + a * 256:512 + a * 256], ob[:, 0:256], sbg[:, i, a * 256:(a + 1) * 256], ALU.mult, [ob, sbg], [otm])
            else:
                accs = pacc
                ckTs = {}
                for b in range(4):
                    for t in range(16):
                        ctk = ctile_k.next(); ctv = ctile_v.next()
                        DMA("sp", ctk[:], ck[l, b, t * 128:(t + 1) * 128, :], [], [ctk])
                        DMA("sp", ctv[:], cv[l, b, t * 128:(t + 1) * 128, :], [], [ctv])
                        if t == 0:
                            DMA("sp", ks[l, b, 0:124, :], ctk[4:128, :], [ctk], [])
                            DMA("sp", vs[l, b, 0:124, :], ctv[4:128, :], [ctv], [])
                        else:
                            DMA("sp", ks[l, b, t * 128 - 4:t * 128 + 124, :], ctk[:], [ctk], [])
                            DMA("sp", vs[l, b, t * 128 - 4:t * 128 + 124, :], ctv[:], [ctv], [])
                        ckb = ckb_r.next(); cvb = cvb_r.next()
                        CP("act", ckb[:], ctk[:], [ctk], [ckb])
                        CP("pool", cvb[:, :, 0:64], ctv[:].rearrange("p (h d) -> p h d", d=64), [ctv], [cvb])
                        pb = pbr.next()
                        for p in range(4):
                            TRN(pb[:, p * 128:(p + 1) * 128], ckb[:, p * 128:(p + 1) * 128], ident[:], [ckb, ident], [pb])
                        ckT = ckT_r.next()
                        CP("dve", ckT[:], pb[:, 0:512].rearrange("p (a b) -> p a b", b=128), [pb], [ckT])
                        pse = pf.next(); pso_ = pf.next()
                        for h in range(8):
                            p, hb_ = h // 2, 64 * (h % 2)
                            pst = pse if h % 2 == 0 else pso_
                            MM(pst[:, p * 16:(p + 1) * 16], ckT[hb_:hb_ + 64, p, :], qT[hb_:hb_ + 64, p, 0:16], [ckT, qT], [pst])
                        pt = pt_r.next(); pm = pm_r.next()
                        ACTV(pt[:, 0:64], pse[:, 0:64], AF.Exp, [pse], [pt])
                        ACTV(pt[:, 64:128], pso_[:, 0:64], AF.Exp, [pso_], [pt])
                        TT("pool", pm[:, 0:128].rearrange("p (h q) -> p h q", q=16), pt[:, 0:128].rearrange("p (h q) -> p h q", q=16),
                           C["c_swam_s"][:, b, t, :].unsqueeze(1).to_broadcast([128, 8, 16]), ALU.mult, [pt, C["c_swam_s"]], [pm])
                        for h in range(8):
                            cbk = (h % 2) * 4 + h // 2
                            MM(accs[h // 4][0:16, (h % 4) * 65:(h % 4 + 1) * 65], pm[:, cbk * 16:(cbk + 1) * 16], cvb[:, h, :], [pm, cvb], [accs[h // 4]],
                               start=(b == 0 and t == 0), stop=False)
                pse = pf.next(); pso_ = pf.next()
                for h in range(8):
                    p, hb_ = h // 2, 64 * (h % 2)
                    pst = pse if h % 2 == 0 else pso_
                    MM(pst[0:16, p * 16:(p + 1) * 16], kTs[hb_:hb_ + 64, p, :], qT[hb_:hb_ + 64, p, 0:16], [kTs, qT], [pst])
                pt = pt_r.next(); pm = pm_r.next()
                ACTV(pt[0:16, 0:64], pse[0:16, 0:64], AF.Exp, [pse], [pt])
                ACTV(pt[0:16, 64:128], pso_[0:16, 0:64], AF.Exp, [pso_], [pt])
                TT("pool", pm[0:16, 0:128].rearrange("p (h q) -> p h q", q=16), pt[0:16, 0:128].rearrange("p (h q) -> p h q", q=16),
                   C["c_swam_n"][:].unsqueeze(1).to_broadcast([16, 8, 16]), ALU.mult, [pt, C["c_swam_n"]], [pm])
                for h in range(8):
                    cbk = (h % 2) * 4 + h // 2
                    MM(accs[h // 4][0:16, (h % 4) * 65:(h % 4 + 1) * 65], pm[0:16, cbk * 16:(cbk + 1) * 16], Vsn[:, h, :], [pm, Vsn], [accs[h // 4]], start=False, stop=True)
                for a in range(2):
                    acc = accs[a]
                    st = st_r.next()
                    av_ = acc[0:16, 0:260].rearrange("p (h d) -> p h d", d=65)
                    P.op("dve", lambda e, st=st, av_=av_: e.reciprocal(st[0:16, 0:4], av_[:, :, 64]), rs([acc]), rs([st]))
                    ob = f5_r.next()
                    TT("dve", ob[0:16, 0:256].rearrange("p (h d) -> p h d", d=64), av_[:, :, 0:64], st[0:16, 0:4].unsqueeze(2).to_broadcast([16, 4, 64]), ALU.mult, [acc, st], [ob])
                    TT("dve", otm[0:16, 0, 256 + a * 256:512 + a * 256], ob[0:16, 0:256], sbg[0:16, 0, a * 256:(a + 1) * 256], ALU.mult, [ob, sbg], [otm])

            chk(39)
            for i in range(nt):
                pb = pbr.next()
                for c6 in range(6):
                    TRN(pb[:, c6 * 128:c6 * 128 + n], otm[0:n, i, c6 * 128:(c6 + 1) * 128], ident[0:n, 0:n], [otm, ident], [pb])
                CP("act", mixT[:, 0:6, i * n:(i + 1) * n], pb[:, 0:768].rearrange("p (k t) -> p k t", t=128)[:, :, 0:n], [pb], [mixT])
            pss = [[pf.next(), pf.next()] for _ in range(nt)]
            for half in range(2):
                wo_ = wst_r.next()
                DMA("sp", wo_[:], wosc[l % 2][:, half * 4096:(half + 1) * 4096].rearrange("p (k m) -> p k m", m=512), [r_wosc[l % 2][half]], [wo_])
                for i in range(nt):
                    ps = pss[i][half]
                    for k in range(8):
                        MM(ps[0:n, :], mixT[:, k, i * n:(i + 1) * n], wo_[:, k, :], [mixT, wo_], [ps], start=(k == 0), stop=(k == 7))
            for i in range(nt):
                psa_, psb_ = pss[i]
                yo = yo_r.next(); jk = junk_r.next(); st = st_r.next()
                CP("act", yo[0:n, 0:512], psa_[0:n, :], [psa_], [yo])
                CP("act", yo[0:n, 512:1024], psb_[0:n, :], [psb_], [yo])
                ACTV(jk[0:n, :], yo[0:n, :], AF.Square, [yo], [jk, st], accum=st[0:n, 0:1])
                ACTV(st[0:n, 1:2], st[0:n, 0:1], AF.Sqrt, [st, eps_c], [st], bias=eps_c[0:n, 0:1], scale=1.0 / D)
                P.op("dve", lambda e, st=st: e.reciprocal(st[0:n, 2:3], st[0:n, 1:2]), rs([st]), rs([st]))
                STT("dve", yo[0:n, :], yo[0:n, :], st[0:n, 2:3], mod[2][0:n, :], ALU.mult, ALU.mult, [yo, st, mod[2]], [yo])
                xt = xt_r.next()
                src, rsrc = xsrc(i)
                DMA("sp", xt[0:n, :], src, rsrc, [xt])
                TT("dve", yo[0:n, :], yo[0:n, :], xt[0:n, :], ALU.add, [yo, xt], [yo])
                dst, rdst = xdst(i)
                DMA("sp", dst, yo[0:n, :], [yo], rdst, owner=yo)

        P.maxops = int(os.environ.get("MK_MAXOPS", "0"))
        try:
          for l in range(DEPTH):
            layer_setup(l)
            chk(1)
            layer_setup2(l)
            chk(2)
            MS("dve", Sg[:], 0.0, [Sg]); MS("pool", Sgb[:], 0.0, [Sgb]); MS("dve", winit[:], 0.0, [winit])
            for blk in range(NB):
                do_block(l, "p", blk)
                chk(3)
            chk(4)
            go = f5_r.next()
            CP("dve", go[:, 0:64], Sg[:], [Sg], [go])
            DMA("sp", glap[l], go[:, 0:64], [go], [])
            wo = f5_r.next()
            CP("dve", wo[:, 0:16], winit[:], [winit], [wo])
            DMA("sp", ssmp[l], wo[:, 0:16], [wo], [])
            DMA("sp", Sgs[:], sgla[l].rearrange("b p e -> p b e"), [], [Sgs])
            CP("dve", Sgsb[:], Sgs[:], [Sgs], [Sgsb])
            DMA("sp", x0s[:], sssm[l], [], [x0s])
            build_rot(4.0)
            do_block(l, "s", 0)
            DMA("sp", glas[l].rearrange("b p e -> p b e"), Sgs[:], [Sgs], [])
        except (Stop, StopBuild):
            pass
        print("MK nops", P.nops, "sbuf_free", nc.sbuf_bytes_remaining, flush=True)
        P.emit()
    return nc


def _host_inputs(inp, core, T, DEPTH):
    f = np.float32
    pb = core // 2
    sbs = slice(4 * core, 4 * core + 4)
    m = {}
    m["xp"] = np.ascontiguousarray(inp["x_prompt"][pb, :T]).astype(f)
    m["xs"] = np.ascontiguousarray(inp["x_sample"][sbs].reshape(16, D)).astype(f)
    c5 = np.concatenate([inp["c_prompt"][pb:pb + 1], inp["c_sample"][sbs]], 0)
    m["cT"] = np.ascontiguousarray(c5.T.reshape(8, 128, 5).transpose(1, 0, 2)).astype(f)
    L = DEPTH
    m["w_in_r"] = np.ascontiguousarray(inp["w_in"][:L].reshape(L, 8, 128, NCOL).transpose(0, 2, 1, 3))
    m["w_out_r"] = np.ascontiguousarray(inp["w_out"][:L].reshape(L, 8, 128, D).transpose(0, 2, 1, 3))
    m["w_ada_r"] = np.ascontiguousarray(inp["w_ada"][:L].reshape(L, 8, 128, 3 * D).transpose(0, 2, 1, 3))
    bc = lambda a: np.ascontiguousarray(np.broadcast_to(a[:L, None, :], (L, 128, a.shape[-1]))).astype(f)
    m["b_ada_b"] = bc(inp["b_ada"]); m["g_pre_b"] = bc(inp["g_pre"]); m["g_post_b"] = bc(inp["g_post"]); m["g_gla_b"] = bc(inp["g_gla"])
    m["w_lr"] = np.ascontiguousarray(inp["w_gla_lr"][:L]); m["b_lr"] = np.ascontiguousarray(inp["b_gla_lr"][:L, :, None])
    lam_re, lam_im, ldt = inp["ssm_lambda_re"][:L], inp["ssm_lambda_im"][:L], inp["ssm_log_dt"][:L]
    tF = lambda a: np.ascontiguousarray(np.concatenate([a.transpose(0, 2, 1)] * 2, 1)).astype(f)
    m["lamF_re"] = tF(lam_re); m["lamF_im"] = tF(lam_im)
    m["logdtF"] = np.ascontiguousarray(np.broadcast_to(ldt[:, None, :], (L, 128, 16))).astype(f)

    def tB(a):
        a4 = a.reshape(L, 2, 8, 64)
        o = np.broadcast_to(a4.transpose(0, 2, 1, 3)[:, :, None, :, :], (L, 8, 16, 2, 64))
        return np.ascontiguousarray(o.reshape(L, 128, 128)).astype(f)
    m["lamB_re"] = tB(lam_re); m["lamB_im"] = tB(lam_im)
    m["logdtB"] = tB(np.broadcast_to(ldt[:, :, None], (L, 16, 64)))

    def tBt(bb):
        b5 = bb.reshape(L, 2, 8, 64, 16)
        return np.ascontiguousarray(b5.transpose(0, 2, 4, 1, 3).reshape(L, 128, 128)).astype(f)
    m["Bt_re"] = tBt(inp["ssm_b_re"][:L]); m["Bt_im"] = tBt(inp["ssm_b_im"][:L])
    tC = lambda cc: np.ascontiguousarray(cc.transpose(0, 3, 1, 2).reshape(L, 64, 256)).astype(f)
    m["Ct_re"] = tC(inp["ssm_c_re"][:L]); m["Ct_im"] = tC(inp["ssm_c_im"][:L])
    kp_ = lambda a: np.ascontiguousarray(a[:L].reshape(L, 2, 128).transpose(0, 2, 1)).astype(f)
    m["dsk"] = kp_(inp["ssm_d"]); m["bglu"] = kp_(inp["b_glu"])
    m["w_glu_r"] = np.ascontiguousarray(inp["w_glu"][:L].reshape(L, 2, 128, 256).transpose(0, 2, 1, 3))
    m["sgla"] = np.ascontiguousarray(inp["state_gla"][:L, sbs].reshape(L, 4, 128, 64))
    sre = inp["state_ssm_re"][:L, sbs]; sim = inp["state_ssm_im"][:L, sbs]
    s2 = np.concatenate([sre.transpose(0, 3, 1, 2), sim.transpose(0, 3, 1, 2)], 1)
    m["sssm"] = np.ascontiguousarray(s2).astype(f)
    m["ck"] = np.ascontiguousarray(inp["cache_swa_k"][:L, sbs].reshape(L, 4, LC, 512))
    m["cv"] = np.ascontiguousarray(inp["cache_swa_v"][:L, sbs].reshape(L, 4, LC, 512))
    return m


_NC_CACHE = {}


def run(inputs, T=4096, DEPTH=4, ncores=8):
    inp = {k: np.asarray(v) for k, v in inputs.items()}
    key = (T, DEPTH)
    if key not in _NC_CACHE:
        _NC_CACHE[key] = build(T, DEPTH)
    nc = _NC_CACHE[key]
    cst = host_consts(T)
    in_maps = []
    for c in range(ncores):
        m = _host_inputs(inp, c, T, DEPTH)
        m.update(cst)
        in_maps.append(m)
    res = run_bass_kernel_spmd(nc, in_maps, core_ids=list(range(ncores)))
    R = res.results
    KEEP = min(LC, T)
    nb = ncores // 2
    f = np.float32
    y_p = np.stack([R[2 * b]["yp"] for b in range(nb)]).astype(f)
    y_s = np.concatenate([R[c]["ys"].reshape(4, 4, D) for c in range(ncores)]).astype(f)
    gla_p = np.stack([R[2 * b]["glap"].reshape(DEPTH, 4, 32, 64) for b in range(nb)], 1).astype(f)
    gla_s = np.concatenate([R[c]["glas"].reshape(DEPTH, 4, 4, 32, 64) for c in range(ncores)], 1).astype(f)
    k_p = np.stack([R[2 * b]["kp"].reshape(DEPTH, KEEP, 8, 64) for b in range(nb)], 1).astype(f)
    v_p = np.stack([R[2 * b]["vp"].reshape(DEPTH, KEEP, 8, 64) for b in range(nb)], 1).astype(f)
    k_s = np.concatenate([R[c]["ks"].reshape(DEPTH, 4, LC, 8, 64) for c in range(ncores)], 1).astype(f)
    v_s = np.concatenate([R[c]["vs"].reshape(DEPTH, 4, LC, 8, 64) for c in range(ncores)], 1).astype(f)

    def unp(a):
        return a[:, 0:64, :].transpose(0, 2, 1), a[:, 64:128, :].transpose(0, 2, 1)
    rp = [unp(R[2 * b]["ssmp"]) for b in range(nb)]
    re_p = np.stack([x[0] for x in rp], 1).astype(f); im_p = np.stack([x[1] for x in rp], 1).astype(f)

    def unps(a):
        return a[:, 0:64].transpose(0, 2, 3, 1), a[:, 64:128].transpose(0, 2, 3, 1)
    rsx = [unps(R[c]["ssms"]) for c in range(ncores)]
    re_s = np.concatenate([x[0] for x in rsx], 1).astype(f); im_s = np.concatenate([x[1] for x in rsx], 1).astype(f)
    return (y_p, y_s, gla_p, gla_s, k_p, v_p, k_s, v_s, re_p, im_p, re_s, im_s)


def kernel(**inputs):
    return run(inputs, 4096, 4, 8)
```

```python
import math
import numpy as np
import ml_dtypes
from contextlib import ExitStack
import concourse.bass as bass
import concourse.mybir as mybir
from concourse.bass_utils import run_bass_kernel_spmd

F32 = mybir.dt.float32
BF16 = mybir.dt.bfloat16
AF = mybir.ActivationFunctionType
ALU = mybir.AluOpType
AX = mybir.AxisListType
PI = math.pi

D = 1024
NCOL = 3344
LC = 2048
SSM_L = 128
NKT = 18
BLK = 256
NTB = BLK // 128
EPS = 1e-6


class Res:
    __slots__ = ("name", "last_w", "reads", "sem", "cnt", "excl")

    def __init__(self, name, excl=False):
        self.excl = excl
        self.name = name
        self.last_w = None
        self.reads = []
        self.sem = None
        self.cnt = 0


class StopBuild(Exception):
    pass


class Prog:
    nops = 0
    maxops = 0
    log = None

    def _tick(self, eng, tag):
        self.nops += 1
        if self.log is not None:
            self.log.append((self.nops, eng, tag))
        if self.maxops and self.nops > self.maxops:
            raise StopBuild()

    ENGS = ("pe", "act", "dve", "pool", "sp")
    COMPUTE = ("pe", "act", "dve", "pool")

    def __init__(self, nc):
        self.nc = nc
        self.streams = {e: [] for e in self.ENGS}
        self.needed = {e: set() for e in self.COMPUTE}
        self.known_c = {e: {x: -1 for x in self.COMPUTE} for e in self.ENGS}
        self.known_d = {e: {} for e in self.ENGS}
        self.dma_sems = []

    def _collect(self, eng, reads, writes, is_dma):
        deps = []
        for r in reads:
            if r.last_w is not None:
                deps.append(r.last_w)
            if r.excl:
                deps.extend(r.reads)
        for w in writes:
            if w.last_w is not None:
                deps.append(w.last_w)
            deps.extend(w.reads)
        waits = []
        for d in deps:
            if d[0] == "c":
                _, x, idx = d
                if x == eng and not is_dma and eng == "pe":
                    continue
                if self.known_c[eng][x] >= idx:
                    continue
                self.known_c[eng][x] = idx
                self.needed[x].add(idx)
                waits.append(d)
            else:
                _, owner, cnt = d
                k = self.known_d[eng].get(id(owner), 0)
                if k >= cnt:
                    continue
                self.known_d[eng][id(owner)] = cnt
                waits.append(d)
        return waits

    def op(self, eng, fn, reads=(), writes=()):
        self._tick(eng, "op")
        waits = self._collect(eng, reads, writes, False)
        idx = len(self.streams[eng])
        ev = ("c", eng, idx)
        self.streams[eng].append({"fn": fn, "waits": waits, "ev": ev})
        for r in reads:
            r.reads.append(ev)
        for w in writes:
            w.last_w = ev
            w.reads = []
        return ev

    def dma(self, eng, fn, reads=(), writes=(), owner=None):
        self._tick(eng, "dma")
        waits = self._collect(eng, reads, writes, True)
        if owner is None:
            owner = writes[0] if writes else reads[0]
        if owner.sem is None:
            owner.sem = True
            self.dma_sems.append(owner)
        owner.cnt += 16
        ev = ("d", owner, owner.cnt)
        self.streams[eng].append({"fn": fn, "waits": waits, "ev": ev})
        for r in reads:
            r.reads.append(ev)
        for w in writes:
            w.last_w = ev
            w.reads = []
        return ev

    def emit(self):
        nc = self.nc
        es = ExitStack()
        with es:
            EPOCH = 12000
            for i, o in enumerate(self.dma_sems):
                o.sem = es.enter_context(nc.semaphore(f"d_{i}"))
            rank = {}
            csem = {}
            for e in self.COMPUTE:
                rank[e] = {idx: (n // EPOCH, n % EPOCH + 1) for n, idx in enumerate(sorted(self.needed[e]))}
                nep = max(1, (len(self.needed[e]) + EPOCH - 1) // EPOCH)
                csem[e] = [es.enter_context(nc.semaphore(f"s_{e}{k}")) for k in range(nep)]
            block = es.enter_context(nc.Block())
            me = self

            def replay(e, name, final=False):
                for o in me.streams[name]:
                    for d in o["waits"]:
                        if d[0] == "c":
                            ep, rv = rank[d[1]][d[2]]
                            e.wait_ge(csem[d[1]][ep], rv)
                        else:
                            e.wait_ge(d[1].sem, d[2])
                    ins = o["fn"](e)
                    ev = o["ev"]
                    if ev[0] == "c":
                        if ev[2] in rank[ev[1]]:
                            ins.then_inc(csem[ev[1]][rank[ev[1]][ev[2]][0]], 1)
                    else:
                        ins.then_inc(ev[1].sem, 16)
                if final:
                    for o in me.dma_sems:
                        e.wait_ge(o.sem, o.cnt)

            @block.tensor
            def _(e):
                replay(e, "pe")

            @block.scalar
            def _(e):
                replay(e, "act")

            @block.vector
            def _(e):
                replay(e, "dve")

            @block.gpsimd
            def _(e):
                replay(e, "pool")

            @block.sync
            def _(e):
                replay(e, "sp", final=True)


class Buf:
    def __init__(self, t, r):
        self.t = t
        self.r = r

    def __getitem__(self, k):
        return self.t[k]


class Ring:
    def __init__(self, bufs):
        self.bufs = bufs
        self.i = 0

    def next(self):
        b = self.bufs[self.i % len(self.bufs)]
        self.i += 1
        return b


def mult_mask(o):
    o = np.asarray(o)
    m = ((o >= 0) & (o <= 128)).astype(np.float32)
    m += ((o >= 0) & (o <= 512) & (o % 4 == 0))
    m += ((o >= 0) & (o <= 2048) & (o % 16 == 0))
    return m


def host_consts(T):
    c = {}
    bf = ml_dtypes.bfloat16
    c["c_ident_bf"] = np.eye(128, dtype=np.float32).astype(bf)
    c["c_ident_f"] = np.eye(128, dtype=np.float32)
    ps = np.zeros((128, 128), np.float32)
    for k in range(128):
        ps[k, (k + 64) % 128] = 1.0
    c["c_pswap"] = ps
    sg = np.ones((128, 1), np.float32)
    sg[64:] = -1.0
    c["c_sgn"] = sg
    half = 8
    inv = (500000.0 ** (-np.arange(half, dtype=np.float32) * (2.0 / 16))).astype(np.float32)

    def rope_tab(pos):
        ang = pos.astype(np.float32)[None, :] * inv[:, None]
        cosd = np.ones((64, len(pos)), np.float32)
        sind = np.zeros((64, len(pos)), np.float32)
        cosd[0:8] = np.cos(ang)
        cosd[8:16] = np.cos(ang)
        sind[0:8] = np.sin(ang)
        sind[8:16] = np.sin(ang)
        return np.concatenate([cosd, cosd], 0), np.concatenate([sind, sind], 0)

    c["c_cos_p"], c["c_sin_p"] = [a.astype(bf) for a in rope_tab(np.arange(T))]
    c["c_cos_s"], c["c_sin_s"] = rope_tab(8192 + (np.arange(16) % 4))
    rot = np.zeros((128, 128), np.float32)
    for hb in (0, 64):
        for j in range(8):
            rot[hb + j + 8, hb + j] = -1.0
            rot[hb + j, hb + j + 8] = 1.0
    c["c_rot"] = rot.astype(bf)
    k = np.arange(128)[:, None]
    q = np.arange(128)[None, :]
    sw = np.zeros((128, 17, 128), np.float32)
    for m in range(17):
        sw[:, m, :] = mult_mask((16 - m) * 128 + q - k)
    c["c_swam"] = sw.astype(bf)
    sws = np.zeros((128, 4, 16, 16), np.float32)
    for b in range(4):
        for t in range(16):
            for i in range(4):
                sws[:, b, t, 4 * b + i] = mult_mask(2048 + i - 128 * t - np.arange(128))
    c["c_swam_s"] = sws.astype(bf)
    swn = np.zeros((16, 16), np.float32)
    for kk in range(16):
        for qq in range(16):
            if kk // 4 == qq // 4:
                swn[kk, qq] = mult_mask(qq - kk)
    c["c_swam_n"] = swn.astype(bf)
    j = np.arange(128)[:, None]
    i = np.arange(128)[None, :]
    cz = (j <= i).astype(np.float32)
    c["c_causal_p"] = np.tile(cz, (1, 4)).astype(bf)
    czs = ((j[:16, :] <= i[:, :16]) & (j[:16, :] // 4 == i[:, :16] // 4)).astype(np.float32)
    c["c_causal_s"] = np.tile(czs, (1, 4)).astype(np.float32)
    rp = np.ones((128, BLK), np.float32)
    rp[:, ::128] = 0.0
    c["c_reset_p"] = rp
    rsx = np.ones((128, 16), np.float32)
    rsx[:, ::4] = 0.0
    c["c_reset_s"] = rsx
    hm = np.zeros((128, 4), np.float32)
    for h in range(4):
        hm[32 * h:32 * h + 32, h] = 1.0
    c["c_hm"] = hm
    bm = np.zeros((16, 4), np.float32)
    for b in range(4):
        bm[4 * b:4 * b + 4, b] = 1.0
    c["c_bm"] = bm
    bcol = np.zeros((128, 4, 16), np.float32)
    for b in range(4):
        bcol[:, b, 4 * b:4 * b + 4] = 1.0
    c["c_bcol"] = bcol
    gm = np.zeros((128, 8), np.float32)
    for g in range(8):
        gm[16 * g:16 * g + 16, g] = 1.0
    c["c_gm"] = gm
    c["c_iota"] = np.tile(np.arange(1, SSM_L + 1, dtype=np.float32)[None, :], (128, 1))
    return c


def build(T, DEPTH):
    NB = T // BLK
    NT = T // 128
    KEEP = min(LC, T)
    nc = bass.Bass("TRN2", target_bir_lowering=False)
    P = Prog(nc)
    es = ExitStack()
    cst = host_consts(T)

    def DI(name, shape, dt=F32):
        return nc.dram_tensor(name, list(shape), dt, kind="ExternalInput").ap()

    def DO(name, shape):
        return nc.dram_tensor(name, list(shape), F32, kind="ExternalOutput").ap()

    xp = DI("xp", [T, D]); xs = DI("xs", [16, D]); cT = DI("cT", [128, 8, 5])
    w_in_r = DI("w_in_r", [DEPTH, 128, 8, NCOL]); w_out_r = DI("w_out_r", [DEPTH, 128, 8, D])
    w_ada_r = DI("w_ada_r", [DEPTH, 128, 8, 3 * D]); b_ada_b = DI("b_ada_b", [DEPTH, 128, 3 * D])
    g_pre_b = DI("g_pre_b", [DEPTH, 128, D]); g_post_b = DI("g_post_b", [DEPTH, 128, D])
    g_gla_b = DI("g_gla_b", [DEPTH, 128, 256]); w_lr = DI("w_lr", [DEPTH, 16, 128]); b_lr = DI("b_lr", [DEPTH, 128, 1])
    lamF_re = DI("lamF_re", [DEPTH, 128, 16]); lamF_im = DI("lamF_im", [DEPTH, 128, 16]); logdtF = DI("logdtF", [DEPTH, 128, 16])
    lamB_re = DI("lamB_re", [DEPTH, 128, 128]); lamB_im = DI("lamB_im", [DEPTH, 128, 128]); logdtB = DI("logdtB", [DEPTH, 128, 128])
    Bt_re = DI("Bt_re", [DEPTH, 128, 128]); Bt_im = DI("Bt_im", [DEPTH, 128, 128])
    Ct_re = DI("Ct_re", [DEPTH, 64, 256]); Ct_im = DI("Ct_im", [DEPTH, 64, 256])
    dsk = DI("dsk", [DEPTH, 128, 2]); bglu = DI("bglu", [DEPTH, 128, 2]); w_glu_r = DI("w_glu_r", [DEPTH, 128, 2, 256])
    sgla = DI("sgla", [DEPTH, 4, 128, 64]); sssm = DI("sssm", [DEPTH, 128, 4, 16])
    ck = DI("ck", [DEPTH, 4, LC, 512]); cv = DI("cv", [DEPTH, 4, LC, 512])
    cin = {}
    for name, arr in cst.items():
        cin[name] = DI(name, arr.shape, BF16 if arr.dtype == ml_dtypes.bfloat16 else F32)

    yp = DO("yp", [T, D]); ys = DO("ys", [16, D])
    glap = DO("glap", [DEPTH, 128, 64]); glas = DO("glas", [DEPTH, 4, 128, 64])
    kp = DO("kp", [DEPTH, KEEP, 512]); vp = DO("vp", [DEPTH, KEEP, 512])
    ks = DO("ks", [DEPTH, 4, LC, 512]); vs = DO("vs", [DEPTH, 4, LC, 512])
    ssmp = DO("ssmp", [DEPTH, 128, 16]); ssms = DO("ssms", [DEPTH, 128, 4, 16])
    xsc = [nc.dram_tensor(f"xsc{i}", [T, D], F32, kind="Internal").ap() for i in range(2)]
    xss = [nc.dram_tensor(f"xss{i}", [16, D], F32, kind="Internal").ap() for i in range(2)]
    r_xsc = [[Res(f"xsc{i}_{b}") for b in range(NB)] for i in range(3)]
    r_xss = [Res(f"xss{i}") for i in range(3)]
    WBLOCKS = [(1296 + 128 * p, 128) for p in range(4)] + [(784 + 128 * p, 128) for p in range(4)] + \
              [(1808, 512), (2320, 512), (256, 256), (528, 256)] + [(2832 + 128 * k, 128) for k in range(2)] + \
              [(3088 + 128 * k, 128) for k in range(2)] + [(0, 128), (128, 128), (512, 16)]
    wsc = [nc.dram_tensor(f"wsc{i}", [128, 8 * NCOL], BF16, kind="Internal").ap() for i in range(2)]
    wosc = [nc.dram_tensor(f"wosc{i}", [128, 8 * D], BF16, kind="Internal").ap() for i in range(2)]
    r_wsc = [{c0: Res(f"wsc{i}_{c0}") for (c0, m) in WBLOCKS} for i in range(2)]
    r_wosc = [[Res(f"wosc{i}_{h}") for h in range(2)] for i in range(2)]

    with es:
        def sb(name, shape, dt=F32):
            return Buf(es.enter_context(nc.sbuf_tensor("sb_" + name, list(shape), dt)), Res(name))

        def ring(name, n, shape, dt=F32):
            return Ring([sb(f"{name}{i}", shape, dt) for i in range(n)])

        def rs(bufs):
            return [b.r if isinstance(b, Buf) else b for b in bufs]

        def MM(out, lhsT, rhs, R, W, start=True, stop=True):
            P.op("pe", lambda e: e.matmul(out, lhsT, rhs, start=start, stop=stop), rs(R), rs(W))

        def TRN(out, in_, ident, R, W):
            P.op("pe", lambda e: e.transpose(out, in_, ident), rs(R), rs(W))

        def ACTV(out, in_, func, R, W, bias=None, scale=None, accum=None):
            kw = {}
            if bias is not None:
                kw["bias"] = bias
            if scale is not None:
                kw["scale"] = scale
            if accum is not None:
                kw["accum_out"] = accum
            P.op("act", lambda e: e.activation(out, in_, func, **kw), rs(R), rs(W))

        def TT(eng, out, a, b, op, R, W):
            P.op(eng, lambda e: e.tensor_tensor(out, a, b, op), rs(R), rs(W))

        def TS(eng, out, a, s1, s2, op0, op1, R, W):
            if op1 is None:
                P.op(eng, lambda e: e.tensor_scalar(out, a, s1, None, op0), rs(R), rs(W))
            else:
                P.op(eng, lambda e: e.tensor_scalar(out, a, s1, s2, op0, op1), rs(R), rs(W))

        def STT(eng, out, a, s, b, op0, op1, R, W):
            P.op(eng, lambda e: e.scalar_tensor_tensor(out, a, s, b, op0, op1), rs(R), rs(W))

        def CP(eng, out, in_, R, W):
            if eng == "act":
                P.op(eng, lambda e: e.copy(out, in_), rs(R), rs(W))
            else:
                P.op(eng, lambda e: e.tensor_copy(out, in_), rs(R), rs(W))

        def MS(eng, out, val, W):
            P.op(eng, lambda e: e.memset(out, val), [], rs(W))

        def DMA(q, out, in_, R, W, owner=None):
            P.dma(q, lambda e: e.dma_start(out=out, in_=in_), rs(R), rs(W), owner.r if isinstance(owner, Buf) else owner)

        pf = Ring([Buf(es.enter_context(nc.psum_tensor(f"pf{i}", [128, 512], F32)), Res(f"pf{i}", True)) for i in range(4)])
        pacc = [Buf(es.enter_context(nc.psum_tensor(f"pacc{i}", [128, 512], F32)), Res(f"pacc{i}", True)) for i in range(3)]
        pbr = Ring([Buf(es.enter_context(nc.psum_tensor(f"pb{i}", [128, 1024], BF16)), Res(f"pb{i}", True)) for i in range(1)])

        C = {}
        for name, arr in cst.items():
            if name in ("c_cos_p", "c_sin_p"):
                continue
            C[name] = sb(name, arr.shape, BF16 if arr.dtype == ml_dtypes.bfloat16 else F32)
        for name, arr in cst.items():
            if name in ("c_cos_p", "c_sin_p"):
                continue
            DMA("sp", C[name][:], cin[name], [], [C[name]])
        ident = C["c_ident_bf"]

        w_glu = sb("w_glu", [128, 2, 256], BF16)
        wlr = sb("wlr", [16, 128], BF16)
        nblr = sb("nblr", [128, 1])
        ggla = sb("ggla", [128, 256])
        dskb = sb("dskb", [128, 2]); bglub = sb("bglub", [128, 2])
        kT = sb("kT", [128, 4, NKT * 128], BF16)
        Vr = sb("Vr", [128, NKT, 8, 65], BF16)
        kTs = sb("kTs", [128, 4, 16], BF16)
        Vsn = sb("Vsn", [16, 8, 65], BF16)
        MS("pool", Vr[:], 1.0, [Vr])
        MS("pool", Vsn[:], 1.0, [Vsn])
        modp = [sb(f"modp{i}", [128, D], BF16) for i in range(3)]
        mods = [sb(f"mods{i}", [16, D], BF16) for i in range(3)]
        scb_p = sb("scb_p", [128, 8, 128], BF16)
        scb_s = sb("scb_s", [128, 8, 16], BF16)
        cosT = sb("cosT", [128, 16, SSM_L], BF16); sinT = sb("sinT", [128, 16, SSM_L], BF16)
        cosS = sb("cosS", [128, 16, 16], BF16); sinS = sb("sinS", [128, 16, 16], BF16)
        rhoF = sb("rhoF", [128, 16]); rhoS = sb("rhoS", [128, 16, 16])
        LA = sb("LA", [128, 16, 128], BF16); LB = sb("LB", [128, 16, 128], BF16)
        C1g = sb("C1g", [128, 16, 128], BF16); C2g = sb("C2g", [128, 16, 128], BF16)
        MS("pool", C1g[:], 0.0, [C1g]); MS("pool", C2g[:], 0.0, [C2g])
        RotP = sb("RotP", [128, 16, 128]); RotS = RotP
        Sg = sb("Sg", [128, 64]); Sgb = sb("Sgb", [128, 64], BF16)
        Sgs = sb("Sgs", [128, 4, 64]); Sgsb = sb("Sgsb", [128, 4, 64], BF16)
        winit = sb("winit", [128, 16])
        x0s = sb("x0s", [128, 4, 16])

        xt_r = ring("xt", 1, [128, D])
        st_r = ring("st", 4, [128, 4])
        hb_r = ring("hb", 1, [128, D], BF16)
        junk_r = hb_r
        hT = sb("hT", [128, 8, BLK], BF16)
        qT = sb("qT", [128, 4, BLK], BF16)
        xb_r = ring("xb", 1, [128, BLK], BF16)
        f5_r = ring("f5", 2, [128, 512])
        sbg = sb("sbg", [128, NTB, 512], BF16)
        sag = sb("sag", [128, NTB, 256], BF16)
        vg = sb("vg", [128, NTB, 256], BF16)
        uT = sb("uT", [128, 2, BLK], BF16)
        scg = sb("scg", [128, 2, BLK], BF16)
        mixT = sb("mixT", [128, 8, BLK], BF16)
        otm = sb("otm", [128, NTB, 768], BF16)
        qd = sb("qd", [128, BLK], BF16); kd = sb("kd", [128, BLK], BF16); kdec = sb("kdec", [128, BLK], BF16)
        qdh = [sb(f"qdh{h}", [128, BLK], BF16) for h in range(4)]
        qdhb = sb("qdhb", [128, 16, 16], BF16)
        E1 = sb("E1", [128, BLK]); E2 = sb("E2", [128, BLK]); csb = sb("csb", [128, BLK])
        alrT = sb("alrT", [16, BLK], BF16)
        kdTh = ring("kdTh", 1, [128, 4, 128], BF16)
        for b_ in kdTh.bufs:
            MS("pool", b_[:], 0.0, [b_])
        kdThs = sb("kdThs", [16, 16, 128], BF16)
        MS("pool", kdThs[:], 0.0, [kdThs])
        attT_r = ring("attT", 1, [128, 512], BF16)
        b5_r = attT_r
        osb_r = ring("osb", 1, [128, 256]); osq_r = ring("osq", 1, [128, 256])
        kout_r = ring("kout", 1, [128, 512]); vout_r = kout_r
        yo_r = ring("yo", 1, [128, D])
        wst_r = ring("wst", 2, [128, 8, 512], BF16)
        pt_r = ring("ptr", 2, [128, 512], BF16); pm_r = ring("pmr", 3, [128, 512], BF16)
        ssmf_r = ring("ssmf", 4, [128, SSM_L]); ssmw_r = ring("ssmw", 2, [128, SSM_L])
        ssmu_r = ring("ssmu", 4, [128, SSM_L], BF16)
        yt = sb("yt", [128, 2, SSM_L]); zT = sb("zT", [128, 2, SSM_L], BF16); sgs = sb("sgs", [128, 2, SSM_L], BF16)
        ctile_k = ring("ctk", 1, [128, 512]); ctile_v = ring("ctv", 1, [128, 512])
        ckb_r = ring("ckb", 1, [128, 512], BF16); cvb_r = ring("cvb", 2, [128, 8, 65], BF16)
        for b_ in cvb_r.bufs:
            MS("pool", b_[:], 1.0, [b_])
        ckT_r = ring("ckT", 2, [128, 4, 128], BF16)
        sm = {}
        for nm in ("dtF", "aF", "thF", "t1", "t2", "cL", "sL", "sLs"):
            sm[nm] = sb("sm_" + nm, [128, 16])
        sB = {}
        for nm in ("lre", "lim", "ldt", "dt", "a", "th", "rho", "c", "s", "lbr", "lbi", "nr", "ni", "den", "kr", "ki", "bre", "bim", "t1", "t2", "t3"):
            sB[nm] = sb("sB_" + nm, [128, 128])
        BbA = sb("BbA", [128, 2, 128]); BbB = sb("BbB", [128, 2, 128])
        A1 = sb("A1", [128, 256]); A2 = sb("A2", [128, 256])
        angt = ring("angt", 2, [128, SSM_L])
        rtab_r = ring("rtab", 1, [128, 2, BLK], BF16)


        I32 = mybir.dt.int32
        sc_f = [sb(f"sc_f{i}", [128, 128]) for i in range(3)]
        sc_i = sb("sc_i", [128, 128], I32)

        def sincos(sin_out, cos_out, ang, w, Rin, Wout):
            tA, tB, tC = sc_f
            TS("dve", tA[:, 0:w], ang, 1.0 / (2 * PI), None, ALU.mult, None, Rin, [tA])
            CP("dve", sc_i[:, 0:w], tA[:, 0:w], [tA], [sc_i])
            CP("dve", tB[:, 0:w], sc_i[:, 0:w], [sc_i], [tB])
            TT("dve", tA[:, 0:w], tA[:, 0:w], tB[:, 0:w], ALU.subtract, [tA, tB], [tA])
            ACTV(tB[:, 0:w], tA[:, 0:w], AF.Sin, [tA], [tB], scale=PI)
            ACTV(tC[:, 0:w], tA[:, 0:w], AF.Sin, [tA], [tC], scale=PI / 2)
            TT("dve", tA[:, 0:w], tB[:, 0:w], tB[:, 0:w], ALU.mult, [tB], [tA])
            TS("dve", cos_out, tA[:, 0:w], -2.0, 1.0, ALU.mult, ALU.add, [tA], Wout)
            TT("dve", tC[:, 0:w], tC[:, 0:w], tC[:, 0:w], ALU.mult, [tC], [tC])
            TS("dve", tC[:, 0:w], tC[:, 0:w], -2.0, 1.0, ALU.mult, ALU.add, [tC], [tC])
            TT("dve", tC[:, 0:w], tC[:, 0:w], tB[:, 0:w], ALU.mult, [tC, tB], [tC])
            TS("dve", sin_out, tC[:, 0:w], 2.0, None, ALU.mult, None, [tC], Wout)

        nPI = sb("nPI", [128, 1])
        MS("dve", nPI[:], -PI, [nPI])
        one_c = sb("one_c", [128, 1])
        MS("dve", one_c[:], 1.0, [one_c])
        eps_c = sb("eps_c", [128, 1])
        MS("dve", eps_c[:], EPS, [eps_c])

        def layer_setup(l):
            for (c0, m) in WBLOCKS:
                wt_ = wst_r.next()
                DMA("pool", wt_[:, :, 0:m], w_in_r[l, :, :, c0:c0 + m], [], [wt_])
                DMA("sp", wsc[l % 2][:, 8 * c0:8 * c0 + 8 * m].rearrange("p (k m) -> p k m", m=m), wt_[:, :, 0:m], [wt_], [r_wsc[l % 2][c0]], owner=wt_)
            for half in range(2):
                wt_ = wst_r.next()
                DMA("pool", wt_[:], w_out_r[l, :, :, half * 512:(half + 1) * 512], [], [wt_])
                DMA("sp", wosc[l % 2][:, half * 4096:(half + 1) * 4096].rearrange("p (k m) -> p k m", m=512), wt_[:], [wt_], [r_wosc[l % 2][half]], owner=wt_)
            DMA("pool", w_glu[:], w_glu_r[l], [], [w_glu])
            DMA("pool", wlr[:], w_lr[l], [], [wlr])
            DMA("sp", nblr[:], b_lr[l], [], [nblr])
            TS("dve", nblr[:], nblr[:], -1.0, None, ALU.mult, None, [nblr], [nblr])
            DMA("sp", ggla[:], g_gla_b[l], [], [ggla])
            gpost = xt_r.next()
            DMA("sp", gpost[:], g_post_b[l], [], [gpost])
            DMA("sp", dskb[:], dsk[l], [], [dskb])
            DMA("sp", bglub[:], bglu[l], [], [bglub])
            if l == 0:
                sc = f5_r.next()
                scv = sc[:, 0:40].rearrange("p (k r) -> p k r", r=5)
                DMA("sp", scv, cT[:, :, :], [], [sc])
                ACTV(scv, scv, AF.Silu, [sc], [sc])
                CP("dve", scb_p[:], scv[:, :, 0:1].to_broadcast([128, 8, 128]), [sc], [scb_p])
                for b in range(4):
                    CP("dve", scb_s[:, :, 4 * b:4 * b + 4], scv[:, :, 1 + b:2 + b].to_broadcast([128, 8, 4]), [sc], [scb_s])
            gp2 = yo_r.next()
            DMA("sp", gp2[:], g_pre_b[l], [], [gp2])
            for cb in range(6):
                wst = wst_r.next()
                DMA("pool", wst[:], w_ada_r[l, :, :, cb * 512:(cb + 1) * 512], [], [wst])
                bad = f5_r.next()
                DMA("sp", bad[:], b_ada_b[l, :, cb * 512:(cb + 1) * 512], [], [bad])
                part = cb // 2
                co = (cb % 2) * 512
                for (lhs, npart, mod) in ((scb_p, 128, modp), (scb_s, 16, mods)):
                    ps = pf.next()
                    for k in range(8):
                        MM(ps[0:npart, :], lhs[:, k, 0:npart], wst[:, k, :], [lhs, wst], [ps], start=(k == 0), stop=(k == 7))
                    tmp = f5_r.next()
                    TT("dve", tmp[0:npart, :], ps[0:npart, :], bad[0:npart, :], ALU.add, [ps, bad], [tmp])
                    if part == 0:
                        CP("pool", mod[1][0:npart, co:co + 512], tmp[0:npart, :], [tmp], [mod[1]])
                    elif part == 1:
                        STT("dve", mod[0][0:npart, co:co + 512], tmp[0:npart, :], 1.0, gp2[0:npart, co:co + 512], ALU.add, ALU.mult, [tmp, gp2], [mod[0]])
                    else:
                        TT("dve", mod[2][0:npart, co:co + 512], tmp[0:npart, :], gpost[0:npart, co:co + 512], ALU.mult, [tmp, gpost], [mod[2]])
            lre = sm["t1"]; lim = sm["t2"]
            DMA("sp", lre[:], lamF_re[l], [], [lre]); DMA("sp", lim[:], lamF_im[l], [], [lim])
            DMA("sp", sm["dtF"][:], logdtF[l], [], [sm["dtF"]])
            ACTV(sm["dtF"][:], sm["dtF"][:], AF.Exp, [sm["dtF"]], [sm["dtF"]])
            TT("dve", sm["aF"][:], lre[:], sm["dtF"][:], ALU.mult, [lre, sm["dtF"]], [sm["aF"]])
            TT("dve", sm["thF"][:], lim[:], sm["dtF"][:], ALU.mult, [lim, sm["dtF"]], [sm["thF"]])
            ACTV(rhoF[:], sm["aF"][:], AF.Exp, [sm["aF"]], [rhoF])
            TT("dve", rhoS[:], rhoF[:].unsqueeze(2).to_broadcast([128, 16, 16]),
               C["c_reset_s"][:].unsqueeze(1).to_broadcast([128, 16, 16]), ALU.mult, [rhoF, C["c_reset_s"]], [rhoS])
            for g in range(16):
                a1 = angt.next()
                TS("dve", a1[:], C["c_iota"][:], sm["thF"][:, g:g + 1], None, ALU.mult, None, [C["c_iota"], sm["thF"]], [a1])
                sincos(sinT[:, g, :], cosT[:, g, :], a1[:], SSM_L, [a1], [sinT, cosT])
            for b in range(4):
                CP("pool", cosS[:, :, 4 * b:4 * b + 4], cosT[:, :, 0:4], [cosT], [cosS])
                CP("pool", sinS[:, :, 4 * b:4 * b + 4], sinT[:, :, 0:4], [sinT], [sinS])
            build_rot(float(SSM_L))

        def build_rot(Lr):
            Rot = RotP
            if True:
                TS("dve", sm["t1"][:], sm["thF"][:], Lr, None, ALU.mult, None, [sm["thF"]], [sm["t1"]])
                sincos(sm["sL"][:], sm["cL"][:], sm["t1"][:], 16, [sm["t1"]], [sm["sL"], sm["cL"]])
                TS("dve", sm["sLs"][:], sm["sL"][:], C["c_sgn"][:, 0:1], None, ALU.mult, None, [sm["sL"], C["c_sgn"]], [sm["sLs"]])
                for g in range(16):
                    TS("dve", Rot[:, g, :], C["c_ident_f"][:], sm["cL"][:, g:g + 1], None, ALU.mult, None, [C["c_ident_f"], sm["cL"]], [Rot])
                    STT("dve", Rot[:, g, :], C["c_pswap"][:], sm["sLs"][:, g:g + 1], Rot[:, g, :], ALU.mult, ALU.add, [C["c_pswap"], sm["sLs"], Rot], [Rot])

        def layer_setup2(l):
            q = sB
            DMA("sp", q["lre"][:], lamB_re[l], [], [q["lre"]]); DMA("sp", q["lim"][:], lamB_im[l], [], [q["lim"]])
            DMA("sp", q["ldt"][:], logdtB[l], [], [q["ldt"]])
            DMA("sp", q["bre"][:], Bt_re[l], [], [q["bre"]]); DMA("sp", q["bim"][:], Bt_im[l], [], [q["bim"]])
            ACTV(q["dt"][:], q["ldt"][:], AF.Exp, [q["ldt"]], [q["dt"]])
            TT("dve", q["a"][:], q["lre"][:], q["dt"][:], ALU.mult, [q["lre"], q["dt"]], [q["a"]])
            TT("dve", q["th"][:], q["lim"][:], q["dt"][:], ALU.mult, [q["lim"], q["dt"]], [q["th"]])
            ACTV(q["rho"][:], q["a"][:], AF.Exp, [q["a"]], [q["rho"]])
            sincos(q["s"][:], q["c"][:], q["th"][:], 128, [q["th"]], [q["s"], q["c"]])
            STT("dve", q["lbr"][:], q["rho"][:], 1.0, q["c"][:], ALU.mult, ALU.mult, [q["rho"], q["c"]], [q["lbr"]])
            TS("dve", q["lbr"][:], q["lbr"][:], -1.0, None, ALU.add, None, [q["lbr"]], [q["lbr"]])
            TT("dve", q["lbi"][:], q["rho"][:], q["s"][:], ALU.mult, [q["rho"], q["s"]], [q["lbi"]])
            TT("dve", q["t1"][:], q["lbr"][:], q["lre"][:], ALU.mult, [q["lbr"], q["lre"]], [q["t1"]])
            TT("dve", q["t2"][:], q["lbi"][:], q["lim"][:], ALU.mult, [q["lbi"], q["lim"]], [q["t2"]])
            TT("dve", q["nr"][:], q["t1"][:], q["t2"][:], ALU.add, [q["t1"], q["t2"]], [q["nr"]])
            TT("dve", q["t1"][:], q["lbi"][:], q["lre"][:], ALU.mult, [q["lbi"], q["lre"]], [q["t1"]])
            TT("dve", q["t2"][:], q["lbr"][:], q["lim"][:], ALU.mult, [q["lbr"], q["lim"]], [q["t2"]])
            TT("dve", q["ni"][:], q["t1"][:], q["t2"][:], ALU.subtract, [q["t1"], q["t2"]], [q["ni"]])
            TT("dve", q["t1"][:], q["lre"][:], q["lre"][:], ALU.mult, [q["lre"]], [q["t1"]])
            TT("dve", q["t2"][:], q["lim"][:], q["lim"][:], ALU.mult, [q["lim"]], [q["t2"]])
            TT("dve", q["den"][:], q["t1"][:], q["t2"][:], ALU.add, [q["t1"], q["t2"]], [q["den"]])
            P.op("dve", lambda e: e.reciprocal(q["den"][:], q["den"][:]), rs([q["den"]]), rs([q["den"]]))
            TT("dve", q["kr"][:], q["nr"][:], q["den"][:], ALU.mult, [q["nr"], q["den"]], [q["kr"]])
            TT("dve", q["ki"][:], q["ni"][:], q["den"][:], ALU.mult, [q["ni"], q["den"]], [q["ki"]])
            TT("dve", q["t1"][:], q["kr"][:], q["bre"][:], ALU.mult, [q["kr"], q["bre"]], [q["t1"]])
            TT("dve", q["t2"][:], q["ki"][:], q["bim"][:], ALU.mult, [q["ki"], q["bim"]], [q["t2"]])
            TT("dve", q["t3"][:], q["t1"][:], q["t2"][:], ALU.subtract, [q["t1"], q["t2"]], [q["t3"]])
            TT("dve", q["t1"][:], q["kr"][:], q["bim"][:], ALU.mult, [q["kr"], q["bim"]], [q["t1"]])
            TT("dve", q["t2"][:], q["ki"][:], q["bre"][:], ALU.mult, [q["ki"], q["bre"]], [q["t2"]])
            TT("dve", q["t1"][:], q["t1"][:], q["t2"][:], ALU.add, [q["t1"], q["t2"]], [q["t1"]])
            for gc in range(2):
                CP("dve", BbA[:, gc, 0:64], q["t3"][:, gc * 64:(gc + 1) * 64], [q["t3"]], [BbA])
                CP("dve", BbA[:, gc, 64:128], q["t1"][:, gc * 64:(gc + 1) * 64], [q["t1"]], [BbA])
                CP("dve", BbB[:, gc, 0:64], q["t1"][:, gc * 64:(gc + 1) * 64], [q["t1"]], [BbB])
                TS("dve", BbB[:, gc, 64:128], q["t3"][:, gc * 64:(gc + 1) * 64], -1.0, None, ALU.mult, None, [q["t3"]], [BbB])
            for g in range(16):
                gc, gl = g // 8, g % 8
                TS("dve", LA[:, g, :], BbA[:, gc, :], C["c_gm"][:, gl:gl + 1], None, ALU.mult, None, [BbA, C["c_gm"]], [LA])
                TS("dve", LB[:, g, :], BbB[:, gc, :], C["c_gm"][:, gl:gl + 1], None, ALU.mult, None, [BbB, C["c_gm"]], [LB])
            DMA("sp", A1[0:64, :], Ct_re[l], [], [A1]); DMA("sp", A1[64:128, :], Ct_im[l], [], [A1])
            DMA("sp", A2[0:64, :], Ct_im[l], [], [A2]); DMA("sp", A2[64:128, :], Ct_re[l], [], [A2])
            TS("dve", A1[64:128, :], A1[64:128, :], -1.0, None, ALU.mult, None, [A1], [A1])
            TS("dve", A2[:], A2[:], -1.0, None, ALU.mult, None, [A2], [A2])
            for g in range(16):
                gl = g % 8
                CP("pool", C1g[:, g, 16 * gl:16 * gl + 16], A1[:, 16 * g:16 * g + 16], [A1], [C1g])
                CP("pool", C2g[:, g, 16 * gl:16 * gl + 16], A2[:, 16 * g:16 * g + 16], [A2], [C2g])

        import os
        STAGE = int(os.environ.get("MK_STAGE", "99"))

        class Stop(Exception):
            pass

        def chk(k):
            if STAGE == k:
                raise Stop()

        ROPES = [0]
        SWA_CNT = [0]

        def do_block(l, mode, blk):
            if mode == "p":
                n, nt = 128, NTB
                mod = modp
            else:
                n, nt = 16, 1
                mod = mods
            NTOK = n * nt
            last = (l == DEPTH - 1)

            def chk(k):
                if STAGE >= 100:
                    if mode == "s" and STAGE - 100 == k:
                        raise Stop()
                elif mode == "p" and STAGE == k:
                    raise Stop()

            def xsrc(i):
                if mode == "p":
                    t0 = blk * BLK + i * 128
                    if l == 0:
                        return xp[t0:t0 + 128, :], []
                    return xsc[(l - 1) % 2][t0:t0 + 128, :], [r_xsc[(l - 1) % 2][blk]]
                if l == 0:
                    return xs[:, :], []
                return xss[(l - 1) % 2][:, :], [r_xss[(l - 1) % 2]]

            def xdst(i):
                if mode == "p":
                    t0 = blk * BLK + i * 128
                    if last:
                        return yp[t0:t0 + 128, :], []
                    return xsc[l % 2][t0:t0 + 128, :], [r_xsc[l % 2][blk]]
                if last:
                    return ys[:, :], []
                return xss[l % 2][:, :], [r_xss[l % 2]]

            for i in range(nt):
                xt = xt_r.next()
                src, rsrc = xsrc(i)
                DMA("sp", xt[0:n, :], src, rsrc, [xt])
                jk = junk_r.next(); st = st_r.next()
                ACTV(jk[0:n, :], xt[0:n, :], AF.Square, [xt], [jk, st], accum=st[0:n, 0:1])
                ACTV(st[0:n, 1:2], st[0:n, 0:1], AF.Sqrt, [st, eps_c], [st], bias=eps_c[0:n, 0:1], scale=1.0 / D)
                P.op("dve", lambda e, st=st: e.reciprocal(st[0:n, 2:3], st[0:n, 1:2]), rs([st]), rs([st]))
                hf = yo_r.next()
                STT("dve", hf[0:n, :], xt[0:n, :], st[0:n, 2:3], mod[0][0:n, :], ALU.mult, ALU.mult, [xt, st, mod[0]], [hf])
                hb = hb_r.next()
                TT("dve", hb[0:n, :], hf[0:n, :], mod[1][0:n, :], ALU.add, [hf, mod[1]], [hb])
                pb = pbr.next()
                for k in range(8):
                    TRN(pb[:, k * 128:k * 128 + n], hb[0:n, k * 128:(k + 1) * 128], ident[0:n, 0:n], [hb, ident], [pb])
                CP("act", hT[:, :, i * n:(i + 1) * n], pb[:].rearrange("p (k t) -> p k t", t=128)[:, :, 0:n], [pb], [hT])

            def wload(c0, m):
                wb_ = wst_r.next()
                DMA("sp", wb_[:, :, 0:m], wsc[l % 2][:, 8 * c0:8 * c0 + 8 * m].rearrange("p (k m) -> p k m", m=m), [r_wsc[l % 2][c0]], [wb_])
                return wb_

            chk(31)
            def fm(c0, m):
                wb_ = wload(c0, m)
                ps = pf.next()
                for k in range(8):
                    MM(ps[0:m, 0:NTOK], wb_[:, k, 0:m], hT[:, k, 0:NTOK], [wb_, hT], [ps], start=(k == 0), stop=(k == 7))
                return ps

            def tm(i, wb_, wd, ps, o0):
                for k in range(8):
                    MM(ps[0:n, o0:o0 + wd], hT[:, k, i * n:(i + 1) * n], wb_[:, k, 0:wd], [wb_, hT], [ps], start=(k == 0), stop=(k == 7))

            if mode == "p":
                rt = rtab_r.next()
                DMA("sp", rt[:, 0, 0:BLK], cin["c_cos_p"][:, blk * BLK:(blk + 1) * BLK], [], [rt])
                DMA("sp", rt[:, 1, 0:BLK], cin["c_sin_p"][:, blk * BLK:(blk + 1) * BLK], [], [rt])
                cosb, sinb, rtb = rt[:, 0, 0:BLK], rt[:, 1, 0:BLK], rt
                s0 = (blk * NTB) % NKT
                kdst = lambda p: kT[:, p, s0 * 128:s0 * 128 + BLK]
                kdst_b = kT
            else:
                cosb, sinb, rtb = C["c_cos_s"][:], C["c_sin_s"][:], C["c_cos_s"]
                kdst = lambda p: kTs[:, p, :]
                kdst_b = kTs

            def rope(ps, dst, dstb, scale):
                xb = xb_r.next()
                P.op("act", lambda e, xb=xb, ps=ps: e.mul(xb[:, 0:NTOK], ps[:, 0:NTOK], float(scale)), rs([ps]), rs([xb]))
                ps2 = pf.next()
                MM(ps2[:, 0:NTOK], C["c_rot"][:], xb[:, 0:NTOK], [C["c_rot"], xb], [ps2])
                chk(321)
                t1 = f5_r.next(); t2 = f5_r.next()
                STT("dve", t1[:, 0:NTOK], ps[:, 0:NTOK], scale, cosb, ALU.mult, ALU.mult, [ps, rtb, C["c_sin_s"]], [t1])
                TT("dve", t2[:, 0:NTOK], ps2[:, 0:NTOK], sinb, ALU.mult, [ps2, rtb, C["c_sin_s"]], [t2])
                chk(322)
                TT("dve", dst, t1[:, 0:NTOK], t2[:, 0:NTOK], ALU.add, [t1, t2], [dstb])
                ROPES[0] += 1
                if ROPES[0] == int(os.environ.get("MK_ROPES", "0")):
                    raise Stop()
                chk(323)

            for p in range(4):
                ps = fm(1296 + 128 * p, 128)
                chk(32)
                rope(ps, kdst(p), kdst_b, 1.0)
            for p in range(4):
                ps = fm(784 + 128 * p, 128)
                rope(ps, qT[:, p, 0:NTOK], qT, 0.125)
            chk(33)
            if mode == "p":
                for i in range(nt):
                    tok0 = blk * BLK + i * 128
                    if tok0 >= T - KEEP:
                        pb = pbr.next()
                        sl = (blk * NTB + i) % NKT
                        for p in range(4):
                            TRN(pb[:, p * 128:(p + 1) * 128], kT[:, p, sl * 128:(sl + 1) * 128], ident[:], [kT, ident], [pb])
                        ko = kout_r.next()
                        CP("act", ko[:], pb[:, 0:512], [pb], [ko])
                        DMA("sp", kp[l, tok0 - (T - KEEP):tok0 - (T - KEEP) + 128, :], ko[:], [ko], [])
            else:
                pb = pbr.next()
                for p in range(4):
                    TRN(pb[0:16, p * 128:(p + 1) * 128], kTs[:, p, :], ident[:], [kTs, ident], [pb])
                ko = kout_r.next()
                CP("act", ko[0:16, :], pb[0:16, 0:512], [pb], [ko])
                for b in range(4):
                    DMA("sp", ks[l, b, LC - 4:LC, :], ko[4 * b:4 * b + 4, :], [ko], [])
            chk(34)
            wv = wload(1808, 512)
            for i in range(nt):
                ps = pf.next()
                tm(i, wv, 512, ps, 0)
                vo = vout_r.next()
                if mode == "p":
                    sl = (blk * NTB + i) % NKT
                    CP("act", Vr[:, sl, :, 0:64], ps[:, :].rearrange("p (h d) -> p h d", d=64), [ps], [Vr])
                    tok0 = blk * BLK + i * 128
                    if tok0 >= T - KEEP:
                        CP("dve", vo[:], ps[:, :], [ps], [vo])
                        DMA("sp", vp[l, tok0 - (T - KEEP):tok0 - (T - KEEP) + 128, :], vo[:], [vo], [])
                else:
                    CP("act", Vsn[:, :, 0:64], ps[0:16, :].rearrange("p (h d) -> p h d", d=64), [ps], [Vsn])
                    CP("dve", vo[0:16, :], ps[0:16, :], [ps], [vo])
                    for b in range(4):
                        DMA("sp", vs[l, b, LC - 4:LC, :], vo[4 * b:4 * b + 4, :], [vo], [])
            wg = wload(2320, 512)
            for i in range(nt):
                ps = pf.next()
                tm(i, wg, 512, ps, 0)
                ACTV(sbg[0:n, i, :], ps[0:n, :], AF.Silu, [ps], [sbg])
            wa1 = wload(256, 256)
            wa2 = wload(528, 256)
            for i in range(nt):
                ps = pf.next()
                tm(i, wa1, 256, ps, 0)
                tm(i, wa2, 256, ps, 256)
                CP("dve", vg[0:n, i, :], ps[0:n, 0:256], [ps], [vg])
                ACTV(sag[0:n, i, :], ps[0:n, 256:512], AF.Silu, [ps], [sag])
            chk(35)
            for k2 in range(2):
                ps = fm(2832 + 128 * k2, 128)
                CP("act", uT[:, k2, 0:NTOK], ps[:, 0:NTOK], [ps], [uT])
                ps = fm(3088 + 128 * k2, 128)
                ACTV(scg[:, k2, 0:NTOK], ps[:, 0:NTOK], AF.Silu, [ps], [scg])

            chk(36)
            psq = fm(0, 128)
            psk = fm(128, 128)
            psa = fm(512, 16)
            CP("act", alrT[:, 0:NTOK], psa[0:16, 0:NTOK], [psa], [alrT])
            psl = pf.next()
            MM(psl[:, 0:NTOK], wlr[:], alrT[:, 0:NTOK], [wlr, alrT], [psl])
            t1 = f5_r.next()
            ACTV(t1[:, 0:NTOK], psl[:, 0:NTOK], AF.Exp, [psl, nblr], [t1], bias=nblr[:, 0:1], scale=-1.0)
            ACTV(t1[:, 0:NTOK], t1[:, 0:NTOK], AF.Ln, [t1, one_c], [t1], bias=one_c[:, 0:1])
            rmask = C["c_reset_p"] if mode == "p" else C["c_reset_s"]
            P.op("dve", lambda e, t1=t1: e.tensor_tensor_scan(csb[:, 0:NTOK], rmask[:, 0:NTOK], t1[:, 0:NTOK], 0.0, ALU.mult, ALU.add),
                 rs([rmask, t1]), rs([csb]))
            ACTV(E1[:, 0:NTOK], csb[:, 0:NTOK], AF.Exp, [csb], [E1], scale=-1.0 / 16)
            ACTV(E2[:, 0:NTOK], csb[:, 0:NTOK], AF.Exp, [csb], [E2], scale=1.0 / 16)
            STT("dve", qd[:, 0:NTOK], psq[:, 0:NTOK], 32 ** -0.5, E1[:, 0:NTOK], ALU.mult, ALU.mult, [psq, E1], [qd])
            TT("dve", kd[:, 0:NTOK], psk[:, 0:NTOK], E2[:, 0:NTOK], ALU.mult, [psk, E2], [kd])
            cl = 128 if mode == "p" else 4
            nch = NTOK // cl
            TT("dve", kdec[:, 0:NTOK].rearrange("p (c t) -> p c t", t=cl), kd[:, 0:NTOK].rearrange("p (c t) -> p c t", t=cl),
               E1[:, cl - 1:NTOK:cl].unsqueeze(2).to_broadcast([128, nch, cl]), ALU.mult, [kd, E1], [kdec])
            for h in range(4):
                TS("dve", qdh[h][:, 0:NTOK], qd[:, 0:NTOK], C["c_hm"][:, h:h + 1], None, ALU.mult, None, [qd, C["c_hm"]], [qdh[h]])
            if mode == "s":
                for h in range(4):
                    for b in range(4):
                        TT("dve", qdhb[:, h * 4 + b, :], qdh[h][:, 0:16], C["c_bcol"][:, b, :], ALU.mult, [qdh[h], C["c_bcol"]], [qdhb])
            causal = C["c_causal_p"] if mode == "p" else C["c_causal_s"]
            for i in range(nt):
                tk = slice(i * n, (i + 1) * n)
                pb = pbr.next()
                TRN(pb[0:n, 0:128], kdec[:, tk], ident[:], [kdec, ident], [pb])
                if mode == "p":
                    kth = kdTh.next()
                    for h in range(4):
                        CP("dve", kth[0:n, h, 32 * h:32 * h + 32], pb[0:n, 32 * h:32 * h + 32], [pb], [kth])
                else:
                    kth = kdThs
                    kraw = b5_r.next()
                    CP("dve", kraw[0:16, 0:128], pb[0:16, 0:128], [pb], [kraw])
                    for h in range(4):
                        for b in range(4):
                            TS("dve", kdThs[0:16, h * 4 + b, 32 * h:32 * h + 32], kraw[0:16, 32 * h:32 * h + 32], C["c_bm"][0:16, b:b + 1], None,
                               ALU.mult, None, [kraw, C["c_bm"]], [kdThs])
                psA = pf.next()
                for h in range(4):
                    MM(psA[0:n, h * n:(h + 1) * n], kd[:, tk], qdh[h][:, tk], [kd, qdh[h]], [psA])
                att = attT_r.next()
                TT("dve", att[0:n, 0:4 * n], psA[0:n, 0:4 * n], causal[0:n, 0:4 * n], ALU.mult, [psA, causal], [att])
                pso = pf.next()
                for h in range(4):
                    MM(pso[0:n, 64 * h:64 * h + 64], att[0:n, h * n:(h + 1) * n], vg[0:n, i, 64 * h:64 * h + 64], [att, vg], [pso], start=True, stop=False)
                    if mode == "p":
                        MM(pso[0:n, 64 * h:64 * h + 64], qdh[h][:, tk], Sgb[:], [qdh[h], Sgb], [pso], start=False, stop=True)
                    else:
                        for b in range(4):
                            MM(pso[0:n, 64 * h:64 * h + 64], qdhb[:, h * 4 + b, :], Sgsb[:, b, :], [qdhb, Sgsb], [pso], start=False, stop=(b == 3))
                if mode == "p":
                    psd = pf.next()
                    for h in range(4):
                        MM(psd[:, 0:64], kth[0:n, h, :], vg[0:n, i, 64 * h:64 * h + 64], [kth, vg], [psd], start=(h == 0), stop=(h == 3))
                    STT("dve", Sg[:], Sg[:], E1[:, i * 128 + 127:i * 128 + 128], psd[:, 0:64], ALU.mult, ALU.add, [Sg, E1, psd], [Sg])
                    CP("pool", Sgb[:], Sg[:], [Sg], [Sgb])
                else:
                    psd = pf.next()
                    for b in range(4):
                        for h in range(4):
                            MM(psd[:, 64 * b:64 * b + 64], kdThs[0:16, h * 4 + b, :], vg[0:16, 0, 64 * h:64 * h + 64], [kdThs, vg], [psd], start=(h == 0), stop=(h == 3))
                    for b in range(4):
                        STT("dve", Sgs[:, b, :], Sgs[:, b, :], E1[:, 4 * b + 3:4 * b + 4], psd[:, 64 * b:64 * b + 64], ALU.mult, ALU.add, [Sgs, E1, psd], [Sgs])
                osb = osb_r.next(); osq = osq_r.next(); st = st_r.next()
                CP("act", osb[0:n, :], pso[0:n, 0:256], [pso], [osb])
                TT("dve", osq[0:n, :], osb[0:n, :], osb[0:n, :], ALU.mult, [osb], [osq])
                P.op("dve", lambda e, st=st, osq=osq: e.tensor_reduce(st[0:n, 0:4], osq[0:n, :].rearrange("p (h d) -> p h d", d=64), AX.X, ALU.add),
                     rs([osq]), rs([st]))
                ACTV(st[0:n, 0:4], st[0:n, 0:4], AF.Sqrt, [st, eps_c], [st], bias=eps_c[0:n, 0:1], scale=1.0 / 64)
                P.op("dve", lambda e, st=st: e.reciprocal(st[0:n, 0:4], st[0:n, 0:4]), rs([st]), rs([st]))
                TT("dve", osb[0:n, :].rearrange("p (h d) -> p h d", d=64), osb[0:n, :].rearrange("p (h d) -> p h d", d=64),
                   st[0:n, 0:4].unsqueeze(2).to_broadcast([n, 4, 64]), ALU.mult, [osb, st], [osb])
                TT("dve", osb[0:n, :], osb[0:n, :], ggla[0:n, :], ALU.mult, [osb, ggla], [osb])
                TT("dve", otm[0:n, i, 0:256], osb[0:n, :], sag[0:n, i, :], ALU.mult, [osb, sag], [otm])

            chk(37)
            def ssm_gen():
                Ls = SSM_L if mode == "p" else 16
                for c0 in range(0, NTOK, Ls):
                    tk = slice(c0, c0 + Ls)
                    psY = pacc[2]
                    W0 = 448
                    ct, stb = (cosT, sinT) if mode == "p" else (cosS, sinS)

                    def ssm_s1(g):
                        gc = g // 8
                        ps = pf.next()
                        MM(ps[:, 0:Ls], LA[:, g, :], uT[:, gc, tk], [LA, uT], [ps])
                        MM(ps[:, 256:256 + Ls], LB[:, g, :], uT[:, gc, tk], [LB, uT], [ps])
                        f1 = ssmf_r.next(); f2 = ssmf_r.next()
                        TT("dve", f1[:, 0:Ls], ps[:, 0:Ls], ct[:, g, 0:Ls], ALU.mult, [ps, ct], [f1])
                        TT("dve", f2[:, 0:Ls], ps[:, 256:256 + Ls], stb[:, g, 0:Ls], ALU.mult, [ps, stb], [f2])
                        TT("pool", f1[:, 0:Ls], f1[:, 0:Ls], f2[:, 0:Ls], ALU.add, [f1, f2], [f1])
                        wb = ssmw_r.next()
                        if mode == "p":
                            P.op("dve", lambda e, wb=wb, f1=f1, g=g: e.tensor_tensor_scan(wb[:, 0:Ls], rhoF[:, g:g + 1].to_broadcast([128, Ls]), f1[:, 0:Ls],
                                                                                       winit[:, g:g + 1], ALU.mult, ALU.add), rs([rhoF, f1, winit]), rs([wb]))
                        else:
                            STT("dve", f1[:, 0:16:4], x0s[:, :, g], rhoF[:, g:g + 1], f1[:, 0:16:4], ALU.mult, ALU.add, [x0s, rhoF, f1], [f1])
                            P.op("dve", lambda e, wb=wb, f1=f1, g=g: e.tensor_tensor_scan(wb[:, 0:16], rhoS[:, g, :], f1[:, 0:16], 0.0, ALU.mult, ALU.add),
                                 rs([rhoS, f1]), rs([wb]))
                        u1 = ssmu_r.next(); u2 = ssmu_r.next()
                        TT("pool", u1[:, 0:Ls], wb[:, 0:Ls], ct[:, g, 0:Ls], ALU.mult, [wb, ct], [u1])
                        TT("pool", u2[:, 0:Ls], wb[:, 0:Ls], stb[:, g, 0:Ls], ALU.mult, [wb, stb], [u2])
                        return (g, wb, u1, u2)

                    def ssm_s2(g, wb, u1, u2):
                        gc, gl = g // 8, g % 8
                        MM(psY[:, gc * 256:gc * 256 + Ls], C1g[:, g, :], u1[:, 0:Ls], [C1g, u1], [psY], start=(gl == 0), stop=False)
                        MM(psY[:, gc * 256:gc * 256 + Ls], C2g[:, g, :], u2[:, 0:Ls], [C2g, u2], [psY], start=False, stop=(gl == 7))
                        if mode == "p":
                            MM(psY[:, W0 + g:W0 + g + 1], RotP[:, g, :], wb[:, Ls - 1:Ls], [RotP, wb], [psY])
                        else:
                            MM(psY[:, W0 + 4 * g:W0 + 4 * g + 4], RotS[:, g, :], wb[:, 3:16:4], [RotS, wb], [psY])

                    spend = []
                    for g in range(16):
                        spend.append(ssm_s1(g))
                        if len(spend) > 1:
                            ssm_s2(*spend.pop(0))
                        yield
                    while spend:
                        ssm_s2(*spend.pop(0))
                    if mode == "p":
                        CP("dve", winit[:], psY[:, W0:W0 + 16], [psY], [winit])
                    else:
                        so = f5_r.next()
                        CP("dve", so[:, 0:64].rearrange("p (b g) -> p b g", g=16), psY[:, W0:W0 + 64].rearrange("p (g b) -> p b g", b=4), [psY], [so])
                        DMA("sp", ssms[l], so[:, 0:64].rearrange("p (b g) -> p b g", g=16), [so], [])
                    for k2 in range(2):
                        STT("dve", yt[:, k2, 0:Ls], uT[:, k2, tk], dskb[:, k2:k2 + 1], psY[:, k2 * 256:k2 * 256 + Ls], ALU.mult, ALU.add, [uT, dskb, psY], [yt])
                    ACTV(zT[:, :, 0:Ls], yt[:, :, 0:Ls], AF.Gelu, [yt], [zT])
                    psG = pf.next()
                    for oc in range(2):
                        for k2 in range(2):
                            MM(psG[:, oc * 256:oc * 256 + Ls], w_glu[:, k2, oc * 128:(oc + 1) * 128], zT[:, k2, 0:Ls], [w_glu, zT], [psG], start=(k2 == 0), stop=(k2 == 1))
                    for oc in range(2):
                        ACTV(sgs[:, oc, 0:Ls], psG[:, oc * 256:oc * 256 + Ls], AF.Sigmoid, [psG, bglub], [sgs], bias=bglub[:, oc:oc + 1])
                    TT("dve", sgs[:, :, 0:Ls], sgs[:, :, 0:Ls], zT[:, :, 0:Ls], ALU.mult, [sgs, zT], [sgs])
                    TT("dve", mixT[:, 6:8, tk], sgs[:, :, 0:Ls], scg[:, :, tk], ALU.mult, [sgs, scg], [mixT])


            chk(38)
            if mode == "p":
                def swa_gen():
                    for i in range(nt):
                        qi = blk * NTB + i
                        k_lo = max(0, qi - 16)
                        kis = list(range(k_lo, qi + 1))
                        accs = pacc[0:2]
                        groups = [kis[g0:g0 + 4] for g0 in range(0, len(kis), 4)]

                        def swa_s1(h, grp, i=i, qi=qi):
                            p, hb_ = h // 2, 64 * (h % 2)
                            ps = pf.next()
                            for idx, ki in enumerate(grp):
                                sl = ki % NKT
                                MM(ps[:, idx * 128:(idx + 1) * 128], kT[hb_:hb_ + 64, p, sl * 128:(sl + 1) * 128], qT[hb_:hb_ + 64, p, i * 128:(i + 1) * 128], [kT, qT], [ps])
                            w = len(grp) * 128
                            pt = pt_r.next(); pm = pm_r.next()
                            ACTV(pt[:, 0:w], ps[:, 0:w], AF.Exp, [ps], [pt])
                            m0 = grp[0] - (qi - 16)
                            SWA_CNT[0] += 1
                            meng = "pool" if SWA_CNT[0] % 2 == 0 else "dve"
                            TT(meng, pm[:, 0:w], pt[:, 0:w], C["c_swam"][:, m0:m0 + len(grp), :].rearrange("p a b -> p (a b)"), ALU.mult, [pt, C["c_swam"]], [pm])
                            return (h, grp, pm)

                        def swa_s2(h, grp, pm, kis=kis):
                            acc = accs[h // 4]
                            hh = h % 4
                            for idx, ki in enumerate(grp):
                                sl = ki % NKT
                                MM(acc[:, hh * 65:(hh + 1) * 65], pm[:, idx * 128:(idx + 1) * 128], Vr[:, sl, h, :], [pm, Vr], [acc],
                                   start=(ki == kis[0]), stop=(ki == kis[-1]))

                        pend = []
                        for h in range(8):
                            for grp in groups:
                                pend.append(swa_s1(h, grp))
                                if len(pend) > 2:
                                    swa_s2(*pend.pop(0))
                                yield
                        while pend:
                            swa_s2(*pend.pop(0))
                        for a in range(2):
                            acc = accs[a]
                            st = st_r.next()
                            av_ = acc[:, 0:260].rearrange("p (h d) -> p h d", d=65)
                            P.op("dve", lambda e, st=st, av_=av_: e.reciprocal(st[:, 0:4], av_[:, :, 64]), rs([acc]), rs([st]))
                            ob = f5_r.next()
                            TT("dve", ob[:, 0:256].rearrange("p (h d) -> p h d", d=64), av_[:, :, 0:64], st[:, 0:4].unsqueeze(2).to_broadcast([128, 4, 64]), ALU.mult, [acc, st], [ob])
                            TT("dve", otm[:, i, 256 + a * 256:512 + a * 256], ob[:, 0:256], sbg[:, i, a * 256:(a + 1) * 256], ALU.mult, [ob, sbg], [otm])

                gens = [[swa_gen(), 3], [ssm_gen(), 1]]
                while gens:
                    for ent in list(gens):
                        for _ in range(ent[1]):
                            try:
                                next(ent[0])
                            except StopIteration:
                                gens.remove(ent)
                                break
            else:
                for _ in ssm_gen():
                    pass
                accs = pacc[0:2]
                ckTs = {}
                for b in range(4):
                    for t in range(16):
                        ctk = ctile_k.next(); ctv = ctile_v.next()
                        DMA("sp", ctk[:], ck[l, b, t * 128:(t + 1) * 128, :], [], [ctk])
                        DMA("sp", ctv[:], cv[l, b, t * 128:(t + 1) * 128, :], [], [ctv])
                        if t == 0:
                            DMA("sp", ks[l, b, 0:124, :], ctk[4:128, :], [ctk], [])
                            DMA("sp", vs[l, b, 0:124, :], ctv[4:128, :], [ctv], [])
                        else:
                            DMA("sp", ks[l, b, t * 128 - 4:t * 128 + 124, :], ctk[:], [ctk], [])
                            DMA("sp", vs[l, b, t * 128 - 4:t * 128 + 124, :], ctv[:], [ctv], [])
                        ckb = ckb_r.next(); cvb = cvb_r.next()
                        CP("act", ckb[:], ctk[:], [ctk], [ckb])
                        CP("pool", cvb[:, :, 0:64], ctv[:].rearrange("p (h d) -> p h d", d=64), [ctv], [cvb])
                        pb = pbr.next()
                        for p in range(4):
                            TRN(pb[:, p * 128:(p + 1) * 128], ckb[:, p * 128:(p + 1) * 128], ident[:], [ckb, ident], [pb])
                        ckT = ckT_r.next()
                        CP("dve", ckT[:], pb[:, 0:512].rearrange("p (a b) -> p a b", b=128), [pb], [ckT])
                        pse = pf.next(); pso_ = pf.next()
                        for h in range(8):
                            p, hb_ = h // 2, 64 * (h % 2)
                            pst = pse if h % 2 == 0 else pso_
                            MM(pst[:, p * 16:(p + 1) * 16], ckT[hb_:hb_ + 64, p, :], qT[hb_:hb_ + 64, p, 0:16], [ckT, qT], [pst])
                        pt = pt_r.next(); pm = pm_r.next()
                        ACTV(pt[:, 0:64], pse[:, 0:64], AF.Exp, [pse], [pt])
                        ACTV(pt[:, 64:128], pso_[:, 0:64], AF.Exp, [pso_], [pt])
                        TT("pool", pm[:, 0:128].rearrange("p (h q) -> p h q", q=16), pt[:, 0:128].rearrange("p (h q) -> p h q", q=16),
                           C["c_swam_s"][:, b, t, :].unsqueeze(1).to_broadcast([128, 8, 16]), ALU.mult, [pt, C["c_swam_s"]], [pm])
                        for h in range(8):
                            cbk = (h % 2) * 4 + h // 2
                            MM(accs[h // 4][0:16, (h % 4) * 65:(h % 4 + 1) * 65], pm[:, cbk * 16:(cbk + 1) * 16], cvb[:, h, :], [pm, cvb], [accs[h // 4]],
                               start=(b == 0 and t == 0), stop=False)
                pse = pf.next(); pso_ = pf.next()
                for h in range(8):
                    p, hb_ = h // 2, 64 * (h % 2)
                    pst = pse if h % 2 == 0 else pso_
                    MM(pst[0:16, p * 16:(p + 1) * 16], kTs[hb_:hb_ + 64, p, :], qT[hb_:hb_ + 64, p, 0:16], [kTs, qT], [pst])
                pt = pt_r.next(); pm = pm_r.next()
                ACTV(pt[0:16, 0:64], pse[0:16, 0:64], AF.Exp, [pse], [pt])
                ACTV(pt[0:16, 64:128], pso_[0:16, 0:64], AF.Exp, [pso_], [pt])
                TT("pool", pm[0:16, 0:128].rearrange("p (h q) -> p h q", q=16), pt[0:16, 0:128].rearrange("p (h q) -> p h q", q=16),
                   C["c_swam_n"][:].unsqueeze(1).to_broadcast([16, 8, 16]), ALU.mult, [pt, C["c_swam_n"]], [pm])
                for h in range(8):
                    cbk = (h % 2) * 4 + h // 2
                    MM(accs[h // 4][0:16, (h % 4) * 65:(h % 4 + 1) * 65], pm[0:16, cbk * 16:(cbk + 1) * 16], Vsn[:, h, :], [pm, Vsn], [accs[h // 4]], start=False, stop=True)
                for a in range(2):
                    acc = accs[a]
                    st = st_r.next()
                    av_ = acc[0:16, 0:260].rearrange("p (h d) -> p h d", d=65)
                    P.op("dve", lambda e, st=st, av_=av_: e.reciprocal(st[0:16, 0:4], av_[:, :, 64]), rs([acc]), rs([st]))
                    ob = f5_r.next()
                    TT("dve", ob[0:16, 0:256].rearrange("p (h d) -> p h d", d=64), av_[:, :, 0:64], st[0:16, 0:4].unsqueeze(2).to_broadcast([16, 4, 64]), ALU.mult, [acc, st], [ob])
                    TT("dve", otm[0:16, 0, 256 + a * 256:512 + a * 256], ob[0:16, 0:256], sbg[0:16, 0, a * 256:(a + 1) * 256], ALU.mult, [ob, sbg], [otm])

            chk(39)
            for i in range(nt):
                pb = pbr.next()
                for c6 in range(6):
                    TRN(pb[:, c6 * 128:c6 * 128 + n], otm[0:n, i, c6 * 128:(c6 + 1) * 128], ident[0:n, 0:n], [otm, ident], [pb])
                CP("act", mixT[:, 0:6, i * n:(i + 1) * n], pb[:, 0:768].rearrange("p (k t) -> p k t", t=128)[:, :, 0:n], [pb], [mixT])
            pss = [[pf.next(), pf.next()] for _ in range(nt)]
            for half in range(2):
                wo_ = wst_r.next()
                DMA("sp", wo_[:], wosc[l % 2][:, half * 4096:(half + 1) * 4096].rearrange("p (k m) -> p k m", m=512), [r_wosc[l % 2][half]], [wo_])
                for i in range(nt):
                    ps = pss[i][half]
                    for k in range(8):
                        MM(ps[0:n, :], mixT[:, k, i * n:(i + 1) * n], wo_[:, k, :], [mixT, wo_], [ps], start=(k == 0), stop=(k == 7))
            for i in range(nt):
                psa_, psb_ = pss[i]
                yo = yo_r.next(); jk = junk_r.next(); st = st_r.next()
                CP("act", yo[0:n, 0:512], psa_[0:n, :], [psa_], [yo])
                CP("act", yo[0:n, 512:1024], psb_[0:n, :], [psb_], [yo])
                ACTV(jk[0:n, :], yo[0:n, :], AF.Square, [yo], [jk, st], accum=st[0:n, 0:1])
                ACTV(st[0:n, 1:2], st[0:n, 0:1], AF.Sqrt, [st, eps_c], [st], bias=eps_c[0:n, 0:1], scale=1.0 / D)
                P.op("dve", lambda e, st=st: e.reciprocal(st[0:n, 2:3], st[0:n, 1:2]), rs([st]), rs([st]))
                STT("dve", yo[0:n, :], yo[0:n, :], st[0:n, 2:3], mod[2][0:n, :], ALU.mult, ALU.mult, [yo, st, mod[2]], [yo])
                xt = xt_r.next()
                src, rsrc = xsrc(i)
                DMA("sp", xt[0:n, :], src, rsrc, [xt])
                TT("dve", yo[0:n, :], yo[0:n, :], xt[0:n, :], ALU.add, [yo, xt], [yo])
                dst, rdst = xdst(i)
                DMA("sp", dst, yo[0:n, :], [yo], rdst, owner=yo)

        P.maxops = int(os.environ.get("MK_MAXOPS", "0"))
        try:
          for l in range(DEPTH):
            layer_setup(l)
            chk(1)
            layer_setup2(l)
            chk(2)
            MS("dve", Sg[:], 0.0, [Sg]); MS("pool", Sgb[:], 0.0, [Sgb]); MS("dve", winit[:], 0.0, [winit])
            for blk in range(NB):
                do_block(l, "p", blk)
                chk(3)
            chk(4)
            go = f5_r.next()
            CP("dve", go[:, 0:64], Sg[:], [Sg], [go])
            DMA("sp", glap[l], go[:, 0:64], [go], [])
            wo = f5_r.next()
            CP("dve", wo[:, 0:16], winit[:], [winit], [wo])
            DMA("sp", ssmp[l], wo[:, 0:16], [wo], [])
            DMA("sp", Sgs[:], sgla[l].rearrange("b p e -> p b e"), [], [Sgs])
            CP("dve", Sgsb[:], Sgs[:], [Sgs], [Sgsb])
            DMA("sp", x0s[:], sssm[l], [], [x0s])
            build_rot(4.0)
            do_block(l, "s", 0)
            DMA("sp", glas[l].rearrange("b p e -> p b e"), Sgs[:], [Sgs], [])
        except (Stop, StopBuild):
            pass
        print("MK nops", P.nops, "sbuf_free", nc.sbuf_bytes_remaining, flush=True)
        P.emit()
    return nc


def _host_inputs(inp, core, T, DEPTH):
    f = np.float32
    pb = core // 2
    sbs = slice(4 * core, 4 * core + 4)
    m = {}
    m["xp"] = np.ascontiguousarray(inp["x_prompt"][pb, :T]).astype(f)
    m["xs"] = np.ascontiguousarray(inp["x_sample"][sbs].reshape(16, D)).astype(f)
    c5 = np.concatenate([inp["c_prompt"][pb:pb + 1], inp["c_sample"][sbs]], 0)
    m["cT"] = np.ascontiguousarray(c5.T.reshape(8, 128, 5).transpose(1, 0, 2)).astype(f)
    L = DEPTH
    m["w_in_r"] = np.ascontiguousarray(inp["w_in"][:L].reshape(L, 8, 128, NCOL).transpose(0, 2, 1, 3))
    m["w_out_r"] = np.ascontiguousarray(inp["w_out"][:L].reshape(L, 8, 128, D).transpose(0, 2, 1, 3))
    m["w_ada_r"] = np.ascontiguousarray(inp["w_ada"][:L].reshape(L, 8, 128, 3 * D).transpose(0, 2, 1, 3))
    bc = lambda a: np.ascontiguousarray(np.broadcast_to(a[:L, None, :], (L, 128, a.shape[-1]))).astype(f)
    m["b_ada_b"] = bc(inp["b_ada"]); m["g_pre_b"] = bc(inp["g_pre"]); m["g_post_b"] = bc(inp["g_post"]); m["g_gla_b"] = bc(inp["g_gla"])
    m["w_lr"] = np.ascontiguousarray(inp["w_gla_lr"][:L]); m["b_lr"] = np.ascontiguousarray(inp["b_gla_lr"][:L, :, None])
    lam_re, lam_im, ldt = inp["ssm_lambda_re"][:L], inp["ssm_lambda_im"][:L], inp["ssm_log_dt"][:L]
    tF = lambda a: np.ascontiguousarray(np.concatenate([a.transpose(0, 2, 1)] * 2, 1)).astype(f)
    m["lamF_re"] = tF(lam_re); m["lamF_im"] = tF(lam_im)
    m["logdtF"] = np.ascontiguousarray(np.broadcast_to(ldt[:, None, :], (L, 128, 16))).astype(f)

    def tB(a):
        a4 = a.reshape(L, 2, 8, 64)
        o = np.broadcast_to(a4.transpose(0, 2, 1, 3)[:, :, None, :, :], (L, 8, 16, 2, 64))
        return np.ascontiguousarray(o.reshape(L, 128, 128)).astype(f)
    m["lamB_re"] = tB(lam_re); m["lamB_im"] = tB(lam_im)
    m["logdtB"] = tB(np.broadcast_to(ldt[:, :, None], (L, 16, 64)))

    def tBt(bb):
        b5 = bb.reshape(L, 2, 8, 64, 16)
        return np.ascontiguousarray(b5.transpose(0, 2, 4, 1, 3).reshape(L, 128, 128)).astype(f)
    m["Bt_re"] = tBt(inp["ssm_b_re"][:L]); m["Bt_im"] = tBt(inp["ssm_b_im"][:L])
    tC = lambda cc: np.ascontiguousarray(cc.transpose(0, 3, 1, 2).reshape(L, 64, 256)).astype(f)
    m["Ct_re"] = tC(inp["ssm_c_re"][:L]); m["Ct_im"] = tC(inp["ssm_c_im"][:L])
    kp_ = lambda a: np.ascontiguousarray(a[:L].reshape(L, 2, 128).transpose(0, 2, 1)).astype(f)
    m["dsk"] = kp_(inp["ssm_d"]); m["bglu"] = kp_(inp["b_glu"])
    m["w_glu_r"] = np.ascontiguousarray(inp["w_glu"][:L].reshape(L, 2, 128, 256).transpose(0, 2, 1, 3))
    m["sgla"] = np.ascontiguousarray(inp["state_gla"][:L, sbs].reshape(L, 4, 128, 64))
    sre = inp["state_ssm_re"][:L, sbs]; sim = inp["state_ssm_im"][:L, sbs]
    s2 = np.concatenate([sre.transpose(0, 3, 1, 2), sim.transpose(0, 3, 1, 2)], 1)
    m["sssm"] = np.ascontiguousarray(s2).astype(f)
    m["ck"] = np.ascontiguousarray(inp["cache_swa_k"][:L, sbs].reshape(L, 4, LC, 512))
    m["cv"] = np.ascontiguousarray(inp["cache_swa_v"][:L, sbs].reshape(L, 4, LC, 512))
    return m


_NC_CACHE = {}


def run(inputs, T=4096, DEPTH=4, ncores=8):
    inp = {k: np.asarray(v) for k, v in inputs.items()}
    key = (T, DEPTH)
    if key not in _NC_CACHE:
        _NC_CACHE[key] = build(T, DEPTH)
    nc = _NC_CACHE[key]
    cst = host_consts(T)
    in_maps = []
    for c in range(ncores):
        m = _host_inputs(inp, c, T, DEPTH)
        m.update(cst)
        in_maps.append(m)
    res = run_bass_kernel_spmd(nc, in_maps, core_ids=list(range(ncores)))
    R = res.results
    KEEP = min(LC, T)
    nb = ncores // 2
    f = np.float32
    y_p = np.stack([R[2 * b]["yp"] for b in range(nb)]).astype(f)
    y_s = np.concatenate([R[c]["ys"].reshape(4, 4, D) for c in range(ncores)]).astype(f)
    gla_p = np.stack([R[2 * b]["glap"].reshape(DEPTH, 4, 32, 64) for b in range(nb)], 1).astype(f)
    gla_s = np.concatenate([R[c]["glas"].reshape(DEPTH, 4, 4, 32, 64) for c in range(ncores)], 1).astype(f)
    k_p = np.stack([R[2 * b]["kp"].reshape(DEPTH, KEEP, 8, 64) for b in range(nb)], 1).astype(f)
    v_p = np.stack([R[2 * b]["vp"].reshape(DEPTH, KEEP, 8, 64) for b in range(nb)], 1).astype(f)
    k_s = np.concatenate([R[c]["ks"].reshape(DEPTH, 4, LC, 8, 64) for c in range(ncores)], 1).astype(f)
    v_s = np.concatenate([R[c]["vs"].reshape(DEPTH, 4, LC, 8, 64) for c in range(ncores)], 1).astype(f)

    def unp(a):
        return a[:, 0:64, :].transpose(0, 2, 1), a[:, 64:128, :].transpose(0, 2, 1)
    rp = [unp(R[2 * b]["ssmp"]) for b in range(nb)]
    re_p = np.stack([x[0] for x in rp], 1).astype(f); im_p = np.stack([x[1] for x in rp], 1).astype(f)

    def unps(a):
        return a[:, 0:64].transpose(0, 2, 3, 1), a[:, 64:128].transpose(0, 2, 3, 1)
    rsx = [unps(R[c]["ssms"]) for c in range(ncores)]
    re_s = np.concatenate([x[0] for x in rsx], 1).astype(f); im_s = np.concatenate([x[1] for x in rsx], 1).astype(f)
    return (y_p, y_s, gla_p, gla_s, k_p, v_p, k_s, v_s, re_p, im_p, re_s, im_s)


def kernel(**inputs):
    return run(inputs, 4096, 4, 8)
```
